# Optimizing a Trainium2 kernel written in Bass

```python
import jax, jax.numpy as jnp
from jax import lax
import numpy as np

D_MODEL = 2048
BATCH = 4
SEQ = 2048
DEPTH = 2

N_META = 16
CONF_W = 1024
CONF_K = 31
SC_W = 1024
SC_K = 3
AB_SIZES = (CONF_W, CONF_W, SC_W, SC_W, SC_W)
AB_IN = sum(AB_SIZES)
GLA_HEADS = 4
GLA_DK = 128
GLA_DV = 256
GLA_QK = GLA_HEADS * GLA_DK
GLA_V = GLA_HEADS * GLA_DV
GLA_RANK = 16
GLA_GATE_NORM = 16.0
GLA_CHUNK = 64
GLA_SIZES = (GLA_QK, GLA_QK, GLA_V, GLA_V, GLA_RANK)
RW_HEADS = 16
RW_N = 64
RW_W = RW_HEADS * RW_N
RW_DECAY_RANK = 64
RW_A_RANK = 64
RW_G_RANK = 128
RW_SIZES = (RW_W, RW_W, RW_W, RW_DECAY_RANK, RW_A_RANK, RW_G_RANK)
GLA_COLS = sum(GLA_SIZES)
RW_COLS = sum(RW_SIZES)
CD_IN = GLA_COLS + RW_COLS
RW_GN_EPS = 64e-5
D_FF = 5632
FFN_K = 3
EPS = 1e-6
LN_EPS = 1e-5
N_EVEN = (DEPTH + 1) // 2
N_ODD = DEPTH // 2

kernel_name = "hybrid_conv_gla_rwkv7_meta"


def _splits(sizes):
    return [int(s) for s in np.cumsum(sizes)[:-1]]


def rmsnorm(x, g):
    x32 = x.astype(jnp.float32)
    y = x32 * lax.rsqrt(jnp.mean(x32 * x32, axis=-1, keepdims=True) + EPS)
    return (y * g.astype(jnp.float32)).astype(x.dtype)


def layernorm(x, g, b):
    x32 = x.astype(jnp.float32)
    mu = jnp.mean(x32, axis=-1, keepdims=True)
    var = jnp.mean(jnp.square(x32 - mu), axis=-1, keepdims=True)
    y = (x32 - mu) * lax.rsqrt(var + LN_EPS) * g.astype(jnp.float32) + b.astype(jnp.float32)
    return y.astype(x.dtype)


def causal_dwconv(x, w):
    K = w.shape[0]
    L = x.shape[1]
    xp = jnp.pad(x, ((0, 0), (K - 1, 0), (0, 0)))
    out = xp[:, 0:L] * w[0]
    for t in range(1, K):
        out = out + xp[:, t:t + L] * w[t]
    return out


def token_shift(z):
    return jnp.pad(z, ((0, 0), (1, 0), (0, 0)))[:, :-1]


def mixer_ab(h, w_in, conf_dw, conf_dw_b, conf_ln_g, conf_ln_b, sc_dw, w_out):
    z = h @ w_in
    a_val, a_gate, s_b, s_c, s_x = jnp.split(z, _splits(AB_SIZES), axis=-1)
    a = a_val * jax.nn.sigmoid(a_gate)
    a = causal_dwconv(a, conf_dw) + conf_dw_b
    a = jax.nn.silu(layernorm(a, conf_ln_g, conf_ln_b))
    s = s_b * causal_dwconv(s_c * s_x, sc_dw)
    return jnp.concatenate([a, s], axis=-1) @ w_out


def gla_chunked(q, k, v, log_a):
    Bs, T, H, dk = q.shape
    dv = v.shape[-1]
    C = GLA_CHUNK
    nc = T // C

    def blk(t):
        return t.reshape(Bs, nc, C, H, t.shape[-1]).transpose(0, 3, 1, 2, 4)

    q, k, v, g = blk(q), blk(k), blk(v), blk(log_a)
    cum = jnp.cumsum(g, axis=3)
    last = cum[:, :, :, -1:, :]
    q_dec = q * jnp.exp(cum)
    k_dec = k * jnp.exp(-cum)
    mask = jnp.tril(jnp.ones((C, C), dtype=bool))
    scores = jnp.where(mask, jnp.einsum('bhncd,bhnsd->bhncs', q_dec, k_dec), 0.0)
    o = jnp.einsum('bhncs,bhnsv->bhncv', scores, v)
    chunk_state = jnp.einsum('bhncd,bhncv->bhndv', k * jnp.exp(last - cum), v)
    chunk_decay = jnp.exp(last[:, :, :, 0, :])

    def step(S, inp):
        dec, cs = inp
        return S * dec[..., None] + cs, S

    _, S_prev = lax.scan(step, jnp.zeros((Bs, H, dk, dv), q.dtype),
                         (jnp.moveaxis(chunk_decay, 2, 0), jnp.moveaxis(chunk_state, 2, 0)))
    S_prev = jnp.moveaxis(S_prev, 0, 2)
    o = o + jnp.einsum('bhncd,bhndv->bhncv', q_dec, S_prev)
    return o.transpose(0, 2, 3, 1, 4).reshape(Bs, T, H, dv)


def rwkv7_scan(r, w, k, v, kk, a):
    Bs, L, H, N = r.shape

    def step(S, inp):
        r_t, w_t, k_t, v_t, kk_t, a_t = inp
        sa = jnp.einsum('bhvk,bhk->bhv', S, -kk_t)
        S = S * w_t[:, :, None, :] + sa[..., None] * (kk_t * a_t)[:, :, None, :] \
            + v_t[..., None] * k_t[:, :, None, :]
        return S, jnp.einsum('bhvk,bhk->bhv', S, r_t)

    xs = tuple(jnp.moveaxis(t, 1, 0) for t in (r, w, k, v, kk, a))
    _, y = lax.scan(step, jnp.zeros((Bs, H, N, N), r.dtype), xs)
    return jnp.moveaxis(y, 0, 1)


def mixer_cd(h, w_in, gla_w2, gla_b, gla_norm_g, rw_mu, rw_w0, rw_w2, rw_a0, rw_a2, rw_g2,
             rw_kk, rw_ka, rw_rk, rw_ln_g, rw_ln_b, w_out):
    dt = h.dtype
    Bs, L, _ = h.shape
    z = h @ w_in
    z_gla, z_rw = z[..., :GLA_COLS], z[..., GLA_COLS:]

    q, k, v, go, glr = jnp.split(z_gla.astype(jnp.float32), _splits(GLA_SIZES), axis=-1)
    log_a = jax.nn.log_sigmoid(glr @ gla_w2.astype(jnp.float32) + gla_b) / GLA_GATE_NORM
    q = q * (GLA_DK ** -0.5)
    heads = lambda t, d: t.reshape(Bs, L, GLA_HEADS, d)
    q, k, log_a, v = heads(q, GLA_DK), heads(k, GLA_DK), heads(log_a, GLA_DK), heads(v, GLA_DV)
    pad_front = (-N_META) % GLA_CHUNK
    pad_back = (-(pad_front + L)) % GLA_CHUNK
    pw = ((0, 0), (pad_front, pad_back), (0, 0), (0, 0))
    o = gla_chunked(jnp.pad(q, pw), jnp.pad(k, pw), jnp.pad(v, pw), jnp.pad(log_a, pw))
    o = o[:, pad_front:pad_front + L]
    o = o * lax.rsqrt(jnp.mean(o * o, axis=-1, keepdims=True) + EPS)
    o = o.reshape(Bs, L, GLA_V) * gla_norm_g * jax.nn.silu(go)

    zr = z_rw.astype(jnp.float32)
    zr = zr + (token_shift(zr) - zr) * rw_mu
    r, kr, vr, xw, xa, xg = jnp.split(zr, _splits(RW_SIZES), axis=-1)
    w_log = -jax.nn.softplus(-(rw_w0 + jnp.tanh(xw) @ rw_w2.astype(jnp.float32))) - 0.5
    decay = jnp.exp(-jnp.exp(w_log))
    a = jax.nn.sigmoid(rw_a0 + xa @ rw_a2.astype(jnp.float32))
    g = jax.nn.sigmoid(xg) @ rw_g2.astype(jnp.float32)
    rh = lambda t: t.reshape(Bs, L, RW_HEADS, RW_N)
    r, kr, vr, decay, a = rh(r), rh(kr), rh(vr), rh(decay), rh(a)
    kk = kr * rw_kk.reshape(RW_HEADS, RW_N)
    kk = kk / jnp.maximum(jnp.sqrt(jnp.sum(kk * kk, axis=-1, keepdims=True)), 1e-12)
    kr = kr * (1.0 + (a - 1.0) * rw_ka.reshape(RW_HEADS, RW_N))
    y = rwkv7_scan(r, decay, kr, vr, kk, a)
    mu = jnp.mean(y, axis=-1, keepdims=True)
    var = jnp.mean(jnp.square(y - mu), axis=-1, keepdims=True)
    y = (y - mu) * lax.rsqrt(var + RW_GN_EPS) * rw_ln_g.reshape(RW_HEADS, RW_N) \
        + rw_ln_b.reshape(RW_HEADS, RW_N)
    y = y + jnp.sum(r * kr * rw_rk.reshape(RW_HEADS, RW_N), axis=-1, keepdims=True) * vr
    y = y.reshape(Bs, L, RW_W) * g

    return jnp.concatenate([o, y], axis=-1).astype(dt) @ w_out


def conv_ffn(h, w_up, dw, w_down):
    u = causal_dwconv(h @ w_up, dw)
    val, gate = u[..., :D_FF], u[..., D_FF:]
    return (jax.nn.silu(gate) * val) @ w_down


def setup_inputs(seed: int = 0) -> dict:
    key = jax.random.key(seed)
    ks = iter(jax.random.split(key, 48))
    nrm = lambda shape, scale: jax.random.normal(next(ks), shape, jnp.float32) * scale
    uni = lambda shape, lo, hi: jax.random.uniform(next(ks), shape, jnp.float32, lo, hi)
    gain = lambda shape: 1.0 + nrm(shape, 0.02)
    D = D_MODEL
    NE, NO = N_EVEN, N_ODD
    return {
        "x": nrm((BATCH, SEQ, D), 1.0),
        "meta": nrm((N_META, D), 1.0),
        "ab_w_in": nrm((NE, D, AB_IN), D ** -0.5),
        "ab_conf_dw": nrm((NE, CONF_K, CONF_W), CONF_K ** -0.5),
        "ab_conf_dw_b": nrm((NE, CONF_W), 0.02),
        "ab_conf_ln_g": gain((NE, CONF_W)),
        "ab_conf_ln_b": nrm((NE, CONF_W), 0.02),
        "ab_sc_dw": nrm((NE, SC_K, SC_W), SC_K ** -0.5),
        "ab_w_out": nrm((NE, CONF_W + SC_W, D), (CONF_W + SC_W) ** -0.5),
        "cd_w_in": nrm((NO, D, CD_IN), D ** -0.5),
        "cd_gla_w2": nrm((NO, GLA_RANK, GLA_QK), GLA_RANK ** -0.5),
        "cd_gla_b": nrm((NO, GLA_QK), 0.1),
        "cd_gla_norm_g": gain((NO, GLA_V)),
        "cd_rw_mu": uni((NO, RW_COLS), 0.0, 1.0),
        "cd_rw_w0": uni((NO, RW_W), -6.0, -1.0),
        "cd_rw_w2": nrm((NO, RW_DECAY_RANK, RW_W), RW_DECAY_RANK ** -0.5),
        "cd_rw_a0": nrm((NO, RW_W), 0.1),
        "cd_rw_a2": nrm((NO, RW_A_RANK, RW_W), RW_A_RANK ** -0.5),
        "cd_rw_g2": nrm((NO, RW_G_RANK, RW_W), RW_G_RANK ** -0.5),
        "cd_rw_kk": 0.85 + nrm((NO, RW_W), 0.05),
        "cd_rw_ka": 1.0 + nrm((NO, RW_W), 0.05),
        "cd_rw_rk": nrm((NO, RW_W), 0.1),
        "cd_rw_ln_g": gain((NO, RW_W)),
        "cd_rw_ln_b": nrm((NO, RW_W), 0.02),
        "cd_w_out": nrm((NO, GLA_V + RW_W, D), (GLA_V + RW_W) ** -0.5),
        "norm_mix": gain((DEPTH, D)),
        "norm_ffn": gain((DEPTH, D)),
        "ffn_w_up": nrm((DEPTH, D, 2 * D_FF), D ** -0.5),
        "ffn_dw": nrm((DEPTH, FFN_K, 2 * D_FF), FFN_K ** -0.5),
        "ffn_w_down": nrm((DEPTH, D_FF, D), D_FF ** -0.5),
        "norm_final": gain((D,)),
    }


def reference(x, meta, ab_w_in, ab_conf_dw, ab_conf_dw_b, ab_conf_ln_g, ab_conf_ln_b, ab_sc_dw,
              ab_w_out, cd_w_in, cd_gla_w2, cd_gla_b, cd_gla_norm_g, cd_rw_mu, cd_rw_w0, cd_rw_w2,
              cd_rw_a0, cd_rw_a2, cd_rw_g2, cd_rw_kk, cd_rw_ka, cd_rw_rk, cd_rw_ln_g, cd_rw_ln_b,
              cd_w_out, norm_mix, norm_ffn, ffn_w_up, ffn_dw, ffn_w_down, norm_final):
    Bs = x.shape[0]
    h = jnp.concatenate([jnp.broadcast_to(meta[None].astype(x.dtype), (Bs, N_META, D_MODEL)), x],
                        axis=1)
    for i in range(DEPTH):
        hn = rmsnorm(h, norm_mix[i])
        j = i // 2
        if i % 2 == 0:
            mix = mixer_ab(hn, ab_w_in[j], ab_conf_dw[j], ab_conf_dw_b[j], ab_conf_ln_g[j],
                           ab_conf_ln_b[j], ab_sc_dw[j], ab_w_out[j])
        else:
            mix = mixer_cd(hn, cd_w_in[j], cd_gla_w2[j], cd_gla_b[j], cd_gla_norm_g[j], cd_rw_mu[j],
                           cd_rw_w0[j], cd_rw_w2[j], cd_rw_a0[j], cd_rw_a2[j], cd_rw_g2[j],
                           cd_rw_kk[j], cd_rw_ka[j], cd_rw_rk[j], cd_rw_ln_g[j], cd_rw_ln_b[j],
                           cd_w_out[j])
        h = h + mix.astype(h.dtype)
        h = h + conv_ffn(rmsnorm(h, norm_ffn[i]), ffn_w_up[i], ffn_dw[i], ffn_w_down[i]).astype(h.dtype)
    return rmsnorm(h, norm_final)[:, N_META:]
```

```python
import numpy as np
from contextlib import ExitStack
import concourse.bass as bass
import concourse.mybir as mybir
from concourse.bass_utils import run_bass_kernel_spmd

F32 = mybir.dt.float32
BF16 = mybir.dt.bfloat16
AF = mybir.ActivationFunctionType
ALU = mybir.AluOpType

P = 128
D = 2048
KC = 16
N_META = 16
SEQ = 2048
L = N_META + SEQ
HALF = L // 2
HALO = 32
W0 = HALO + HALF
WM = 2 + HALF
DFF = 5632
NJ = DFF // P
NHF = 2
JH = NJ // NHF
EPS = 1e-6
LN_EPS = 1e-5


PROFILE = False


def split3(n):
    a = (n + 2) // 3
    out = []
    c = 0
    while c < n:
        m = min(a, n - c)
        out.append((c, m))
        c += m
    return out


class _Rec:
    def __init__(self):
        self.calls = []

    def __getattr__(self, name):
        def f(*a, **k):
            self.calls.append((name, a, k))
            return None
        return f


DEFER = False
LOOKAHEAD = 96
PER_ENGINE = 24
HOP_US = 0.7


def _free_elems(ap):
    try:
        sh = ap.shape
        n = 1
        for d in sh[1:]:
            n *= int(d)
        return n
    except Exception:
        return 64


def _op_cost(kind, e, call):
    name, a, k = call
    if kind == "dma":
        out = k.get("out")
        try:
            nb = out.nbytes()
        except Exception:
            nb = _free_elems(out) * 4 * 128
        return 0.15, 2.0 + nb / 2.0e5
    out = k.get("out", a[0] if a else None)
    n = _free_elems(out) if out is not None else 64
    if e == "pe":
        lhsT = k.get("lhsT")
        fp32 = (lhsT is not None and lhsT.dtype == F32) or name == "transpose"
        c = (0.30 + 4.0 * n / 2400.0) if fp32 else (0.07 + n / 2400.0)
        return c, c + 0.15
    if e == "act":
        c = 0.22 + n / 1400.0
        return c, c
    c = 0.12 + n / 960.0
    return c, c


class FW:
    NDMA = 24

    def __init__(self, nc, es):
        self.nc = nc
        self.es = es
        self.eng = {"pe": nc.tensor, "dve": nc.vector, "act": nc.scalar,
                    "pool": nc.gpsimd, "sp": nc.sync}
        self.sem = {}
        self.cnt = {}
        for k in self.eng:
            self.sem[k] = es.enter_context(nc.semaphore("s_" + k))
            self.cnt[k] = 0
        for i in range(self.NDMA):
            k = "d%d" % i
            self.sem[k] = es.enter_context(nc.semaphore("s_" + k))
            self.cnt[k] = 0
        self.dma_rr = 0
        self.waited = {k: {} for k in self.eng}
        self.lastw = {}
        self.readers = {}
        self.nwaits = 0
        self.nins = 0
        self.bank_rr = 0
        self.banks = [es.enter_context(nc.psum_tensor("pb%d" % i, [P, 512], F32)) for i in range(8)]
        self.pending = []
        self.defer = False

    def sb(self, es, name, shape, dtype=F32):
        self.uid = getattr(self, "uid", 0) + 1
        return es.enter_context(self.nc.sbuf_tensor("%s_%d" % (name, self.uid), list(shape), dtype))

    def nb(self):
        i = self.bank_rr
        self.bank_rr = (i + 1) % 8
        return ("pb", i), self.banks[i]

    def _wait(self, e, tok):
        k, v = tok
        if k == e and e == "pe":
            return
        w = self.waited[e]
        if w.get(k, 0) >= v:
            return
        self.eng[e].wait_ge(self.sem[k], v)
        w[k] = v
        self.nwaits += 1

    def _deps(self, e, reads, writes):
        need = {}

        def add(tok):
            if tok is None:
                return
            k, v = tok
            if need.get(k, 0) < v:
                need[k] = v
        for r in reads:
            add(self.lastw.get(r))
        for w in writes:
            add(self.lastw.get(w))
            for k, v in self.readers.get(w, {}).items():
                add((k, v))
        for k, v in need.items():
            self._wait(e, (k, v))

    def _commit(self, tok, reads, writes):
        k, v = tok
        for r in reads:
            d = self.readers.setdefault(r, {})
            if d.get(k, 0) < v:
                d[k] = v
        for w in writes:
            self.lastw[w] = tok
            self.readers[w] = {}

    def op(self, e, fn, reads=(), writes=()):
        rec = _Rec()
        fn(rec)
        assert len(rec.calls) == 1, rec.calls
        item = ("op", e, rec.calls[0], tuple(reads), tuple(writes))
        if self.defer and not PROFILE:
            self.pending.append(item)
            return None
        return self._emit(item)

    def dma(self, q, out, in_, reads=(), writes=(), **kw):
        item = ("dma", q, ("dma_start", (), dict(out=out, in_=in_, **kw)), tuple(reads), tuple(writes))
        if self.defer and not PROFILE:
            self.pending.append(item)
            return None
        return self._emit(item)

    def _emit(self, item):
        kind, e, (name, a, k), reads, writes = item
        if kind == "dma":
            return self._dma_now(e, reads, writes, k)
        return self._op_now(e, lambda eng: getattr(eng, name)(*a, **k), reads, writes)

    def flush(self):
        ops = self.pending
        self.pending = []
        n = len(ops)
        if n == 0:
            return
        deps = [[] for _ in range(n)]
        succ = [[] for _ in range(n)]
        lastw = {}
        rdrs = {}
        for i, (kind, e, call, reads, writes) in enumerate(ops):
            d = set()
            for r in reads:
                if r in lastw:
                    d.add(lastw[r])
            for w in writes:
                if w in lastw:
                    d.add(lastw[w])
                for j in rdrs.get(w, ()):
                    d.add(j)
            d.discard(i)
            deps[i] = sorted(d)
            for j in deps[i]:
                succ[j].append(i)
            for r in reads:
                rdrs.setdefault(r, []).append(i)
            for w in writes:
                lastw[w] = i
                rdrs[w] = []
        cost = [_op_cost(o[0], o[1], o[2]) for o in ops]
        blevel = [0.0] * n
        for i in range(n - 1, -1, -1):
            m = 0.0
            for j in succ[i]:
                if blevel[j] > m:
                    m = blevel[j]
            blevel[i] = cost[i][1] + HOP_US + m
        indeg = [len(deps[i]) for i in range(n)]
        ready = [i for i in range(n) if indeg[i] == 0]
        efree = {}
        fin = [0.0] * n
        order = []
        while ready:
            best = None
            bkey = None
            seen = {}
            for i in ready:
                e = ops[i][1]
                c_ = seen.get(e, 0)
                if c_ >= PER_ENGINE:
                    if len(seen) >= 5 and min(seen.values()) >= PER_ENGINE:
                        break
                    continue
                seen[e] = c_ + 1
                t = efree.get(e, 0.0)
                for j in deps[i]:
                    tj = fin[j] + (HOP_US if ops[j][1] != e or ops[j][0] == "dma" else 0.05)
                    if tj > t:
                        t = tj
                key = (round(t, 1), -blevel[i], i)
                if bkey is None or key < bkey:
                    bkey = key
                    best = (i, t)
            i, t = best
            ready.remove(i)
            e = ops[i][1]
            efree[e] = t + cost[i][0]
            fin[i] = t + cost[i][1]
            order.append(i)
            for j in succ[i]:
                indeg[j] -= 1
                if indeg[j] == 0:
                    lo = 0
                    hi = len(ready)
                    while lo < hi:
                        mid = (lo + hi) // 2
                        if ready[mid] < j:
                            lo = mid + 1
                        else:
                            hi = mid
                    ready.insert(lo, j)
        assert len(order) == n
        for i in order:
            self._emit(ops[i])

    def _op_now(self, e, fn, reads=(), writes=()):
        self._deps(e, reads, writes)
        ins = fn(self.eng[e])
        self.cnt[e] += 1
        ins.then_inc(self.sem[e], 1)
        tok = (e, self.cnt[e])
        self._commit(tok, reads, writes)
        self.nins += 1
        return tok

    def _dma_now(self, q, reads, writes, kw):
        k = "d%d" % self.dma_rr
        self.dma_rr = (self.dma_rr + 1) % self.NDMA
        if self.cnt[k] > 0:
            self._wait(q, (k, self.cnt[k]))
        self._deps(q, reads, writes)
        ins = self.eng[q].dma_start(**kw)
        self.cnt[k] += 16
        ins.then_inc(self.sem[k], 16)
        tok = (k, self.cnt[k])
        self._commit(tok, reads, writes)
        self.nins += 1
        return tok

    def cc(self, kind, groups, src, dst, reads=(), writes=()):
        k = "d%d" % self.dma_rr
        self.dma_rr = (self.dma_rr + 1) % self.NDMA
        if self.cnt[k] > 0:
            self._wait("pool", (k, self.cnt[k]))
        self._deps("pool", reads, writes)
        ins = self.nc.gpsimd.collective_compute(kind, mybir.AluOpType.bypass, replica_groups=groups, ins=[src], outs=[dst])
        self.cnt[k] += 16
        ins.then_inc(self.sem[k], 16)
        tok = (k, self.cnt[k])
        self._commit(tok, reads, writes)
        self.nins += 1
        return tok

    def mark(self, name):
        if not PROFILE:
            return
        sc = getattr(self, "_scope", None)
        if sc is not None:
            self.nc.leave_named_scope(sc[0], sc[1], False)
            self._scope = None
        if name:
            sid, _ = self.nc.enter_named_scope(name, False)
            self._scope = (name, sid)

    def barrier(self):
        self.flush()
        for e in self.eng:
            for k, v in self.cnt.items():
                if v > 0:
                    self._wait(e, (k, v))

    def finish(self, e="sp"):
        self.flush()
        for k, v in self.cnt.items():
            if v > 0:
                self._wait(e, (k, v))


class WStream:
    def __init__(self, fw, es, name, shape, nbuf):
        self.fw = fw
        self.name = name
        self.bufs = [fw.sb(es, "%s%d" % (name, i), shape, BF16) for i in range(nbuf)]
        self.i = 0

    def load(self, src):
        i = self.i
        self.i = (i + 1) % len(self.bufs)
        key = (self.name, i)
        self.fw.dma("pool", self.bufs[i][:], src, writes=[key])
        return key, self.bufs[i]


def emit_sumsq_rstd(fw, src_fn, nblk, n, ones, sqs, rs_out, scale, eps, rkeys):
    bk, bank = fw.nb()
    for b in range(nblk):
        sk, sq = sqs[b % len(sqs)]
        fw.op("act", lambda e: e.activation(out=sq[:, :n], in_=src_fn(b), func=AF.Square),
              reads=[rkeys(b)], writes=[sk])
        fw.op("pe", lambda e: e.matmul(bank[:, :n], lhsT=ones[:], rhs=sq[:, :n], start=(b == 0), stop=(b == nblk - 1)),
              reads=[sk, "const"], writes=[bk])
    rk, rs = rs_out
    fw.op("act", lambda e: e.activation(out=rs[:, :n], in_=bank[:, :n], func=AF.Sqrt, scale=scale, bias=eps),
          reads=[bk], writes=[rk])
    fw.op("dve", lambda e: e.reciprocal(out=rs[:, :n], in_=rs[:, :n]), reads=[rk], writes=[rk])


def emit_rmsnorm(fw, hT, hoff, hn, ncols, g, ones, sqs, rss, epsap):
    for ti, (c0, n) in enumerate(split3(ncols)):
        rs_out = rss[ti % len(rss)]
        emit_sumsq_rstd(fw, lambda b: hT[:, b, hoff + c0:hoff + c0 + n], KC, n, ones, sqs, rs_out,
                        1.0 / D, epsap, lambda b: ("h", b))
        rk, rs = rs_out
        for kc in range(KC):
            fw.op("dve", lambda e: e.scalar_tensor_tensor(out=hn[:, kc, c0:c0 + n], in0=hT[:, kc, hoff + c0:hoff + c0 + n],
                                                          scalar=g[:, kc:kc + 1], in1=rs[:, :n], op0=ALU.mult, op1=ALU.mult),
                  reads=[("h", kc), rk, "const"], writes=[("hn", kc)])


def emit_ffn(fw, nc, hT, hoff, hn, wup_r, wdn_r, dwv, dwg, fmask):
    with ExitStack() as es:
        actF = fw.sb(es, "actF", [P, JH, HALF], BF16)
        wu = WStream(fw, es, "wu", [P, 2, KC, P], 2)
        wd = WStream(fw, es, "wd", [P, JH, P], 2)
        cvs = [(("cv", i), fw.sb(es, "cv%d" % i, [P, 344])) for i in range(2)]
        cgs = [(("cg", i), fw.sb(es, "cg%d" % i, [P, 344])) for i in range(2)]
        sgs = [(("sgf", i), fw.sb(es, "sgf%d" % i, [P, 344])) for i in range(2)]
        fw.op("dve", lambda e: e.tensor_tensor(out=hn[:, :, 0:2], in0=hn[:, :, 0:2],
                                               in1=fmask[:, :].unsqueeze(1).to_broadcast([P, KC, 2]), op=ALU.mult),
              reads=[("hn", kc) for kc in range(KC)] + ["const"], writes=[("hn", kc) for kc in range(KC)])
        tiles = split3(HALF)
        hnkeys = [("hn", kc) for kc in range(KC)]
        it = 0
        for hf in range(NHF):
            for jj in range(JH):
                j = hf * JH + jj
                wk, wt = wu.load(wup_r[j].rearrange("p (a k c) -> p a k c", a=2, k=KC))
                for (o0, n) in tiles:
                    bv, bankv = fw.nb()
                    for kc in range(KC):
                        fw.op("pe", lambda e: e.matmul(bankv[:, :n + 2], lhsT=wt[:, 0, kc, :], rhs=hn[:, kc, o0:o0 + n + 2],
                                                       start=(kc == 0), stop=(kc == KC - 1)),
                              reads=[wk, ("hn", kc)], writes=[bv])
                    bg, bankg = fw.nb()
                    for kc in range(KC):
                        fw.op("pe", lambda e: e.matmul(bankg[:, :n + 2], lhsT=wt[:, 1, kc, :], rhs=hn[:, kc, o0:o0 + n + 2],
                                                       start=(kc == 0), stop=(kc == KC - 1)),
                              reads=[wk, ("hn", kc)], writes=[bg])
                    cvk, cv = cvs[it % 2]
                    cgk, cg = cgs[it % 2]
                    sgk, sg = sgs[it % 2]
                    it += 1
                    fw.op("act", lambda e: e.mul(out=cg[:, :n], in_=bankg[:, 2:n + 2], mul=dwg[:, j, 2:3]),
                          reads=[bg, "const"], writes=[cgk])
                    for tp in (1, 0):
                        fw.op("dve", lambda e: e.scalar_tensor_tensor(out=cg[:, :n], in0=bankg[:, tp:tp + n], scalar=dwg[:, j, tp:tp + 1],
                                                                      in1=cg[:, :n], op0=ALU.mult, op1=ALU.add),
                              reads=[bg, cgk, "const"], writes=[cgk])
                    fw.op("act", lambda e: e.activation(out=sg[:, :n], in_=cg[:, :n], func=AF.Silu), reads=[cgk], writes=[sgk])
                    fw.op("act", lambda e: e.mul(out=cv[:, :n], in_=bankv[:, 2:n + 2], mul=dwv[:, j, 2:3]),
                          reads=[bv, "const"], writes=[cvk])
                    for tp in (1, 0):
                        fw.op("dve", lambda e: e.scalar_tensor_tensor(out=cv[:, :n], in0=bankv[:, tp:tp + n], scalar=dwv[:, j, tp:tp + 1],
                                                                      in1=cv[:, :n], op0=ALU.mult, op1=ALU.add),
                              reads=[bv, cvk, "const"], writes=[cvk])
                    fw.op("dve", lambda e: e.tensor_tensor(out=actF[:, jj, o0:o0 + n], in0=cv[:, :n], in1=sg[:, :n], op=ALU.mult),
                          reads=[cvk, sgk], writes=[("actF", jj)])
            for m in range(KC):
                wk, wt = wd.load(wdn_r[hf, m].rearrange("p (j c) -> p j c", j=JH))
                for (o0, n) in tiles:
                    bk, bank = fw.nb()
                    for jj in range(JH):
                        fw.op("pe", lambda e: e.matmul(bank[:, :n], lhsT=wt[:, jj, :], rhs=actF[:, jj, o0:o0 + n],
                                                       start=(jj == 0), stop=(jj == JH - 1)),
                              reads=[wk, ("actF", jj)], writes=[bk])
                    c = hoff + 2 + o0
                    fw.op("dve", lambda e: e.tensor_tensor(out=hT[:, m, c:c + n], in0=hT[:, m, c:c + n], in1=bank[:, :n], op=ALU.add),
                          reads=[bk, ("h", m)], writes=[("h", m)])
        fw.barrier()


def emit_outproj(fw, wo, hT, hoff, act, wout_r):
    for m in range(KC):
        wk, wt = wo.load(wout_r[m].rearrange("p (k c) -> p k c", k=KC))
        for (c0, n) in split3(WM):
            bk, bank = fw.nb()
            for kb in range(KC):
                fw.op("pe", lambda e: e.matmul(bank[:, :n], lhsT=wt[:, kb, :], rhs=act[:, kb, c0:c0 + n],
                                               start=(kb == 0), stop=(kb == KC - 1)),
                      reads=[wk, ("act", kb)], writes=[bk])
            c = hoff + c0
            fw.op("dve", lambda e: e.tensor_tensor(out=hT[:, m, c:c + n], in0=hT[:, m, c:c + n], in1=bank[:, :n], op=ALU.add),
                  reads=[bk, ("h", m)], writes=[("h", m)])


_DIN_CACHE = {}


def make_din(nc, pfx, overrides=None):
    overrides = overrides or {}

    def din(name, shape):
        full = overrides.get(name, pfx + name)
        key = (id(nc), full)
        if key not in _DIN_CACHE:
            _DIN_CACHE[key] = nc.dram_tensor(full, list(shape), F32, kind="ExternalInput").ap()
        return _DIN_CACHE[key]
    return din


def load_consts(fw, es, nc, specs):
    out = {}
    for name, ap, shape in specs:
        t = fw.sb(es, "c_" + name, shape)
        fw.dma("sp", t[:], ap, writes=["const"])
        out[name] = t
    return out


def build_k1(ctx=None, pfx="", overrides=None, store=None):
    standalone = ctx is None
    nc = bass.Bass("TRN2", target_bir_lowering=False) if standalone else ctx[0]
    din = make_din(nc, pfx, overrides)
    xin = din("xin", [W0, D])
    ident_d = din("ident", [P, P])
    ones_d = din("ones", [P, P])
    fmask_d = din("fmask", [P, 2])
    g1_d = din("g1", [P, KC])
    g2_d = din("g2", [P, KC])
    win_r = din("win_r", [40, P, KC * P])
    cdw_d = din("cdw", [P, 8, 31])
    cvec_d = din("cvec", [P, 3, 8])
    sdw_d = din("sdw", [P, 8, 3])
    wout_r = din("wout_r", [KC, P, KC * P])
    wup_r = din("wup_r", [NJ, P, 2 * KC * P])
    dwv_d = din("dwv", [P, NJ, 3])
    dwg_d = din("dwg", [P, NJ, 3])
    wdn_r = din("wdn_r", [NHF, KC, P, JH * P])
    if standalone:
        h1T = nc.dram_tensor("h1T", [KC, P, HALF], F32, kind="ExternalOutput").ap()

    with ExitStack() as es:
        fw = FW(nc, es) if standalone else ctx[1]
        C = load_consts(fw, es, nc, [("ident", ident_d, [P, P]), ("ones", ones_d, [P, P]), ("fmask", fmask_d, [P, 2]),
                                     ("g1", g1_d, [P, KC]), ("g2", g2_d, [P, KC]), ("cdw", cdw_d, [P, 8, 31]),
                                     ("cvec", cvec_d, [P, 3, 8]), ("sdw", sdw_d, [P, 8, 3]),
                                     ("dwv", dwv_d, [P, NJ, 3]), ("dwg", dwg_d, [P, NJ, 3])])
        ident, ones = C["ident"], C["ones"]
        hT = fw.sb(es, "hT", [P, KC, W0])
        hn = fw.sb(es, "hn", [P, KC, W0], BF16)
        sqs = [(("sq", i), fw.sb(es, "sq%d" % i, [P, 356])) for i in range(3)]
        rss = [(("rs", i), fw.sb(es, "rs%d" % i, [P, 356])) for i in range(2)]

        with ExitStack() as e0:
            xts = [fw.sb(e0, "xt%d" % i, [P, D]) for i in range(2)]
            nt = (W0 + P - 1) // P
            for t in range(nt):
                r0 = t * P
                nr = min(P, W0 - r0)
                xt = xts[t % 2]
                xk = ("xt", t % 2)
                fw.dma("sp", xt[:nr, :], xin[r0:r0 + nr, :], writes=[xk])
                for gq in range(4):
                    bk, bank = fw.nb()
                    for q in range(4):
                        kc = gq * 4 + q
                        fw.op("pe", lambda e: e.transpose(bank[:, q * P:q * P + nr], xt[:nr, kc * P:(kc + 1) * P], ident[:nr, :nr]),
                              reads=[xk, "const"], writes=[bk])
                    src = bank[:, :].rearrange("p (a b) -> p a b", a=4)[:, :, :nr]
                    eng = "dve" if gq % 2 == 0 else "act"
                    if eng == "dve":
                        fw.op("dve", lambda e: e.tensor_copy(out=hT[:, gq * 4:gq * 4 + 4, r0:r0 + nr], in_=src),
                              reads=[bk], writes=[("h", gq * 4 + q) for q in range(4)])
                    else:
                        fw.op("act", lambda e: e.copy(out=hT[:, gq * 4:gq * 4 + 4, r0:r0 + nr], in_=src),
                              reads=[bk], writes=[("h", gq * 4 + q) for q in range(4)])
            fw.barrier()

        emit_rmsnorm(fw, hT, 0, hn, W0, C["g1"], ones, sqs, rss, EPS)

        with ExitStack() as e2:
            cT = fw.sb(e2, "cT", [P, 8, WM])
            act = fw.sb(e2, "actAS", [P, KC, WM], BF16)
            wi = WStream(fw, e2, "wi", [P, KC, P], 3)
            sg = fw.sb(e2, "sg", [P, W0])
            a_s = fw.sb(e2, "a_s", [P, W0])
            b_s = fw.sb(e2, "b_s", [P, W0])
            q_s = fw.sb(e2, "q_s", [P, WM])
            mean = q_s
            rstd = fw.sb(e2, "rstd", [P, WM])
            t_s = [("a_s", a_s), ("sg", sg)]
            cdw, cvec, sdw = C["cdw"], C["cvec"], C["sdw"]
            tiles0 = split3(W0)
            tilesm = split3(WM)

            def inproj(blk, tiles, coff, consume):
                wk, wt = wi.load(win_r[blk].rearrange("p (k c) -> p k c", k=KC))
                for (c0, n) in tiles:
                    bk, bank = fw.nb()
                    for kc in range(KC):
                        fw.op("pe", lambda e: e.matmul(bank[:, :n], lhsT=wt[:, kc, :], rhs=hn[:, kc, coff + c0:coff + c0 + n],
                                                       start=(kc == 0), stop=(kc == KC - 1)),
                              reads=[wk, ("hn", kc)], writes=[bk])
                    consume(bk, bank, c0, n)

            for blk in range(8):
                inproj(8 + blk, tiles0, 0, lambda bk, bank, c0, n: fw.op(
                    "act", lambda e: e.activation(out=sg[:, c0:c0 + n], in_=bank[:, :n], func=AF.Sigmoid),
                    reads=[bk], writes=["sg"]))
                inproj(blk, tiles0, 0, lambda bk, bank, c0, n: fw.op(
                    "dve", lambda e: e.tensor_tensor(out=a_s[:, c0:c0 + n], in0=bank[:, :n], in1=sg[:, c0:c0 + n], op=ALU.mult),
                    reads=[bk, "sg"], writes=["a_s"]))
                fw.op("dve", lambda e: e.tensor_scalar(out=cT[:, blk, :], in0=a_s[:, 0:WM], scalar1=cdw[:, blk, 0:1],
                                                       scalar2=cvec[:, 0, blk:blk + 1], op0=ALU.mult, op1=ALU.add),
                      reads=["a_s", "const"], writes=[("cT", blk)])
                for j in range(1, 31):
                    fw.op("dve", lambda e: e.scalar_tensor_tensor(out=cT[:, blk, :], in0=a_s[:, j:j + WM], scalar=cdw[:, blk, j:j + 1],
                                                                  in1=cT[:, blk, :], op0=ALU.mult, op1=ALU.add),
                          reads=["a_s", ("cT", blk), "const"], writes=[("cT", blk)])
                inproj(24 + blk, tiles0, 0, lambda bk, bank, c0, n: fw.op(
                    "act", lambda e: e.copy(out=sg[:, c0:c0 + n], in_=bank[:, :n]), reads=[bk], writes=["sg"]))
                inproj(32 + blk, tiles0, 0, lambda bk, bank, c0, n: fw.op(
                    "dve", lambda e: e.tensor_tensor(out=b_s[:, c0:c0 + n], in0=bank[:, :n], in1=sg[:, c0:c0 + n], op=ALU.mult),
                    reads=[bk, "sg"], writes=["b_s"]))
                fw.op("dve", lambda e: e.tensor_scalar(out=q_s[:, :], in0=b_s[:, 28:28 + WM], scalar1=sdw[:, blk, 0:1], scalar2=None,
                                                       op0=ALU.mult),
                      reads=["b_s", "const"], writes=["q_s"])
                for tp in (1, 2):
                    fw.op("dve", lambda e: e.scalar_tensor_tensor(out=q_s[:, :], in0=b_s[:, 28 + tp:28 + tp + WM], scalar=sdw[:, blk, tp:tp + 1],
                                                                  in1=q_s[:, :], op0=ALU.mult, op1=ALU.add),
                          reads=["b_s", "q_s", "const"], writes=["q_s"])
                inproj(16 + blk, tilesm, 30, lambda bk, bank, c0, n: fw.op(
                    "dve", lambda e: e.tensor_tensor(out=act[:, 8 + blk, c0:c0 + n], in0=bank[:, :n], in1=q_s[:, c0:c0 + n], op=ALU.mult),
                    reads=[bk, "q_s"], writes=[("act", 8 + blk)]))

            for (c0, n) in tilesm:
                b1k, bank1 = fw.nb()
                b2k, bank2 = fw.nb()
                for blk in range(8):
                    sk, sq = sqs[blk % 3]
                    fw.op("act", lambda e: e.activation(out=sq[:, :n], in_=cT[:, blk, c0:c0 + n], func=AF.Square),
                          reads=[("cT", blk)], writes=[sk])
                    fw.op("pe", lambda e: e.matmul(bank2[:, :n], lhsT=ones[:], rhs=sq[:, :n], start=(blk == 0), stop=(blk == 7)),
                          reads=[sk, "const"], writes=[b2k])
                    fw.op("pe", lambda e: e.matmul(bank1[:, :n], lhsT=ones[:], rhs=cT[:, blk, c0:c0 + n], start=(blk == 0), stop=(blk == 7)),
                          reads=[("cT", blk), "const"], writes=[b1k])
                fw.op("act", lambda e: e.mul(out=mean[:, c0:c0 + n], in_=bank1[:, :n], mul=1.0 / 1024), reads=[b1k], writes=["q_s"])
                sk, sq = sqs[0]
                fw.op("dve", lambda e: e.tensor_tensor(out=sq[:, :n], in0=mean[:, c0:c0 + n], in1=mean[:, c0:c0 + n], op=ALU.mult),
                      reads=["q_s"], writes=[sk])
                fw.op("dve", lambda e: e.scalar_tensor_tensor(out=sq[:, :n], in0=bank2[:, :n], scalar=1.0 / 1024, in1=sq[:, :n],
                                                              op0=ALU.mult, op1=ALU.subtract),
                      reads=[b2k, sk], writes=[sk])
                fw.op("act", lambda e: e.activation(out=sq[:, :n], in_=sq[:, :n], func=AF.Sqrt, bias=LN_EPS), reads=[sk], writes=[sk])
                fw.op("dve", lambda e: e.reciprocal(out=rstd[:, c0:c0 + n], in_=sq[:, :n]), reads=[sk], writes=["rstd"])
            for blk in range(8):
                tk, tt = t_s[blk % 2]
                fw.op("dve", lambda e: e.tensor_tensor(out=tt[:, 0:WM], in0=cT[:, blk, :], in1=mean[:, :], op=ALU.subtract),
                      reads=[("cT", blk), "q_s"], writes=[tk])
                fw.op("dve", lambda e: e.tensor_tensor(out=tt[:, 0:WM], in0=tt[:, 0:WM], in1=rstd[:, :], op=ALU.mult),
                      reads=[tk, "rstd"], writes=[tk])
                fw.op("act", lambda e: e.activation(out=act[:, blk, :], in_=tt[:, 0:WM], func=AF.Silu,
                                                    scale=cvec[:, 1, blk:blk + 1], bias=cvec[:, 2, blk:blk + 1]),
                      reads=[tk, "const"], writes=[("act", blk)])

            emit_outproj(fw, wi, hT, 30, act, wout_r)
            fw.barrier()

        emit_rmsnorm(fw, hT, 30, hn, WM, C["g2"], ones, sqs, rss, EPS)
        emit_ffn(fw, nc, hT, 30, hn, wup_r, wdn_r, C["dwv"], C["dwg"], C["fmask"])

        for kc in range(KC):
            if standalone:
                fw.dma("sp", h1T[kc], hT[:, kc, HALO:HALO + HALF], reads=[("h", kc)])
            else:
                store(fw, kc, hT[:, kc, HALO:HALO + HALF], [("h", kc)])
        if standalone:
            fw.finish("sp")
        else:
            fw.barrier()
        print("K1: ins", fw.nins, "waits", fw.nwaits)
    return nc


def blk_layout(w, kc=KC):
    K, N = w.shape
    return np.ascontiguousarray(w.reshape(K // P, P, N // P, P).transpose(2, 1, 0, 3).reshape(N // P, P, (K // P) * P))


def vec_layout(v):
    return np.ascontiguousarray(v.reshape(-1, P).T)


def prep_common():
    return dict(ident=np.eye(P, dtype=np.float32), ones=np.ones((P, P), np.float32))


def prep_ffn(w_up, dw, w_down):
    up_v = blk_layout(w_up[:, :DFF])
    up_g = blk_layout(w_up[:, DFF:])
    wup_r = np.ascontiguousarray(np.concatenate([up_v, up_g], axis=2))
    dwv = np.ascontiguousarray(dw[:, :DFF].reshape(3, NJ, P).transpose(2, 1, 0))
    dwg = np.ascontiguousarray(dw[:, DFF:].reshape(3, NJ, P).transpose(2, 1, 0))
    wd = w_down.reshape(NHF, JH, P, KC, P).transpose(0, 3, 2, 1, 4).reshape(NHF, KC, P, JH * P)
    return dict(wup_r=wup_r, dwv=dwv, dwg=dwg, wdn_r=np.ascontiguousarray(wd))


def prep_k1_common(inp):
    common = prep_common()
    common.update(prep_ffn(inp["ffn_w_up"][0], inp["ffn_dw"][0], inp["ffn_w_down"][0]))
    common["g1"] = vec_layout(inp["norm_mix"][0])
    common["g2"] = vec_layout(inp["norm_ffn"][0])
    common["win_r"] = blk_layout(inp["ab_w_in"][0])
    common["cdw"] = np.ascontiguousarray(inp["ab_conf_dw"][0].reshape(31, 8, P).transpose(2, 1, 0))
    common["cvec"] = np.ascontiguousarray(np.stack([inp["ab_conf_dw_b"][0].reshape(8, P).T, inp["ab_conf_ln_g"][0].reshape(8, P).T,
                                                    inp["ab_conf_ln_b"][0].reshape(8, P).T], axis=1))
    common["sdw"] = np.ascontiguousarray(inp["ab_sc_dw"][0].reshape(3, 8, P).transpose(2, 1, 0))
    common["wout_r"] = blk_layout(inp["ab_w_out"][0])
    return common


def k1_windows(inp, b):
    f = np.float32
    full = np.concatenate([inp["meta"], inp["x"][b]], axis=0)
    full = np.concatenate([np.zeros((HALO, D), f), full], axis=0)
    return [np.ascontiguousarray(full[half * HALF:half * HALF + W0]) for half in range(2)]


def prep_k1(inp):
    f = np.float32
    common = prep_k1_common(inp)
    maps = []
    B = inp["x"].shape[0]
    for b in range(B):
        full = np.concatenate([inp["meta"], inp["x"][b]], axis=0)
        full = np.concatenate([np.zeros((HALO, D), f), full], axis=0)
        for half in range(2):
            s = half * HALF
            m = dict(common)
            m["xin"] = np.ascontiguousarray(full[s:s + W0])
            m["fmask"] = np.full((P, 2), 0.0 if half == 0 else 1.0, f)
            maps.append(m)
    return maps


def build_k3(ctx=None, pfx="", overrides=None, load_h=None, load_act=None):
    standalone = ctx is None
    nc = bass.Bass("TRN2", target_bir_lowering=False) if standalone else ctx[0]
    din = make_din(nc, pfx, overrides)
    if standalone:
        h1w = din("h1w", [KC, P, WM])
        oyw = din("oyw", [KC, P, WM])
    ident_d = din("ident", [P, P])
    ones_d = din("ones", [P, P])
    fmask_d = din("fmask", [P, 2])
    g2_d = din("g2", [P, KC])
    gf_d = din("gf", [P, KC])
    wout_r = din("wout_r", [KC, P, KC * P])
    wup_r = din("wup_r", [NJ, P, 2 * KC * P])
    dwv_d = din("dwv", [P, NJ, 3])
    dwg_d = din("dwg", [P, NJ, 3])
    wdn_r = din("wdn_r", [NHF, KC, P, JH * P])
    out = nc.dram_tensor("out", [HALF, D], F32, kind="ExternalOutput").ap()

    with ExitStack() as es:
        fw = FW(nc, es) if standalone else ctx[1]
        C = load_consts(fw, es, nc, [("ident", ident_d, [P, P]), ("ones", ones_d, [P, P]), ("fmask", fmask_d, [P, 2]),
                                     ("g2", g2_d, [P, KC]), ("gf", gf_d, [P, KC]),
                                     ("dwv", dwv_d, [P, NJ, 3]), ("dwg", dwg_d, [P, NJ, 3])])
        ident, ones = C["ident"], C["ones"]
        hT = fw.sb(es, "hT", [P, KC, WM])
        hn = fw.sb(es, "hn", [P, KC, WM], BF16)
        sqs = [(("sq", i), fw.sb(es, "sq%d" % i, [P, 356])) for i in range(3)]
        rss = [(("rs", i), fw.sb(es, "rs%d" % i, [P, 356])) for i in range(2)]
        if standalone:
            for kc in range(KC):
                fw.dma("sp", hT[:, kc, :], h1w[kc], writes=[("h", kc)])
        else:
            load_h(fw, hT)
        with ExitStack() as e2:
            act = fw.sb(e2, "act", [P, KC, WM], BF16)
            wo = WStream(fw, e2, "wo", [P, KC, P], 2)
            if standalone:
                for kb in range(KC):
                    fw.dma("pool", act[:, kb, :], oyw[kb], writes=[("act", kb)])
            else:
                load_act(fw, act)
            emit_outproj(fw, wo, hT, 0, act, wout_r)
            fw.barrier()
        emit_rmsnorm(fw, hT, 0, hn, WM, C["g2"], ones, sqs, rss, EPS)
        emit_ffn(fw, nc, hT, 0, hn, wup_r, wdn_r, C["dwv"], C["dwg"], C["fmask"])
        with ExitStack() as e5:
            hf = fw.sb(e5, "hf", [P, KC, 344])
            ots = [(("ot", i), fw.sb(e5, "ot%d" % i, [P, D])) for i in range(2)]
            gf = C["gf"]
            oi = 0
            for ti, (o0, n) in enumerate(split3(HALF)):
                rs_out = rss[ti % 2]
                emit_sumsq_rstd(fw, lambda b: hT[:, b, 2 + o0:2 + o0 + n], KC, n, ones, sqs, rs_out, 1.0 / D, EPS,
                                lambda b: ("h", b))
                rk, rs = rs_out
                for kc in range(KC):
                    fw.op("dve", lambda e: e.scalar_tensor_tensor(out=hf[:, kc, :n], in0=hT[:, kc, 2 + o0:2 + o0 + n], scalar=gf[:, kc:kc + 1],
                                                                  in1=rs[:, :n], op0=ALU.mult, op1=ALU.mult),
                          reads=[("h", kc), rk, "const"], writes=[("hf", kc)])
                t0 = 0
                while t0 < n:
                    nr = min(P, n - t0)
                    otk, ot = ots[oi % 2]
                    oi += 1
                    for gq in range(4):
                        bk, bank = fw.nb()
                        for q in range(4):
                            kc = gq * 4 + q
                            fw.op("pe", lambda e: e.transpose(bank[:nr, q * P:(q + 1) * P], hf[:, kc, t0:t0 + nr], ident[:, :]),
                                  reads=[("hf", kc), "const"], writes=[bk])
                        if gq % 2 == 0:
                            fw.op("dve", lambda e: e.tensor_copy(out=ot[:nr, gq * 512:(gq + 1) * 512], in_=bank[:nr, :]), reads=[bk], writes=[otk])
                        else:
                            fw.op("act", lambda e: e.copy(out=ot[:nr, gq * 512:(gq + 1) * 512], in_=bank[:nr, :]), reads=[bk], writes=[otk])
                    fw.dma("sp", out[o0 + t0:o0 + t0 + nr, :], ot[:nr, :], reads=[otk])
                    t0 += nr
        fw.finish("sp")
        print("K3: ins", fw.nins, "waits", fw.nwaits)
    return nc


def prep_k3_common(inp):
    common = prep_common()
    common.update(prep_ffn(inp["ffn_w_up"][1], inp["ffn_dw"][1], inp["ffn_w_down"][1]))
    common["g2"] = vec_layout(inp["norm_ffn"][1])
    common["gf"] = vec_layout(inp["norm_final"])
    common["wout_r"] = blk_layout(inp["cd_w_out"][0])
    return common


def prep_k3(inp, h1T_list, oyT_list):
    f = np.float32
    common = prep_k3_common(inp)
    maps = []
    for b in range(len(h1T_list)):
        hp = np.concatenate([np.zeros((D, 2), f), h1T_list[b]], axis=1)
        op = np.concatenate([np.zeros((D, 2), f), oyT_list[b]], axis=1)
        for half in range(2):
            s = half * HALF
            m = dict(common)
            m["h1w"] = np.ascontiguousarray(hp[:, s:s + WM]).reshape(KC, P, WM)
            m["oyw"] = np.ascontiguousarray(op[:, s:s + WM]).reshape(KC, P, WM)
            m["fmask"] = np.full((P, 2), 0.0 if half == 0 else 1.0, f)
            maps.append(m)
    return maps


CH = 64
TP = 2112
NSEG = 3
TS = TP // NSEG
NCH = TS // CH
C05 = float(np.exp(-0.5))
GN_EPS = 64e-5


class StopBuild(Exception):
    pass


def build_k2(stop_at=None, ctx=None, pfx="", overrides=None, load_h=None, store=None):
    standalone = ctx is None
    nc = bass.Bass("TRN2", target_bir_lowering=False) if standalone else ctx[0]

    pcur = {"seg": 0, "pr": 0, "fw": None}

    def chk(name):
        if pcur["fw"] is not None:
            pcur["fw"].mark("post%s_s%d_p%d" % (name, pcur["seg"], pcur["pr"]))
        if stop_at == name:
            raise StopBuild()

    din = make_din(nc, pfx, overrides)
    if standalone:
        h1p = din("h1p", [KC, P, TP])
    cs_specs = [("ident", [P, P]), ("ones", [P, P]), ("bones", [P, P]), ("M4", [CH, 4 * CH]), ("M5", [CH, 6 * CH]),
                ("mincl", [CH, CH]), ("rmask", [P, TS]), ("g1", [P, KC]), ("rwp", [P, 10, 4]), ("mul", [P, 2]),
                ("w2z", [P, 512]), ("a2z", [P, 512]), ("hm", [P, 4]), ("g2s", [P, 512]), ("gw2", [16, 256]), ("glab", [P, 2]), ("glan", [P, 4])]
    cd = {n: din(n, s) for n, s in cs_specs}
    wr_r = din("wr_r", [4, 3, P, KC * P])
    wl_r = din("wl_r", [2, P, KC * P])
    wg_r = din("wg_r", [2, 6, P, KC * P])
    wglr_r = din("wglr_r", [P, KC * 16])
    if standalone:
        oy = nc.dram_tensor("oy", [8, P, L], F32, kind="ExternalOutput").ap()
    tiles2 = [(0, TS // 2), (TS // 2, TS // 2)]

    es = ExitStack()
    stopped = [False]
    if True:
        fw = FW(nc, es) if standalone else ctx[1]
        fw.flush()
        fw.defer = True
        C = load_consts(fw, es, nc, [(n, cd[n], s) for n, s in cs_specs])
        ident, ones, bones, M4, M5, mincl, rmask = (C[k] for k in ("ident", "ones", "bones", "M4", "M5", "mincl", "rmask"))
        rwp, mul, w2z, a2z, hm, g2s, gw2, glab, glan = (C[k] for k in ("rwp", "mul", "w2z", "a2z", "hm", "g2s", "gw2", "glab", "glan"))
        MU_R, MU_K, MU_V, W0I, A0I, KKI, KAI, RKI, LGI, LBI = range(10)
        omr = fw.sb(es, "omr", [P, 10, 4])
        oml = fw.sb(es, "oml", [P, 2])
        fw.op("dve", lambda e: e.tensor_scalar(out=omr[:], in0=rwp[:], scalar1=-1.0, scalar2=1.0, op0=ALU.mult, op1=ALU.add),
              reads=["const"], writes=["const2"])
        fw.op("dve", lambda e: e.tensor_scalar(out=oml[:], in0=mul[:], scalar1=-1.0, scalar2=1.0, op0=ALU.mult, op1=ALU.add),
              reads=["const"], writes=["const2"])
        identb = fw.sb(es, "identb", [P, P], BF16)
        fw.op("dve", lambda e: e.tensor_copy(out=identb[:], in_=ident[:]), reads=["const"], writes=["const2"])
        CK = ["const", "const2"]
        hn = fw.sb(es, "hn", [P, KC, TS], BF16)
        wi = WStream(fw, es, "wi", [P, KC, P], 2)
        wglr = fw.sb(es, "wglr", [P, KC, 16], BF16)
        fw.dma("pool", wglr[:], wglr_r.rearrange("p (k c) -> p k c", k=KC), writes=["wglr"])
        sqs = [(("sq", i), fw.sb(es, "sq%d" % i, [P, 352])) for i in range(3)]
        rss = [(("rs", i), fw.sb(es, "rs%d" % i, [P, 352])) for i in range(2)]
        carry = fw.sb(es, "carry", [P, 14])
        Hs = fw.sb(es, "Hs", [P, 4, CH])
        Scar = fw.sb(es, "Scar", [P, 2, 256])
        tl0 = fw.sb(es, "tl0", [P, TS])
        sgx = fw.sb(es, "sgx", [P, TS])
        glr = fw.sb(es, "glr", [16, TS])
        fw.op("dve", lambda e: e.memset(carry[:], 0.0), writes=["carry"])
        fw.op("dve", lambda e: e.memset(Hs[:], 0.0), writes=["Hs"])
        fw.op("dve", lambda e: e.memset(Scar[:], 0.0), writes=["Scar"])

        def inproj2(src_ap, evac, M=P, wt_override=None):
            if wt_override is None:
                wk, wt = wi.load(src_ap.rearrange("p (k c) -> p k c", k=KC))
            else:
                wk, wt = wt_override
            for (c0, n) in tiles2:
                bk, bank = fw.nb()
                for kc in range(KC):
                    fw.op("pe", lambda e: e.matmul(bank[0:M, :n], lhsT=wt[:, kc, 0:M], rhs=hn[:, kc, c0:c0 + n],
                                                   start=(kc == 0), stop=(kc == KC - 1)),
                          reads=[wk] + [("hn", kc)], writes=[bk])
                evac(bk, bank, c0, n)

        evac_flip = [0]

        def evac_copy(dst, dkey, off=0, M=P):
            def f(bk, bank, c0, n):
                evac_flip[0] ^= 1
                if evac_flip[0]:
                    fw.op("act", lambda e: e.copy(out=dst[0:M, off + c0:off + c0 + n], in_=bank[0:M, :n]), reads=[bk], writes=[dkey])
                else:
                    fw.op("dve", lambda e: e.tensor_copy(out=dst[0:M, off + c0:off + c0 + n], in_=bank[0:M, :n]), reads=[bk], writes=[dkey])
            return f

        def lerp(raw, rkey, cidx, mu_ap, om_ap, tmp, tkey, dst, dkey):
            fw.op("dve", lambda e: e.tensor_copy(out=raw[:, 0:1], in_=carry[:, cidx:cidx + 1]), reads=["carry"], writes=[rkey])
            fw.op("dve", lambda e: e.tensor_scalar(out=tmp[:, :], in0=raw[:, 1:1 + TS], scalar1=om_ap, scalar2=None, op0=ALU.mult),
                  reads=[rkey] + CK, writes=[tkey])
            fw.op("dve", lambda e: e.scalar_tensor_tensor(out=dst[:, :], in0=raw[:, 0:TS], scalar=mu_ap, in1=tmp[:, :],
                                                          op0=ALU.mult, op1=ALU.add),
                  reads=[rkey, tkey] + CK, writes=[dkey])
            fw.op("dve", lambda e: e.tensor_copy(out=carry[:, cidx:cidx + 1], in_=raw[:, TS:TS + 1]), reads=[rkey], writes=["carry"])

        def c3(t):
            return t[:, :].rearrange("p (c k) -> p c k", k=CH)

        for seg in range(NSEG):
          try:
            s0 = seg * TS
            nvalid = min(TS, L - s0)
            pcur["seg"] = seg
            pcur["pr"] = 0
            pcur["fw"] = fw
            fw.mark("start_s%d_p0" % seg)
            with ExitStack() as ea:
                hT = fw.sb(ea, "hTs", [P, KC, TS])
                for kc in range(KC):
                    if standalone:
                        fw.dma("sp", hT[:, kc, :], h1p[kc][:, s0:s0 + TS], writes=[("h", kc)])
                    else:
                        load_h(fw, kc, hT[:, kc, :], s0, [("h", kc)])
                for ti, (c0, n) in enumerate(tiles2):
                    rs_out = rss[ti % 2]
                    emit_sumsq_rstd(fw, lambda b: hT[:, b, c0:c0 + n], KC, n, ones, sqs, rs_out, 1.0 / D, EPS, lambda b: ("h", b))
                    rk, rs = rs_out
                    for kc in range(KC):
                        fw.op("dve", lambda e: e.scalar_tensor_tensor(out=hn[:, kc, c0:c0 + n], in0=hT[:, kc, c0:c0 + n], scalar=C["g1"][:, kc:kc + 1],
                                                                      in1=rs[:, :n], op0=ALU.mult, op1=ALU.mult),
                              reads=[("h", kc), rk] + CK, writes=[("hn", kc)])
                fw.barrier()
            chk("A")

            with ExitStack() as eb:
                raw = fw.sb(eb, "rawl", [P, 1 + TS])
                tmp = fw.sb(eb, "tmpl", [P, TS])
                inproj2(wl_r[0], evac_copy(raw, "rawl", off=1))
                lerp(raw, "rawl", 0, mul[:, 0:1], oml[:, 0:1], tmp, "tmpl", tl0, "tl0")
                fw.op("act", lambda e: e.activation(out=tl0[0:64, :], in_=tl0[0:64, :], func=AF.Tanh), reads=["tl0"], writes=["tl0"])
                inproj2(wl_r[1], evac_copy(raw, "rawl", off=1))
                lerp(raw, "rawl", 1, mul[:, 1:2], oml[:, 1:2], tmp, "tmpl", sgx, "sgx")
                fw.op("act", lambda e: e.activation(out=sgx[:, :], in_=sgx[:, :], func=AF.Sigmoid), reads=["sgx"], writes=["sgx"])
                inproj2(None, evac_copy(glr, "glr", M=16), M=16, wt_override=("wglr", wglr))
                fw.barrier()

            chk("B")
            eR = ExitStack()
            RC = {}

            def salloc(name, shape, dtype=F32):
                if name not in RC:
                    RC[name] = fw.sb(eR, name, shape, dtype)
                return RC[name]

            for pr in range(4):
                pcur["pr"] = pr

                def pk(n, par=pr % 2):
                    return "%s_%d" % (n, par)
                with ExitStack() as ep:
                    names = ["v", "epos", "rt0", "rt1", "at0", "at1", "kt", "bt", "kg", "bg", "gg", "bonus", "ynT"]
                    BFN = ("v", "rt0", "rt1", "at0", "at1", "kt", "bt", "kg", "bg")
                    T_ = {n: salloc("p%d_%s" % (pr % 2, n), [P, TS], BF16 if n in BFN else F32) for n in names}
                    v, epos, rt0, rt1, at0, at1, kt, bt, kg, bg, gg, bonus, ynT = (T_[n] for n in names)
                    rtz = [rt0, rt1]
                    atz = [at0, at1]
                    with ExitStack() as eq:
                        tn = ["r", "k", "sgm", "a", "kk", "kmod", "b", "cs", "e1", "e2"]
                        Q_ = {n: salloc("q_" + n, [P, TS]) for n in tn}
                        r, k, sgm, a, kk, kmod, b, cs, e1, e2 = (Q_[n] for n in tn)
                        raws = [salloc("raw%d" % i, [P, 1 + TS]) for i in range(2)]
                        for i, (dst, dkey, mi) in enumerate([(r, "r", MU_R), (k, "k", MU_K), (v, pk("v"), MU_V)]):
                            raw = raws[i % 2]
                            rkey = "raw%d" % (i % 2)
                            inproj2(wr_r[pr, i], evac_copy(raw, rkey, off=1))
                            lerp(raw, rkey, 2 + pr * 3 + i, rwp[:, mi, pr:pr + 1], omr[:, mi, pr:pr + 1], e1, "e1", dst, dkey)
                        pc = slice(pr * P, (pr + 1) * P)
                        for (c0, n) in tiles2:
                            bk, bank = fw.nb()
                            fw.op("pe", lambda e: e.matmul(bank[:, :n], lhsT=w2z[:, pc], rhs=tl0[:, c0:c0 + n], start=True, stop=True),
                                  reads=["tl0"] + CK, writes=[bk])
                            fw.op("act", lambda e: e.activation(out=sgm[:, c0:c0 + n], in_=bank[:, :n], func=AF.Sigmoid, bias=rwp[:, W0I, pr:pr + 1]),
                                  reads=[bk] + CK, writes=["sgm"])
                            bk, bank = fw.nb()
                            fw.op("pe", lambda e: e.matmul(bank[:, :n], lhsT=a2z[:, pc], rhs=tl0[:, c0:c0 + n], start=True, stop=True),
                                  reads=["tl0"] + CK, writes=[bk])
                            fw.op("act", lambda e: e.activation(out=a[:, c0:c0 + n], in_=bank[:, :n], func=AF.Sigmoid, bias=rwp[:, A0I, pr:pr + 1]),
                                  reads=[bk] + CK, writes=["a"])
                            bk, bank = fw.nb()
                            fw.op("pe", lambda e: e.matmul(bank[:, :n], lhsT=g2s[:, pc], rhs=sgx[:, c0:c0 + n], start=True, stop=True),
                                  reads=["sgx"] + CK, writes=[bk])
                            fw.op("act", lambda e: e.copy(out=gg[:, c0:c0 + n], in_=bank[:, :n]), reads=[bk], writes=[pk("gg")])
                        fw.op("dve", lambda e: e.tensor_scalar(out=kk[:, :], in0=k[:, :], scalar1=rwp[:, KKI, pr:pr + 1], scalar2=None, op0=ALU.mult),
                              reads=["k"] + CK, writes=["kk"])
                        fw.op("act", lambda e: e.activation(out=e1[:, :], in_=kk[:, :], func=AF.Square), reads=["kk"], writes=["e1"])
                        for (c0, n) in tiles2:
                            bk, bank = fw.nb()
                            fw.op("pe", lambda e: e.matmul(bank[:, :n], lhsT=bones[:, :], rhs=e1[:, c0:c0 + n], start=True, stop=True),
                                  reads=["e1"] + CK, writes=[bk])
                            fw.op("act", lambda e: e.activation(out=e2[:, c0:c0 + n], in_=bank[:, :n], func=AF.Sqrt, bias=1e-30),
                                  reads=[bk], writes=["e2"])
                        fw.op("dve", lambda e: e.tensor_scalar(out=e2[:, :], in0=e2[:, :], scalar1=1e-12, scalar2=None, op0=ALU.max),
                              reads=["e2"], writes=["e2"])
                        fw.op("dve", lambda e: e.reciprocal(out=e2[:, :], in_=e2[:, :]), reads=["e2"], writes=["e2"])
                        fw.op("dve", lambda e: e.tensor_tensor(out=kk[:, :], in0=kk[:, :], in1=e2[:, :], op=ALU.mult),
                              reads=["kk", "e2"], writes=["kk"])
                        fw.op("dve", lambda e: e.tensor_scalar(out=e1[:, :], in0=a[:, :], scalar1=rwp[:, KAI, pr:pr + 1],
                                                               scalar2=omr[:, KAI, pr:pr + 1], op0=ALU.mult, op1=ALU.add),
                              reads=["a"] + CK, writes=["e1"])
                        fw.op("dve", lambda e: e.tensor_tensor(out=kmod[:, :], in0=k[:, :], in1=e1[:, :], op=ALU.mult),
                              reads=["k", "e1"], writes=["kmod"])
                        fw.op("dve", lambda e: e.tensor_tensor(out=b[:, :], in0=kk[:, :], in1=a[:, :], op=ALU.mult),
                              reads=["kk", "a"], writes=["b"])
                        fw.op("dve", lambda e: e.scalar_tensor_tensor(out=e1[:, :], in0=r[:, :], scalar=rwp[:, RKI, pr:pr + 1], in1=kmod[:, :],
                                                                      op0=ALU.mult, op1=ALU.mult),
                              reads=["r", "kmod"] + CK, writes=["e1"])
                        for (c0, n) in tiles2:
                            bk, bank = fw.nb()
                            fw.op("pe", lambda e: e.matmul(bank[:, :n], lhsT=bones[:, :], rhs=e1[:, c0:c0 + n], start=True, stop=True),
                                  reads=["e1"] + CK, writes=[bk])
                            fw.op("dve", lambda e: e.tensor_tensor(out=bonus[:, c0:c0 + n], in0=bank[:, :n], in1=v[:, c0:c0 + n], op=ALU.mult),
                                  reads=[bk, pk("v")], writes=[pk("bonus")])
                        fw.op("dve", lambda e: e.tensor_tensor_scan(out=cs[:, :], data0=rmask[:, :], data1=sgm[:, :], initial=0.0,
                                                                    op0=ALU.mult, op1=ALU.add),
                              reads=["sgm"] + CK, writes=["cs"])
                        fw.op("act", lambda e: e.activation(out=epos[:, :], in_=cs[:, :], func=AF.Exp, scale=-C05), reads=["cs"], writes=[pk("epos")])
                        for h in range(2):
                            fw.op("dve", lambda e: e.scalar_tensor_tensor(out=rtz[h][:, :], in0=r[:, :], scalar=hm[:, h:h + 1], in1=epos[:, :],
                                                                          op0=ALU.mult, op1=ALU.mult),
                                  reads=["r", pk("epos")] + CK, writes=[pk("rt")])
                        fw.op("act", lambda e: e.activation(out=e2[:, :], in_=cs[:, :], func=AF.Exp, scale=C05), reads=["cs"], writes=["e2"])
                        fw.op("dve", lambda e: e.tensor_tensor(out=kt[:, :], in0=kmod[:, :], in1=e2[:, :], op=ALU.mult),
                              reads=["kmod", "e2"], writes=[pk("kt")])
                        fw.op("dve", lambda e: e.tensor_tensor(out=bt[:, :], in0=b[:, :], in1=e2[:, :], op=ALU.mult),
                              reads=["b", "e2"], writes=[pk("bt")])
                        fw.op("dve", lambda e: e.tensor_tensor(out=e2[:, :], in0=cs[:, :], in1=sgm[:, :], op=ALU.subtract),
                              reads=["cs", "sgm"], writes=["e2"])
                        fw.op("act", lambda e: e.activation(out=e2[:, :], in_=e2[:, :], func=AF.Exp, scale=-C05), reads=["e2"], writes=["e2"])
                        for h in range(2):
                            fw.op("dve", lambda e: e.scalar_tensor_tensor(out=atz[h][:, :], in0=kk[:, :], scalar=hm[:, 2 + h:3 + h], in1=e2[:, :],
                                                                          op0=ALU.mult, op1=ALU.mult),
                                  reads=["kk", "e2"] + CK, writes=[pk("at")])
                        fw.op("dve", lambda e: e.tensor_tensor(out=c3(e2), in0=c3(cs)[:, :, CH - 1:CH].to_broadcast([P, NCH, CH]), in1=c3(cs),
                                                               op=ALU.subtract),
                              reads=["cs"], writes=["e2"])
                        fw.op("act", lambda e: e.activation(out=e2[:, :], in_=e2[:, :], func=AF.Exp, scale=-C05), reads=["e2"], writes=["e2"])
                        fw.op("dve", lambda e: e.tensor_tensor(out=kg[:, :], in0=kmod[:, :], in1=e2[:, :], op=ALU.mult),
                              reads=["kmod", "e2"], writes=[pk("kg")])
                        fw.op("dve", lambda e: e.tensor_tensor(out=bg[:, :], in0=b[:, :], in1=e2[:, :], op=ALU.mult),
                              reads=["b", "e2"], writes=[pk("bg")])
                    chk("Cpre")
                    with ExitStack() as ec:
                        tok = salloc("tok", [CH, NCH, 3, P], BF16)
                        LA = [salloc("LA%d" % i, [CH, NCH, 4, CH], BF16) for i in range(2)]
                        A3 = salloc("A3", [CH, NCH, 6, CH], BF16)
                        Tt = salloc("Tt", [CH, NCH, 2, CH], BF16)
                        Tt32 = salloc("Tt32", [CH, NCH, 2, CH])
                        Wsb = [salloc("Wsb%d" % i, [CH, P], BF16) for i in range(2)]
                        Usb = salloc("Usb", [CH, NCH, P], BF16)
                        Hs16 = salloc("Hs16", [P, NCH + 1, CH], BF16)
                        fw.op("act", lambda e: e.copy(out=Hs16[:, 0, :], in_=Hs[:, pr, :]), reads=[("Hs", pr)], writes=[("Hs16", 0)])
                        Ysb = salloc("Ysb", [CH, 2, P])
                        gsq = salloc("gsq", [P, TS])
                        gmean = salloc("gmean", [P, TS])
                        grs = salloc("grs", [P, TS])
                        hps = [slice(0, 64), slice(64, 128)]
                        for c in range(NCH):
                            cc = slice(c * CH, (c + 1) * CH)
                            bk, bank = fw.nb()
                            bankb = bank[0:CH, 0:3 * P // 2].bitcast(BF16)
                            for i, (src, skey) in enumerate([(v, pk("v")), (kg, pk("kg")), (bg, pk("bg"))]):
                                fw.op("pe", lambda e: e.transpose(bankb[:, i * P:(i + 1) * P], src[:, cc], identb[:, :]),
                                      reads=[skey] + CK, writes=[bk])
                            srcv = bankb[:, 0:3 * P].rearrange("p (a b) -> p a b", a=3)
                            if c % 2 == 0:
                                fw.op("dve", lambda e: e.tensor_copy(out=tok[:, c, :, :], in_=srcv), reads=[bk], writes=[("tok", c)])
                            else:
                                fw.op("act", lambda e: e.copy(out=tok[:, c, :, :], in_=srcv), reads=[bk], writes=[("tok", c)])
                        chk("C1")
                        for c in range(NCH):
                            cc = slice(c * CH, (c + 1) * CH)
                            bk, bank = fw.nb()
                            for h in range(2):
                                hp = hps[h]
                                fw.op("pe", lambda e: e.matmul(bank[0:CH, h * CH:(h + 1) * CH], lhsT=bt[:, cc], rhs=atz[h][:, cc], start=True, stop=True),
                                      reads=[pk("bt"), pk("at")], writes=[bk])
                                fw.op("pe", lambda e: e.matmul(bank[0:CH, 128 + h * CH:128 + (h + 1) * CH], lhsT=atz[h][:, cc], rhs=bt[:, cc], start=True, stop=True),
                                      reads=[pk("bt"), pk("at")], writes=[bk])
                            fw.op("dve", lambda e: e.tensor_tensor(out=LA[0][:, c, :, :], in0=bank[0:CH, 0:256].rearrange("p (a b) -> p a b", a=4),
                                                                   in1=M4[:, :].rearrange("p (a b) -> p a b", a=4), op=ALU.mult),
                                  reads=[bk] + CK, writes=[("LA0", c)])
                            bk, bank = fw.nb()
                            for h in range(2):
                                hp = hps[h]
                                fw.op("pe", lambda e: e.matmul(bank[0:CH, h * CH:(h + 1) * CH], lhsT=kt[:, cc], rhs=rtz[h][:, cc], start=True, stop=True),
                                      reads=[pk("kt"), pk("rt")], writes=[bk])
                                fw.op("pe", lambda e: e.matmul(bank[0:CH, 128 + h * CH:128 + (h + 1) * CH], lhsT=bt[:, cc], rhs=rtz[h][:, cc], start=True, stop=True),
                                      reads=[pk("bt"), pk("rt")], writes=[bk])
                                fw.op("pe", lambda e: e.matmul(bank[0:CH, 256 + h * CH:256 + (h + 1) * CH], lhsT=kt[:, cc], rhs=atz[h][:, cc], start=True, stop=True),
                                      reads=[pk("kt"), pk("at")], writes=[bk])
                            fw.op("dve", lambda e: e.tensor_tensor(out=A3[:, c, :, :], in0=bank[0:CH, 0:384].rearrange("p (a b) -> p a b", a=6),
                                                                   in1=M5[:, :].rearrange("p (a b) -> p a b", a=6), op=ALU.mult),
                                  reads=[bk] + CK, writes=[("A3", c)])
                            fw.op("dve", lambda e: e.tensor_tensor(out=Tt32[:, c, :, :], in0=LA[0][:, c, 0:2, :],
                                                                   in1=ident[0:CH, 0:CH].unsqueeze(1).to_broadcast([CH, 2, CH]), op=ALU.add),
                                  reads=[("LA0", c)] + CK, writes=[("Tt32", c)])
                            fw.op("act", lambda e: e.copy(out=Tt[:, c, :, :], in_=Tt32[:, c, :, :]), reads=[("Tt32", c)], writes=[("Tt", c)])
                        chk("C2")
                        cur = 0
                        for lvl in range(1, 6):
                            nxt = 1 - cur
                            last = (lvl == 5)
                            for c in range(NCH):
                                bk, bank = fw.nb()
                                for h in range(2):
                                    Lh = LA[cur][:, c, h, :]
                                    Ah = LA[cur][:, c, 2 + h, :]
                                    fw.op("pe", lambda e: e.matmul(bank[0:CH, 128 + h * CH:128 + (h + 1) * CH], lhsT=Lh, rhs=Ah, start=True, stop=True),
                                          reads=[("LA%d" % cur, c)], writes=[bk])
                                    if not last:
                                        fw.op("pe", lambda e: e.matmul(bank[0:CH, h * CH:(h + 1) * CH], lhsT=Ah, rhs=Lh, start=True, stop=True),
                                              reads=[("LA%d" % cur, c)], writes=[bk])
                                lo = 128 if last else 0
                                dstv = LA[nxt][:, c, lo // CH:4, :]
                                srcv = bank[0:CH, lo:256].rearrange("p (a b) -> p a b", b=CH)
                                if c % 2 == 0:
                                    fw.op("act", lambda e: e.copy(out=dstv, in_=srcv), reads=[bk], writes=[("LA%d" % nxt, c)])
                                else:
                                    fw.op("dve", lambda e: e.tensor_copy(out=dstv, in_=srcv), reads=[bk], writes=[("LA%d" % nxt, c)])
                            for c in range(NCH):
                                bk, bank = fw.nb()
                                for h in range(2):
                                    fw.op("pe", lambda e: e.matmul(bank[0:CH, h * CH:(h + 1) * CH], lhsT=LA[nxt][:, c, 2 + h, :], rhs=Tt[:, c, h, :],
                                                                   start=True, stop=True),
                                          reads=[("LA%d" % nxt, c), ("Tt", c)], writes=[bk])
                                fw.op("dve", lambda e: e.tensor_tensor(out=Tt32[:, c, :, :], in0=Tt32[:, c, :, :],
                                                                       in1=bank[0:CH, 0:128].rearrange("p (a b) -> p a b", a=2), op=ALU.add),
                                      reads=[bk, ("Tt32", c)], writes=[("Tt32", c)])
                                fw.op("act", lambda e: e.copy(out=Tt[:, c, :, :], in_=Tt32[:, c, :, :]), reads=[("Tt32", c)], writes=[("Tt", c)])
                            cur = nxt
                        chk("C3")
                        for c in range(NCH):
                            cc = slice(c * CH, (c + 1) * CH)
                            bk, bank = fw.nb()
                            for h in range(2):
                                hp = hps[h]
                                fw.op("pe", lambda e: e.matmul(bank[0:CH, h * CH:(h + 1) * CH], lhsT=atz[h][:, cc], rhs=Hs16[:, c, :], start=True, stop=False),
                                      reads=[pk("at"), ("Hs16", c)], writes=[bk])
                                fw.op("pe", lambda e: e.matmul(bank[0:CH, h * CH:(h + 1) * CH], lhsT=A3[:, c, 4 + h, :], rhs=tok[:, c, 0, h * CH:(h + 1) * CH],
                                                               start=False, stop=True),
                                      reads=[("A3", c), ("tok", c)], writes=[bk])
                            W = Wsb[c % 2]
                            wkey = ("Wsb", c % 2)
                            fw.op("act", lambda e: e.copy(out=W[:, :], in_=bank[0:CH, 0:P]), reads=[bk], writes=[wkey])
                            bk, bank = fw.nb()
                            for h in range(2):
                                fw.op("pe", lambda e: e.matmul(bank[0:CH, h * CH:(h + 1) * CH], lhsT=Tt[:, c, h, :], rhs=W[:, h * CH:(h + 1) * CH],
                                                               start=True, stop=True),
                                      reads=[("Tt", c), wkey], writes=[bk])
                            fw.op("dve", lambda e: e.tensor_copy(out=Usb[:, c, :], in_=bank[0:CH, 0:P]), reads=[bk], writes=[("Usb", c)])
                            bk, bank = fw.nb()
                            for h in range(2):
                                fw.op("pe", lambda e: e.matmul(bank[:, h * CH:(h + 1) * CH], lhsT=tok[:, c, 2, :], rhs=Usb[:, c, h * CH:(h + 1) * CH],
                                                               start=True, stop=False),
                                      reads=[("tok", c), ("Usb", c)], writes=[bk])
                                fw.op("pe", lambda e: e.matmul(bank[:, h * CH:(h + 1) * CH], lhsT=tok[:, c, 1, :], rhs=tok[:, c, 0, h * CH:(h + 1) * CH],
                                                               start=False, stop=True),
                                      reads=[("tok", c)], writes=[bk])
                            for h in range(2):
                                hp = hps[h]
                                fw.op("dve", lambda e: e.scalar_tensor_tensor(out=Hs16[hp, c + 1, :], in0=Hs[hp, pr, :],
                                                                              scalar=epos[hp, c * CH + CH - 1:c * CH + CH],
                                                                              in1=bank[hp, h * CH:(h + 1) * CH], op0=ALU.mult, op1=ALU.add),
                                      reads=[bk, ("Hs", pr), pk("epos")], writes=[("Hs16", c + 1)])
                            for h in range(2):
                                hp = hps[h]
                                fw.op("dve", lambda e: e.scalar_tensor_tensor(out=Hs[hp, pr, :], in0=Hs[hp, pr, :],
                                                                              scalar=epos[hp, c * CH + CH - 1:c * CH + CH],
                                                                              in1=bank[hp, h * CH:(h + 1) * CH], op0=ALU.mult, op1=ALU.add),
                                      reads=[bk, ("Hs", pr), pk("epos")], writes=[("Hs", pr)])
                        chk("C4")
                        for c in range(NCH):
                            cc = slice(c * CH, (c + 1) * CH)
                            bk, bank = fw.nb()
                            for h in range(2):
                                hp = hps[h]
                                o_ = bank[0:CH, h * CH:(h + 1) * CH]
                                fw.op("pe", lambda e: e.matmul(o_, lhsT=rtz[h][:, cc], rhs=Hs16[:, c, :], start=True, stop=False),
                                      reads=[pk("rt"), ("Hs16", c)], writes=[bk])
                                fw.op("pe", lambda e: e.matmul(o_, lhsT=A3[:, c, 2 + h, :], rhs=Usb[:, c, h * CH:(h + 1) * CH], start=False, stop=False),
                                      reads=[("A3", c), ("Usb", c)], writes=[bk])
                                fw.op("pe", lambda e: e.matmul(o_, lhsT=A3[:, c, h, :], rhs=tok[:, c, 0, h * CH:(h + 1) * CH], start=False, stop=True),
                                      reads=[("A3", c), ("tok", c)], writes=[bk])
                            fw.op("act", lambda e: e.copy(out=Ysb[:, c % 2, :], in_=bank[0:CH, 0:P]), reads=[bk], writes=[("Ysb", c % 2)])
                            bk2, bank2 = fw.nb()
                            fw.op("pe", lambda e: e.transpose(bank2[:, 0:CH], Ysb[:, c % 2, :], ident[0:CH, 0:CH]), reads=[("Ysb", c % 2)] + CK, writes=[bk2])
                            fw.op("dve", lambda e: e.tensor_copy(out=ynT[:, cc], in_=bank2[:, 0:CH]), reads=[bk2], writes=[pk("ynT")])
                        chk("C5a")
                        fw.op("act", lambda e: e.activation(out=gsq[:, :], in_=ynT[:, :], func=AF.Square), reads=[pk("ynT")], writes=["gsq"])
                        for (c0, n) in tiles2:
                            bk, bank = fw.nb()
                            fw.op("pe", lambda e: e.matmul(bank[:, :n], lhsT=bones[:, :], rhs=ynT[:, c0:c0 + n], start=True, stop=True),
                                  reads=[pk("ynT")] + CK, writes=[bk])
                            fw.op("act", lambda e: e.mul(out=gmean[:, c0:c0 + n], in_=bank[:, :n], mul=1.0 / CH), reads=[bk], writes=["gmean"])
                            bk, bank = fw.nb()
                            fw.op("pe", lambda e: e.matmul(bank[:, :n], lhsT=bones[:, :], rhs=gsq[:, c0:c0 + n], start=True, stop=True),
                                  reads=["gsq"] + CK, writes=[bk])
                            fw.op("act", lambda e: e.mul(out=grs[:, c0:c0 + n], in_=bank[:, :n], mul=1.0 / CH), reads=[bk], writes=["grs"])
                        fw.op("dve", lambda e: e.tensor_tensor(out=gsq[:, :], in0=gmean[:, :], in1=gmean[:, :], op=ALU.mult), reads=["gmean"], writes=["gsq"])
                        fw.op("dve", lambda e: e.tensor_tensor(out=grs[:, :], in0=grs[:, :], in1=gsq[:, :], op=ALU.subtract), reads=["grs", "gsq"], writes=["grs"])
                        fw.op("act", lambda e: e.activation(out=grs[:, :], in_=grs[:, :], func=AF.Sqrt, bias=GN_EPS), reads=["grs"], writes=["grs"])
                        fw.op("dve", lambda e: e.reciprocal(out=grs[:, :], in_=grs[:, :]), reads=["grs"], writes=["grs"])
                        fw.op("dve", lambda e: e.tensor_tensor(out=ynT[:, :], in0=ynT[:, :], in1=gmean[:, :], op=ALU.subtract), reads=[pk("ynT"), "gmean"], writes=[pk("ynT")])
                        fw.op("dve", lambda e: e.tensor_tensor(out=ynT[:, :], in0=ynT[:, :], in1=grs[:, :], op=ALU.mult), reads=[pk("ynT"), "grs"], writes=[pk("ynT")])
                        fw.op("act", lambda e: e.activation(out=ynT[:, :], in_=ynT[:, :], func=AF.Identity,
                                                            scale=rwp[:, LGI, pr:pr + 1], bias=rwp[:, LBI, pr:pr + 1]),
                              reads=[pk("ynT")] + CK, writes=[pk("ynT")])
                        chk("C5c")
                        fw.op("dve", lambda e: e.tensor_tensor(out=ynT[:, :], in0=ynT[:, :], in1=bonus[:, :], op=ALU.add),
                              reads=[pk("ynT"), pk("bonus")], writes=[pk("ynT")])
                        fw.op("dve", lambda e: e.tensor_tensor(out=ynT[:, :], in0=ynT[:, :], in1=gg[:, :], op=ALU.mult),
                              reads=[pk("ynT"), pk("gg")], writes=[pk("ynT")])
                        if standalone:
                            fw.dma("sp", oy[4 + pr][:, s0:s0 + nvalid], ynT[:, 0:nvalid], reads=[pk("ynT")])
                        else:
                            store(fw, 4 + pr, ynT[:, 0:nvalid], s0, nvalid, [pk("ynT")])
                        chk("C5d")

            fw.barrier()
            eR.close()
            chk("C5")
            eG = ExitStack()
            GC_ = {}

            def galloc(name, shape, dtype=F32):
                if name not in GC_:
                    GC_[name] = fw.sb(eG, name, shape, dtype)
                return GC_[name]

            for hh in range(2):
                pcur["pr"] = 4 + hh

                def gk_(n, par=hh % 2):
                    return "%s_g%d" % (n, par)
                fw.mark("gla_s%d_p%d" % (seg, 4 + hh))
                with ExitStack() as eg:
                    gn = ["q", "k", "v0", "v1", "go0", "go1", "ls", "cs", "e1", "e2", "qd", "kd", "kgm", "ob"]
                    GBF = ("v0", "v1", "qd", "kd", "kgm")
                    G_ = {n: galloc("g%d_%s" % (hh % 2, n), [P, TS], BF16 if n in GBF else F32) for n in gn}
                    q, k, v0, v1, go0, go1, ls, cs, e1, e2, qd, kd, kgm, ob = (G_[n] for n in gn)
                    oT = galloc("g%d_oT" % (hh % 2), [P, 2, TS])
                    vtok = galloc("vtok", [CH, NCH, 3, P], BF16)
                    sc = galloc("sc", [CH, NCH, CH], BF16)
                    Ss = galloc("Ss", [P, 256])
                    Ss16 = galloc("Ss16", [P, NCH + 1, 256], BF16)
                    rsg = galloc("g%d_rsg" % (hh % 2), [P, TS])
                    for i, (dst, dkey) in enumerate([(q, gk_("q")), (k, gk_("gk")), (v0, gk_("v0")), (v1, gk_("v1")), (go0, gk_("go0")), (go1, gk_("go1"))]):
                        inproj2(wg_r[hh, i], evac_copy(dst, dkey))
                    hc = slice(hh * P, (hh + 1) * P)
                    for (c0, n) in tiles2:
                        bk, bank = fw.nb()
                        fw.op("pe", lambda e: e.matmul(bank[:, :n], lhsT=gw2[0:16, hc], rhs=glr[0:16, c0:c0 + n], start=True, stop=True),
                              reads=["glr"] + CK, writes=[bk])
                        fw.op("act", lambda e: e.activation(out=e1[:, c0:c0 + n], in_=bank[:, :n], func=AF.Sigmoid, bias=glab[:, hh:hh + 1]),
                              reads=[bk] + CK, writes=[gk_("ge1")])
                    fw.op("act", lambda e: e.activation(out=ls[:, :], in_=e1[:, :], func=AF.Ln), reads=[gk_("ge1")], writes=[gk_("ls")])
                    fw.op("dve", lambda e: e.tensor_tensor_scan(out=cs[:, :], data0=rmask[:, :], data1=ls[:, :], initial=0.0,
                                                                op0=ALU.mult, op1=ALU.add),
                          reads=[gk_("ls")] + CK, writes=[gk_("gcs")])
                    fw.op("act", lambda e: e.activation(out=e1[:, :], in_=cs[:, :], func=AF.Exp, scale=1.0 / 16), reads=[gk_("gcs")], writes=[gk_("ge1")])
                    fw.op("dve", lambda e: e.scalar_tensor_tensor(out=qd[:, :], in0=q[:, :], scalar=float(128 ** -0.5), in1=e1[:, :],
                                                                  op0=ALU.mult, op1=ALU.mult),
                          reads=[gk_("q"), gk_("ge1")], writes=[gk_("qd")])
                    fw.op("act", lambda e: e.activation(out=e2[:, :], in_=cs[:, :], func=AF.Exp, scale=-1.0 / 16), reads=[gk_("gcs")], writes=[gk_("ge2")])
                    fw.op("dve", lambda e: e.tensor_tensor(out=kd[:, :], in0=k[:, :], in1=e2[:, :], op=ALU.mult), reads=[gk_("gk"), gk_("ge2")], writes=[gk_("kd")])
                    fw.op("dve", lambda e: e.tensor_tensor(out=c3(e2), in0=c3(cs)[:, :, CH - 1:CH].to_broadcast([P, NCH, CH]), in1=c3(cs),
                                                           op=ALU.subtract),
                          reads=[gk_("gcs")], writes=[gk_("ge2")])
                    fw.op("act", lambda e: e.activation(out=e2[:, :], in_=e2[:, :], func=AF.Exp, scale=1.0 / 16), reads=[gk_("ge2")], writes=[gk_("ge2")])
                    fw.op("dve", lambda e: e.tensor_tensor(out=kgm[:, :], in0=k[:, :], in1=e2[:, :], op=ALU.mult), reads=[gk_("gk"), gk_("ge2")], writes=[gk_("kgm")])
                    fw.op("dve", lambda e: e.tensor_copy(out=Ss[:, :], in_=Scar[:, hh, :]), reads=["Scar"], writes=["Ss"])
                    fw.op("act", lambda e: e.copy(out=Ss16[:, 0, :], in_=Scar[:, hh, :]), reads=["Scar"], writes=[("Ss16", 0)])
                    for c in range(NCH):
                        cc = slice(c * CH, (c + 1) * CH)
                        bk, bank = fw.nb()
                        bankb = bank[0:CH, 0:3 * P // 2].bitcast(BF16)
                        for i, (src, skey) in enumerate([(v0, gk_("v0")), (v1, gk_("v1")), (kgm, gk_("kgm"))]):
                            fw.op("pe", lambda e: e.transpose(bankb[:, i * P:(i + 1) * P], src[:, cc], identb[:, :]),
                                  reads=[skey] + CK, writes=[bk])
                        srcv = bankb[:, 0:3 * P].rearrange("p (a b) -> p a b", a=3)
                        if c % 2 == 0:
                            fw.op("dve", lambda e: e.tensor_copy(out=vtok[:, c, :, :], in_=srcv), reads=[bk], writes=[("vtok", c)])
                        else:
                            fw.op("act", lambda e: e.copy(out=vtok[:, c, :, :], in_=srcv), reads=[bk], writes=[("vtok", c)])
                        bk, bank = fw.nb()
                        fw.op("pe", lambda e: e.matmul(bank[0:CH, 0:CH], lhsT=kd[:, cc], rhs=qd[:, cc], start=True, stop=True),
                              reads=[gk_("kd"), gk_("qd")], writes=[bk])
                        fw.op("dve", lambda e: e.tensor_tensor(out=sc[:, c, :], in0=bank[0:CH, 0:CH], in1=mincl[:, :], op=ALU.mult),
                              reads=[bk] + CK, writes=[("sc", c)])
                    for c in range(NCH):
                        bk, bank = fw.nb()
                        fw.op("pe", lambda e: e.matmul(bank[:, 0:256], lhsT=vtok[:, c, 2, :], rhs=vtok[:, c, 0:2, :], start=True, stop=True),
                              reads=[("vtok", c)], writes=[bk])
                        fw.op("dve", lambda e: e.scalar_tensor_tensor(out=Ss16[:, c + 1, :], in0=Ss[:, :], scalar=e1[:, c * CH + CH - 1:c * CH + CH],
                                                                      in1=bank[:, 0:256], op0=ALU.mult, op1=ALU.add),
                              reads=[bk, "Ss", gk_("ge1")], writes=[("Ss16", c + 1)])
                        fw.op("dve", lambda e: e.scalar_tensor_tensor(out=Ss[:, :], in0=Ss[:, :], scalar=e1[:, c * CH + CH - 1:c * CH + CH],
                                                                      in1=bank[:, 0:256], op0=ALU.mult, op1=ALU.add),
                              reads=[bk, "Ss", gk_("ge1")], writes=["Ss"])
                    fw.op("dve", lambda e: e.tensor_copy(out=Scar[:, hh, :], in_=Ss[:, :]), reads=["Ss"], writes=["Scar"])
                    for c in range(NCH):
                        cc = slice(c * CH, (c + 1) * CH)
                        bk, bank = fw.nb()
                        for vb in range(2):
                            o_ = bank[:, vb * CH:(vb + 1) * CH]
                            fw.op("pe", lambda e: e.matmul(o_, lhsT=vtok[:, c, vb, :], rhs=sc[:, c, :], start=True, stop=False),
                                  reads=[("vtok", c), ("sc", c)], writes=[bk])
                            fw.op("pe", lambda e: e.matmul(o_, lhsT=Ss16[:, c, vb * P:(vb + 1) * P], rhs=qd[:, cc], start=False, stop=True),
                                  reads=[("Ss16", c), gk_("qd")], writes=[bk])
                        srcv = bank[:, 0:2 * CH].rearrange("p (a b) -> p a b", a=2)
                        if c % 2 == 0:
                            fw.op("act", lambda e: e.copy(out=oT[:, :, cc], in_=srcv), reads=[bk], writes=[gk_("oT")])
                        else:
                            fw.op("dve", lambda e: e.tensor_copy(out=oT[:, :, cc], in_=srcv), reads=[bk], writes=[gk_("oT")])
                    for (c0, n) in tiles2:
                        bk, bank = fw.nb()
                        for vb in range(2):
                            sk, sq = sqs[vb]
                            fw.op("act", lambda e: e.activation(out=sq[:, :n], in_=oT[:, vb, c0:c0 + n], func=AF.Square), reads=[gk_("oT")], writes=[sk])
                            fw.op("pe", lambda e: e.matmul(bank[:, :n], lhsT=ones[:, :], rhs=sq[:, :n], start=(vb == 0), stop=(vb == 1)),
                                  reads=[sk] + CK, writes=[bk])
                        fw.op("act", lambda e: e.activation(out=rsg[:, c0:c0 + n], in_=bank[:, :n], func=AF.Sqrt, scale=1.0 / 256, bias=EPS),
                              reads=[bk], writes=[gk_("rsg")])
                    fw.op("dve", lambda e: e.reciprocal(out=rsg[:, :], in_=rsg[:, :]), reads=[gk_("rsg")], writes=[gk_("rsg")])
                    for vb, (go, gokey) in enumerate([(go0, gk_("go0")), (go1, gk_("go1"))]):
                        fw.op("dve", lambda e: e.scalar_tensor_tensor(out=ob[:, :], in0=oT[:, vb, :], scalar=glan[:, hh * 2 + vb:hh * 2 + vb + 1],
                                                                      in1=rsg[:, :], op0=ALU.mult, op1=ALU.mult),
                              reads=[gk_("oT"), gk_("rsg")] + CK, writes=[gk_("ob")])
                        fw.op("act", lambda e: e.activation(out=e2[:, :], in_=go[:, :], func=AF.Silu), reads=[gokey], writes=[gk_("ge2")])
                        fw.op("dve", lambda e: e.tensor_tensor(out=ob[:, :], in0=ob[:, :], in1=e2[:, :], op=ALU.mult), reads=[gk_("ob"), gk_("ge2")], writes=[gk_("ob")])
                        if standalone:
                            fw.dma("sp", oy[hh * 2 + vb][:, s0:s0 + nvalid], ob[:, 0:nvalid], reads=[gk_("ob")])
                        else:
                            store(fw, hh * 2 + vb, ob[:, 0:nvalid], s0, nvalid, [gk_("ob")])
            fw.barrier()
            eG.close()
          except StopBuild:
            stopped[0] = True
            break
          except AssertionError:
            if stopped[0]:
                break
            raise
        fw.mark(None)
        if standalone:
            fw.finish("sp")
        else:
            fw.barrier()
        fw.defer = False
        print("K2: ins", fw.nins, "waits", fw.nwaits)
        if not stopped[0]:
            es.close()
    return nc


def prep_k2_common(inp):
    f = np.float32
    GC = 3088
    w_in = inp["cd_w_in"][0]
    mu = inp["cd_rw_mu"][0]
    s_le = np.triu(np.ones((CH, CH), f))
    s_lt = np.triu(np.ones((CH, CH), f), 1)
    t_gt = np.tril(np.ones((CH, CH), f), -1)
    rmask = np.ones((P, TS), f)
    rmask[:, ::CH] = 0.0
    bones = np.zeros((P, P), f)
    bones[:64, :64] = 1
    bones[64:, 64:] = 1
    common = prep_common()
    hm = np.zeros((P, 4), f)
    hm[:64, 0] = 1
    hm[64:, 1] = 1
    hm[:64, 2] = -1
    hm[64:, 3] = -1
    common.update(hm=hm, bones=bones, M4=np.concatenate([s_lt, s_lt, t_gt, t_gt], 1), M5=np.concatenate([s_le] * 4 + [s_lt] * 2, 1),
                  mincl=s_le, rmask=rmask, g1=vec_layout(inp["norm_mix"][1]))

    def blk(c0):
        return blk_layout(w_in[:, c0:c0 + P])[0]

    per_hs = []
    for hs in range(2):
        m = {}
        m["wr_r"] = np.stack([np.stack([blk(GC + i * 1024 + (hs * 4 + pr) * P) for i in range(3)]) for pr in range(4)])
        m["wl_r"] = np.stack([blk(GC + 3072), blk(GC + 3200)])
        gl = []
        for hh in range(2):
            gh = hs * 2 + hh
            gl.append(np.stack([blk(gh * P), blk(512 + gh * P), blk(1024 + gh * 256), blk(1024 + gh * 256 + P),
                                blk(2048 + gh * 256), blk(2048 + gh * 256 + P)]))
        m["wg_r"] = np.stack(gl)
        wg = w_in[:, 3072:3088]
        m["wglr_r"] = np.ascontiguousarray(wg.reshape(KC, P, 16).transpose(1, 0, 2).reshape(P, KC * 16))
        ch = slice(hs * 512, (hs + 1) * 512)

        def pv(vec):
            return vec.reshape(4, P).T
        rwp = np.stack([pv(mu[0:1024][ch]), pv(mu[1024:2048][ch]), pv(mu[2048:3072][ch]), pv(inp["cd_rw_w0"][0][ch]),
                        pv(inp["cd_rw_a0"][0][ch]), pv(inp["cd_rw_kk"][0][ch]), pv(inp["cd_rw_ka"][0][ch]),
                        pv(inp["cd_rw_rk"][0][ch]), pv(inp["cd_rw_ln_g"][0][ch]), pv(inp["cd_rw_ln_b"][0][ch])], axis=1)
        m["rwp"] = np.ascontiguousarray(rwp.astype(f))
        m["mul"] = np.ascontiguousarray(np.stack([mu[3072:3200], mu[3200:3328]], 1))
        z64 = np.zeros((64, 512), f)
        m["w2z"] = np.ascontiguousarray(np.concatenate([inp["cd_rw_w2"][0][:, ch], z64], 0))
        m["a2z"] = np.ascontiguousarray(np.concatenate([z64, inp["cd_rw_a2"][0][:, ch]], 0))
        m["g2s"] = np.ascontiguousarray(inp["cd_rw_g2"][0][:, ch])
        gch = slice(hs * 256, (hs + 1) * 256)
        m["gw2"] = np.ascontiguousarray(inp["cd_gla_w2"][0][:, gch])
        m["glab"] = np.ascontiguousarray(inp["cd_gla_b"][0][gch].reshape(2, P).T)
        m["glan"] = np.ascontiguousarray(inp["cd_gla_norm_g"][0][ch].reshape(4, P).T)
        per_hs.append(m)
    return common, per_hs


def prep_k2(inp, h1T_list):
    f = np.float32
    common, per_hs = prep_k2_common(inp)
    maps = []
    for b in range(len(h1T_list)):
        hp = np.concatenate([h1T_list[b], np.zeros((D, TP - L), f)], axis=1).reshape(KC, P, TP)
        hp = np.ascontiguousarray(hp)
        for hs in range(2):
            m = dict(common)
            m.update(per_hs[hs])
            m["h1p"] = hp
            maps.append(m)
    return maps


def assemble_oy(res_pair):
    out = np.empty((KC, P, L), np.float32)
    for hs in range(2):
        out[hs * 4:hs * 4 + 4] = res_pair[hs][0:4]
        out[8 + hs * 4:8 + hs * 4 + 4] = res_pair[hs][4:8]
    return out.reshape(D, L)


def kernel_unfused(**inputs):
    inp = {k: np.ascontiguousarray(np.asarray(v, dtype=np.float32)) for k, v in inputs.items()}
    B = inp["x"].shape[0]
    cores = list(range(2 * B))
    r1 = run_bass_kernel_spmd(build_k1(), prep_k1(inp), core_ids=cores)
    h1T = [np.concatenate([np.asarray(r1.results[2 * b + h]["h1T"]).reshape(D, HALF) for h in range(2)], axis=1) for b in range(B)]
    r2 = run_bass_kernel_spmd(build_k2(), prep_k2(inp, h1T), core_ids=cores)
    oyT = [assemble_oy([np.asarray(r2.results[2 * b + h]["oy"]) for h in range(2)]) for b in range(B)]
    r3 = run_bass_kernel_spmd(build_k3(), prep_k3(inp, h1T, oyT), core_ids=cores)
    out = np.stack([np.concatenate([np.asarray(r3.results[2 * b + h]["out"]) for h in range(2)], axis=0)[N_META:] for b in range(B)])
    return np.ascontiguousarray(out.astype(np.float32))


def build_fused():
    nc = bass.Bass("TRN2", target_bir_lowering=False)
    h1s = nc.dram_tensor("h1s", [KC, P, 2 + TP], F32).ap()
    oys = nc.dram_tensor("oys", [KC, P, 2 + L], F32).ap()
    sel_d = nc.dram_tensor("sel", [P, 2], F32, kind="ExternalInput").ap()
    with ExitStack() as es:
        fw = FW(nc, es)
        ctx = (nc, fw)
        with ExitStack() as ez:
            z = fw.sb(ez, "zeros", [P, 64])
            fw.op("dve", lambda e: e.memset(z[:], 0.0), writes=["z"])
            for kc in range(KC):
                fw.dma("sp", h1s[kc][:, 0:2], z[:, 0:2], reads=["z"])
                fw.dma("sp", h1s[kc][:, 2 + L:2 + TP], z[:, 0:TP - L], reads=["z"])
                fw.dma("sp", oys[kc][:, 0:2], z[:, 0:2], reads=["z"])
            fw.barrier()
        for half, nm in enumerate(["A", "B"]):
            build_k1(ctx=ctx, pfx="k1_", overrides={"xin": "k1_xin" + nm, "fmask": "k1_fmask" + nm},
                     store=lambda fw_, kc, src, reads, half=half: fw_.dma(
                         "sp", h1s[kc][:, 2 + half * HALF:2 + (half + 1) * HALF], src, reads=reads))
        for hs, nm in enumerate(["a", "b"]):
            build_k2(ctx=ctx, pfx="k2%s_" % nm,
                     load_h=lambda fw_, kc, dst, s0, writes: fw_.dma("sp", dst, h1s[kc][:, 2 + s0:2 + s0 + TS], writes=writes),
                     store=lambda fw_, blk, src, s0, nvalid, reads, hs=hs: fw_.dma(
                         "sp", oys[(hs * 4 + blk) if blk < 4 else (4 + hs * 4 + blk)][:, 2 + s0:2 + s0 + nvalid], src, reads=reads))

        def blend_load(dst, src_dram, dkey):
            with ExitStack() as eb:
                sel = fw.sb(eb, "sel", [P, 2])
                fw.dma("sp", sel[:], sel_d, writes=["sel"])
                tA = [fw.sb(eb, "tA%d" % i, [P, WM]) for i in range(2)]
                tB = [fw.sb(eb, "tB%d" % i, [P, WM]) for i in range(2)]
                for kc in range(KC):
                    a_, b_ = tA[kc % 2], tB[kc % 2]
                    ak, bk = ("tA", kc % 2), ("tB", kc % 2)
                    fw.dma("sp", a_[:, :], src_dram[kc][:, 0:WM], writes=[ak])
                    fw.dma("sp", b_[:, :], src_dram[kc][:, HALF:HALF + WM], writes=[bk])
                    fw.op("dve", lambda e: e.tensor_scalar(out=a_[:, :], in0=a_[:, :], scalar1=sel[:, 0:1], scalar2=None, op0=ALU.mult),
                          reads=[ak, "sel"], writes=[ak])
                    fw.op("dve", lambda e: e.scalar_tensor_tensor(out=dst[:, kc, :], in0=b_[:, :], scalar=sel[:, 1:2], in1=a_[:, :],
                                                                  op0=ALU.mult, op1=ALU.add),
                          reads=[ak, bk, "sel"], writes=[(dkey, kc)])
                fw.barrier()

        build_k3(ctx=ctx, pfx="k3_", load_h=lambda fw_, hT: blend_load(hT, h1s, "h"),
                 load_act=lambda fw_, act: blend_load(act, oys, "act"))
        print("FUSED: ins", fw.nins, "waits", fw.nwaits)
    return nc


def prep_fused(inp):
    f = np.float32
    B = inp["x"].shape[0]
    base = {}
    for k, v in prep_k1_common(inp).items():
        base["k1_" + k] = v
    base["k1_fmaskA"] = np.zeros((P, 2), f)
    base["k1_fmaskB"] = np.ones((P, 2), f)
    common2, per_hs = prep_k2_common(inp)
    for hs, nm in enumerate(["a", "b"]):
        for k, v in list(common2.items()) + list(per_hs[hs].items()):
            base["k2%s_%s" % (nm, k)] = v
    for k, v in prep_k3_common(inp).items():
        base["k3_" + k] = v
    maps = []
    for b in range(B):
        wa, wb = k1_windows(inp, b)
        for half in range(2):
            m = dict(base)
            m["k1_xinA"] = wa
            m["k1_xinB"] = wb
            m["k3_fmask"] = np.full((P, 2), 0.0 if half == 0 else 1.0, f)
            sel = np.zeros((P, 2), f)
            sel[:, half] = 1.0
            m["sel"] = sel
            maps.append(m)
    return maps


def kernel(**inputs):
    inp = {k: np.ascontiguousarray(np.asarray(v, dtype=np.float32)) for k, v in inputs.items()}
    B = inp["x"].shape[0]
    res = run_bass_kernel_spmd(build_fused(), prep_fused(inp), core_ids=list(range(2 * B)))
    out = np.stack([np.concatenate([np.asarray(res.results[2 * b + h]["out"]) for h in range(2)], axis=0)[N_META:] for b in range(B)])
    return np.ascontiguousarray(out.astype(np.float32))
```

```python
import numpy as np
from contextlib import ExitStack
import concourse.bass as bass
import concourse.mybir as mybir
from concourse.bass_utils import run_bass_kernel_spmd

F32 = mybir.dt.float32
BF16 = mybir.dt.bfloat16
AF = mybir.ActivationFunctionType
ALU = mybir.AluOpType

P = 128
D = 2048
KC = 16
N_META = 16
SEQ = 2048
L = N_META + SEQ
HALF = L // 2
HALO = 32
W0 = HALO + HALF
WM = 2 + HALF
DFF = 5632
NJ = DFF // P
NHF = 2
JH = NJ // NHF
EPS = 1e-6
LN_EPS = 1e-5


PROFILE = False


def split3(n):
    a = (n + 2) // 3
    out = []
    c = 0
    while c < n:
        m = min(a, n - c)
        out.append((c, m))
        c += m
    return out


class _Rec:
    def __init__(self):
        self.calls = []

    def __getattr__(self, name):
        def f(*a, **k):
            self.calls.append((name, a, k))
            return None
        return f


DEFER = False
LOOKAHEAD = 96
PER_ENGINE = 24
HOP_US = 0.7


def _free_elems(ap):
    try:
        sh = ap.shape
        n = 1
        for d in sh[1:]:
            n *= int(d)
        return n
    except Exception:
        return 64


def _op_cost(kind, e, call):
    name, a, k = call
    if kind == "dma":
        out = k.get("out")
        try:
            nb = out.nbytes()
        except Exception:
            nb = _free_elems(out) * 4 * 128
        return 0.15, 2.0 + nb / 2.0e5
    out = k.get("out", a[0] if a else None)
    n = _free_elems(out) if out is not None else 64
    if e == "pe":
        lhsT = k.get("lhsT")
        fp32 = (lhsT is not None and lhsT.dtype == F32) or name == "transpose"
        c = (0.30 + 4.0 * n / 2400.0) if fp32 else (0.07 + n / 2400.0)
        return c, c + 0.15
    if e == "act":
        c = 0.22 + n / 1400.0
        return c, c
    c = 0.12 + n / 960.0
    return c, c


class FW:
    NDMA = 24

    def __init__(self, nc, es):
        self.nc = nc
        self.es = es
        self.eng = {"pe": nc.tensor, "dve": nc.vector, "act": nc.scalar,
                    "pool": nc.gpsimd, "sp": nc.sync}
        self.sem = {}
        self.cnt = {}
        for k in self.eng:
            self.sem[k] = es.enter_context(nc.semaphore("s_" + k))
            self.cnt[k] = 0
        for i in range(self.NDMA):
            k = "d%d" % i
            self.sem[k] = es.enter_context(nc.semaphore("s_" + k))
            self.cnt[k] = 0
        self.dma_rr = 0
        self.waited = {k: {} for k in self.eng}
        self.lastw = {}
        self.readers = {}
        self.nwaits = 0
        self.nins = 0
        self.bank_rr = 0
        self.banks = [es.enter_context(nc.psum_tensor("pb%d" % i, [P, 512], F32)) for i in range(8)]
        self.pending = []
        self.defer = False

    def sb(self, es, name, shape, dtype=F32):
        self.uid = getattr(self, "uid", 0) + 1
        return es.enter_context(self.nc.sbuf_tensor("%s_%d" % (name, self.uid), list(shape), dtype))

    def nb(self):
        i = self.bank_rr
        self.bank_rr = (i + 1) % 8
        return ("pb", i), self.banks[i]

    def _wait(self, e, tok):
        k, v = tok
        if k == e and e == "pe":
            return
        w = self.waited[e]
        if w.get(k, 0) >= v:
            return
        self.eng[e].wait_ge(self.sem[k], v)
        w[k] = v
        self.nwaits += 1

    def _deps(self, e, reads, writes):
        need = {}

        def add(tok):
            if tok is None:
                return
            k, v = tok
            if need.get(k, 0) < v:
                need[k] = v
        for r in reads:
            add(self.lastw.get(r))
        for w in writes:
            add(self.lastw.get(w))
            for k, v in self.readers.get(w, {}).items():
                add((k, v))
        for k, v in need.items():
            self._wait(e, (k, v))

    def _commit(self, tok, reads, writes):
        k, v = tok
        for r in reads:
            d = self.readers.setdefault(r, {})
            if d.get(k, 0) < v:
                d[k] = v
        for w in writes:
            self.lastw[w] = tok
            self.readers[w] = {}

    def op(self, e, fn, reads=(), writes=()):
        rec = _Rec()
        fn(rec)
        assert len(rec.calls) == 1, rec.calls
        item = ("op", e, rec.calls[0], tuple(reads), tuple(writes))
        if self.defer and not PROFILE:
            self.pending.append(item)
            return None
        return self._emit(item)

    def dma(self, q, out, in_, reads=(), writes=(), **kw):
        item = ("dma", q, ("dma_start", (), dict(out=out, in_=in_, **kw)), tuple(reads), tuple(writes))
        if self.defer and not PROFILE:
            self.pending.append(item)
            return None
        return self._emit(item)

    def _emit(self, item):
        kind, e, (name, a, k), reads, writes = item
        if kind == "dma":
            return self._dma_now(e, reads, writes, k)
        return self._op_now(e, lambda eng: getattr(eng, name)(*a, **k), reads, writes)

    def flush(self):
        ops = self.pending
        self.pending = []
        n = len(ops)
        if n == 0:
            return
        deps = [[] for _ in range(n)]
        succ = [[] for _ in range(n)]
        lastw = {}
        rdrs = {}
        for i, (kind, e, call, reads, writes) in enumerate(ops):
            d = set()
            for r in reads:
                if r in lastw:
                    d.add(lastw[r])
            for w in writes:
                if w in lastw:
                    d.add(lastw[w])
                for j in rdrs.get(w, ()):
                    d.add(j)
            d.discard(i)
            deps[i] = sorted(d)
            for j in deps[i]:
                succ[j].append(i)
            for r in reads:
                rdrs.setdefault(r, []).append(i)
            for w in writes:
                lastw[w] = i
                rdrs[w] = []
        cost = [_op_cost(o[0], o[1], o[2]) for o in ops]
        blevel = [0.0] * n
        for i in range(n - 1, -1, -1):
            m = 0.0
            for j in succ[i]:
                if blevel[j] > m:
                    m = blevel[j]
            blevel[i] = cost[i][1] + HOP_US + m
        indeg = [len(deps[i]) for i in range(n)]
        ready = [i for i in range(n) if indeg[i] == 0]
        efree = {}
        fin = [0.0] * n
        order = []
        while ready:
            best = None
            bkey = None
            seen = {}
            for i in ready:
                e = ops[i][1]
                c_ = seen.get(e, 0)
                if c_ >= PER_ENGINE:
                    if len(seen) >= 5 and min(seen.values()) >= PER_ENGINE:
                        break
                    continue
                seen[e] = c_ + 1
                t = efree.get(e, 0.0)
                for j in deps[i]:
                    tj = fin[j] + (HOP_US if ops[j][1] != e or ops[j][0] == "dma" else 0.05)
                    if tj > t:
                        t = tj
                key = (round(t, 1), -blevel[i], i)
                if bkey is None or key < bkey:
                    bkey = key
                    best = (i, t)
            i, t = best
            ready.remove(i)
            e = ops[i][1]
            efree[e] = t + cost[i][0]
            fin[i] = t + cost[i][1]
            order.append(i)
            for j in succ[i]:
                indeg[j] -= 1
                if indeg[j] == 0:
                    lo = 0
                    hi = len(ready)
                    while lo < hi:
                        mid = (lo + hi) // 2
                        if ready[mid] < j:
                            lo = mid + 1
                        else:
                            hi = mid
                    ready.insert(lo, j)
        assert len(order) == n
        for i in order:
            self._emit(ops[i])

    def _op_now(self, e, fn, reads=(), writes=()):
        self._deps(e, reads, writes)
        ins = fn(self.eng[e])
        self.cnt[e] += 1
        ins.then_inc(self.sem[e], 1)
        tok = (e, self.cnt[e])
        self._commit(tok, reads, writes)
        self.nins += 1
        return tok

    def _dma_now(self, q, reads, writes, kw):
        k = "d%d" % self.dma_rr
        self.dma_rr = (self.dma_rr + 1) % self.NDMA
        if self.cnt[k] > 0:
            self._wait(q, (k, self.cnt[k]))
        self._deps(q, reads, writes)
        ins = self.eng[q].dma_start(**kw)
        self.cnt[k] += 16
        ins.then_inc(self.sem[k], 16)
        tok = (k, self.cnt[k])
        self._commit(tok, reads, writes)
        self.nins += 1
        return tok

    def cc(self, kind, groups, src, dst, reads=(), writes=()):
        k = "d%d" % self.dma_rr
        self.dma_rr = (self.dma_rr + 1) % self.NDMA
        if self.cnt[k] > 0:
            self._wait("pool", (k, self.cnt[k]))
        self._deps("pool", reads, writes)
        ins = self.nc.gpsimd.collective_compute(kind, mybir.AluOpType.bypass, replica_groups=groups, ins=[src], outs=[dst])
        self.cnt[k] += 16
        ins.then_inc(self.sem[k], 16)
        tok = (k, self.cnt[k])
        self._commit(tok, reads, writes)
        self.nins += 1
        return tok

    def mark(self, name):
        if not PROFILE:
            return
        sc = getattr(self, "_scope", None)
        if sc is not None:
            self.nc.leave_named_scope(sc[0], sc[1], False)
            self._scope = None
        if name:
            sid, _ = self.nc.enter_named_scope(name, False)
            self._scope = (name, sid)

    def barrier(self):
        self.flush()
        for e in self.eng:
            for k, v in self.cnt.items():
                if v > 0:
                    self._wait(e, (k, v))

    def finish(self, e="sp"):
        self.flush()
        for k, v in self.cnt.items():
            if v > 0:
                self._wait(e, (k, v))


class WStream:
    def __init__(self, fw, es, name, shape, nbuf):
        self.fw = fw
        self.name = name
        self.bufs = [fw.sb(es, "%s%d" % (name, i), shape, BF16) for i in range(nbuf)]
        self.i = 0

    def load(self, src):
        i = self.i
        self.i = (i + 1) % len(self.bufs)
        key = (self.name, i)
        self.fw.dma("pool", self.bufs[i][:], src, writes=[key])
        return key, self.bufs[i]


def emit_sumsq_rstd(fw, src_fn, nblk, n, ones, sqs, rs_out, scale, eps, rkeys):
    bk, bank = fw.nb()
    for b in range(nblk):
        sk, sq = sqs[b % len(sqs)]
        fw.op("act", lambda e: e.activation(out=sq[:, :n], in_=src_fn(b), func=AF.Square),
              reads=[rkeys(b)], writes=[sk])
        fw.op("pe", lambda e: e.matmul(bank[:, :n], lhsT=ones[:], rhs=sq[:, :n], start=(b == 0), stop=(b == nblk - 1)),
              reads=[sk, "const"], writes=[bk])
    rk, rs = rs_out
    fw.op("act", lambda e: e.activation(out=rs[:, :n], in_=bank[:, :n], func=AF.Sqrt, scale=scale, bias=eps),
          reads=[bk], writes=[rk])
    fw.op("dve", lambda e: e.reciprocal(out=rs[:, :n], in_=rs[:, :n]), reads=[rk], writes=[rk])


def emit_rmsnorm(fw, hT, hoff, hn, ncols, g, ones, sqs, rss, epsap):
    for ti, (c0, n) in enumerate(split3(ncols)):
        rs_out = rss[ti % len(rss)]
        emit_sumsq_rstd(fw, lambda b: hT[:, b, hoff + c0:hoff + c0 + n], KC, n, ones, sqs, rs_out,
                        1.0 / D, epsap, lambda b: ("h", b))
        rk, rs = rs_out
        for kc in range(KC):
            fw.op("dve", lambda e: e.scalar_tensor_tensor(out=hn[:, kc, c0:c0 + n], in0=hT[:, kc, hoff + c0:hoff + c0 + n],
                                                          scalar=g[:, kc:kc + 1], in1=rs[:, :n], op0=ALU.mult, op1=ALU.mult),
                  reads=[("h", kc), rk, "const"], writes=[("hn", kc)])


def emit_ffn(fw, nc, hT, hoff, hn, wup_r, wdn_r, dwv, dwg, fmask):
    with ExitStack() as es:
        actF = fw.sb(es, "actF", [P, JH, HALF], BF16)
        wu = WStream(fw, es, "wu", [P, 2, KC, P], 2)
        wd = WStream(fw, es, "wd", [P, JH, P], 2)
        cvs = [(("cv", i), fw.sb(es, "cv%d" % i, [P, 344])) for i in range(2)]
        cgs = [(("cg", i), fw.sb(es, "cg%d" % i, [P, 344])) for i in range(2)]
        sgs = [(("sgf", i), fw.sb(es, "sgf%d" % i, [P, 344])) for i in range(2)]
        fw.op("dve", lambda e: e.tensor_tensor(out=hn[:, :, 0:2], in0=hn[:, :, 0:2],
                                               in1=fmask[:, :].unsqueeze(1).to_broadcast([P, KC, 2]), op=ALU.mult),
              reads=[("hn", kc) for kc in range(KC)] + ["const"], writes=[("hn", kc) for kc in range(KC)])
        tiles = split3(HALF)
        hnkeys = [("hn", kc) for kc in range(KC)]
        it = 0
        for hf in range(NHF):
            for jj in range(JH):
                j = hf * JH + jj
                wk, wt = wu.load(wup_r[j].rearrange("p (a k c) -> p a k c", a=2, k=KC))
                for (o0, n) in tiles:
                    bv, bankv = fw.nb()
                    for kc in range(KC):
                        fw.op("pe", lambda e: e.matmul(bankv[:, :n + 2], lhsT=wt[:, 0, kc, :], rhs=hn[:, kc, o0:o0 + n + 2],
                                                       start=(kc == 0), stop=(kc == KC - 1)),
                              reads=[wk, ("hn", kc)], writes=[bv])
                    bg, bankg = fw.nb()
                    for kc in range(KC):
                        fw.op("pe", lambda e: e.matmul(bankg[:, :n + 2], lhsT=wt[:, 1, kc, :], rhs=hn[:, kc, o0:o0 + n + 2],
                                                       start=(kc == 0), stop=(kc == KC - 1)),
                              reads=[wk, ("hn", kc)], writes=[bg])
                    cvk, cv = cvs[it % 2]
                    cgk, cg = cgs[it % 2]
                    sgk, sg = sgs[it % 2]
                    it += 1
                    fw.op("act", lambda e: e.mul(out=cg[:, :n], in_=bankg[:, 2:n + 2], mul=dwg[:, j, 2:3]),
                          reads=[bg, "const"], writes=[cgk])
                    for tp in (1, 0):
                        fw.op("dve", lambda e: e.scalar_tensor_tensor(out=cg[:, :n], in0=bankg[:, tp:tp + n], scalar=dwg[:, j, tp:tp + 1],
                                                                      in1=cg[:, :n], op0=ALU.mult, op1=ALU.add),
                              reads=[bg, cgk, "const"], writes=[cgk])
                    fw.op("act", lambda e: e.activation(out=sg[:, :n], in_=cg[:, :n], func=AF.Silu), reads=[cgk], writes=[sgk])
                    fw.op("act", lambda e: e.mul(out=cv[:, :n], in_=bankv[:, 2:n + 2], mul=dwv[:, j, 2:3]),
                          reads=[bv, "const"], writes=[cvk])
                    for tp in (1, 0):
                        fw.op("dve", lambda e: e.scalar_tensor_tensor(out=cv[:, :n], in0=bankv[:, tp:tp + n], scalar=dwv[:, j, tp:tp + 1],
                                                                      in1=cv[:, :n], op0=ALU.mult, op1=ALU.add),
                              reads=[bv, cvk, "const"], writes=[cvk])
                    fw.op("dve", lambda e: e.tensor_tensor(out=actF[:, jj, o0:o0 + n], in0=cv[:, :n], in1=sg[:, :n], op=ALU.mult),
                          reads=[cvk, sgk], writes=[("actF", jj)])
            for m in range(KC):
                wk, wt = wd.load(wdn_r[hf, m].rearrange("p (j c) -> p j c", j=JH))
                for (o0, n) in tiles:
                    bk, bank = fw.nb()
                    for jj in range(JH):
                        fw.op("pe", lambda e: e.matmul(bank[:, :n], lhsT=wt[:, jj, :], rhs=actF[:, jj, o0:o0 + n],
                                                       start=(jj == 0), stop=(jj == JH - 1)),
                              reads=[wk, ("actF", jj)], writes=[bk])
                    c = hoff + 2 + o0
                    fw.op("dve", lambda e: e.tensor_tensor(out=hT[:, m, c:c + n], in0=hT[:, m, c:c + n], in1=bank[:, :n], op=ALU.add),
                          reads=[bk, ("h", m)], writes=[("h", m)])
        fw.barrier()


def emit_outproj(fw, wo, hT, hoff, act, wout_r):
    for m in range(KC):
        wk, wt = wo.load(wout_r[m].rearrange("p (k c) -> p k c", k=KC))
        for (c0, n) in split3(WM):
            bk, bank = fw.nb()
            for kb in range(KC):
                fw.op("pe", lambda e: e.matmul(bank[:, :n], lhsT=wt[:, kb, :], rhs=act[:, kb, c0:c0 + n],
                                               start=(kb == 0), stop=(kb == KC - 1)),
                      reads=[wk, ("act", kb)], writes=[bk])
            c = hoff + c0
            fw.op("dve", lambda e: e.tensor_tensor(out=hT[:, m, c:c + n], in0=hT[:, m, c:c + n], in1=bank[:, :n], op=ALU.add),
                  reads=[bk, ("h", m)], writes=[("h", m)])


_DIN_CACHE = {}


def make_din(nc, pfx, overrides=None):
    overrides = overrides or {}

    def din(name, shape):
        full = overrides.get(name, pfx + name)
        key = (id(nc), full)
        if key not in _DIN_CACHE:
            _DIN_CACHE[key] = nc.dram_tensor(full, list(shape), F32, kind="ExternalInput").ap()
        return _DIN_CACHE[key]
    return din


def load_consts(fw, es, nc, specs):
    out = {}
    for name, ap, shape in specs:
        t = fw.sb(es, "c_" + name, shape)
        fw.dma("sp", t[:], ap, writes=["const"])
        out[name] = t
    return out


def build_k1(ctx=None, pfx="", overrides=None, store=None):
    standalone = ctx is None
    nc = bass.Bass("TRN2", target_bir_lowering=False) if standalone else ctx[0]
    din = make_din(nc, pfx, overrides)
    xin = din("xin", [W0, D])
    ident_d = din("ident", [P, P])
    ones_d = din("ones", [P, P])
    fmask_d = din("fmask", [P, 2])
    g1_d = din("g1", [P, KC])
    g2_d = din("g2", [P, KC])
    win_r = din("win_r", [40, P, KC * P])
    cdw_d = din("cdw", [P, 8, 31])
    cvec_d = din("cvec", [P, 3, 8])
    sdw_d = din("sdw", [P, 8, 3])
    wout_r = din("wout_r", [KC, P, KC * P])
    wup_r = din("wup_r", [NJ, P, 2 * KC * P])
    dwv_d = din("dwv", [P, NJ, 3])
    dwg_d = din("dwg", [P, NJ, 3])
    wdn_r = din("wdn_r", [NHF, KC, P, JH * P])
    if standalone:
        h1T = nc.dram_tensor("h1T", [KC, P, HALF], F32, kind="ExternalOutput").ap()

    with ExitStack() as es:
        fw = FW(nc, es) if standalone else ctx[1]
        C = load_consts(fw, es, nc, [("ident", ident_d, [P, P]), ("ones", ones_d, [P, P]), ("fmask", fmask_d, [P, 2]),
                                     ("g1", g1_d, [P, KC]), ("g2", g2_d, [P, KC]), ("cdw", cdw_d, [P, 8, 31]),
                                     ("cvec", cvec_d, [P, 3, 8]), ("sdw", sdw_d, [P, 8, 3]),
                                     ("dwv", dwv_d, [P, NJ, 3]), ("dwg", dwg_d, [P, NJ, 3])])
        ident, ones = C["ident"], C["ones"]
        hT = fw.sb(es, "hT", [P, KC, W0])
        hn = fw.sb(es, "hn", [P, KC, W0], BF16)
        sqs = [(("sq", i), fw.sb(es, "sq%d" % i, [P, 356])) for i in range(3)]
        rss = [(("rs", i), fw.sb(es, "rs%d" % i, [P, 356])) for i in range(2)]

        with ExitStack() as e0:
            xts = [fw.sb(e0, "xt%d" % i, [P, D]) for i in range(2)]
            nt = (W0 + P - 1) // P
            for t in range(nt):
                r0 = t * P
                nr = min(P, W0 - r0)
                xt = xts[t % 2]
                xk = ("xt", t % 2)
                fw.dma("sp", xt[:nr, :], xin[r0:r0 + nr, :], writes=[xk])
                for gq in range(4):
                    bk, bank = fw.nb()
                    for q in range(4):
                        kc = gq * 4 + q
                        fw.op("pe", lambda e: e.transpose(bank[:, q * P:q * P + nr], xt[:nr, kc * P:(kc + 1) * P], ident[:nr, :nr]),
                              reads=[xk, "const"], writes=[bk])
                    src = bank[:, :].rearrange("p (a b) -> p a b", a=4)[:, :, :nr]
                    eng = "dve" if gq % 2 == 0 else "act"
                    if eng == "dve":
                        fw.op("dve", lambda e: e.tensor_copy(out=hT[:, gq * 4:gq * 4 + 4, r0:r0 + nr], in_=src),
                              reads=[bk], writes=[("h", gq * 4 + q) for q in range(4)])
                    else:
                        fw.op("act", lambda e: e.copy(out=hT[:, gq * 4:gq * 4 + 4, r0:r0 + nr], in_=src),
                              reads=[bk], writes=[("h", gq * 4 + q) for q in range(4)])
            fw.barrier()

        emit_rmsnorm(fw, hT, 0, hn, W0, C["g1"], ones, sqs, rss, EPS)

        with ExitStack() as e2:
            cT = fw.sb(e2, "cT", [P, 8, WM])
            act = fw.sb(e2, "actAS", [P, KC, WM], BF16)
            wi = WStream(fw, e2, "wi", [P, KC, P], 3)
            sg = fw.sb(e2, "sg", [P, W0])
            a_s = fw.sb(e2, "a_s", [P, W0])
            b_s = fw.sb(e2, "b_s", [P, W0])
            q_s = fw.sb(e2, "q_s", [P, WM])
            mean = q_s
            rstd = fw.sb(e2, "rstd", [P, WM])
            t_s = [("a_s", a_s), ("sg", sg)]
            cdw, cvec, sdw = C["cdw"], C["cvec"], C["sdw"]
            tiles0 = split3(W0)
            tilesm = split3(WM)

            def inproj(blk, tiles, coff, consume):
                wk, wt = wi.load(win_r[blk].rearrange("p (k c) -> p k c", k=KC))
                for (c0, n) in tiles:
                    bk, bank = fw.nb()
                    for kc in range(KC):
                        fw.op("pe", lambda e: e.matmul(bank[:, :n], lhsT=wt[:, kc, :], rhs=hn[:, kc, coff + c0:coff + c0 + n],
                                                       start=(kc == 0), stop=(kc == KC - 1)),
                              reads=[wk, ("hn", kc)], writes=[bk])
                    consume(bk, bank, c0, n)

            for blk in range(8):
                inproj(8 + blk, tiles0, 0, lambda bk, bank, c0, n: fw.op(
                    "act", lambda e: e.activation(out=sg[:, c0:c0 + n], in_=bank[:, :n], func=AF.Sigmoid),
                    reads=[bk], writes=["sg"]))
                inproj(blk, tiles0, 0, lambda bk, bank, c0, n: fw.op(
                    "dve", lambda e: e.tensor_tensor(out=a_s[:, c0:c0 + n], in0=bank[:, :n], in1=sg[:, c0:c0 + n], op=ALU.mult),
                    reads=[bk, "sg"], writes=["a_s"]))
                fw.op("dve", lambda e: e.tensor_scalar(out=cT[:, blk, :], in0=a_s[:, 0:WM], scalar1=cdw[:, blk, 0:1],
                                                       scalar2=cvec[:, 0, blk:blk + 1], op0=ALU.mult, op1=ALU.add),
                      reads=["a_s", "const"], writes=[("cT", blk)])
                for j in range(1, 31):
                    fw.op("dve", lambda e: e.scalar_tensor_tensor(out=cT[:, blk, :], in0=a_s[:, j:j + WM], scalar=cdw[:, blk, j:j + 1],
                                                                  in1=cT[:, blk, :], op0=ALU.mult, op1=ALU.add),
                          reads=["a_s", ("cT", blk), "const"], writes=[("cT", blk)])
                inproj(24 + blk, tiles0, 0, lambda bk, bank, c0, n: fw.op(
                    "act", lambda e: e.copy(out=sg[:, c0:c0 + n], in_=bank[:, :n]), reads=[bk], writes=["sg"]))
                inproj(32 + blk, tiles0, 0, lambda bk, bank, c0, n: fw.op(
                    "dve", lambda e: e.tensor_tensor(out=b_s[:, c0:c0 + n], in0=bank[:, :n], in1=sg[:, c0:c0 + n], op=ALU.mult),
                    reads=[bk, "sg"], writes=["b_s"]))
                fw.op("dve", lambda e: e.tensor_scalar(out=q_s[:, :], in0=b_s[:, 28:28 + WM], scalar1=sdw[:, blk, 0:1], scalar2=None,
                                                       op0=ALU.mult),
                      reads=["b_s", "const"], writes=["q_s"])
                for tp in (1, 2):
                    fw.op("dve", lambda e: e.scalar_tensor_tensor(out=q_s[:, :], in0=b_s[:, 28 + tp:28 + tp + WM], scalar=sdw[:, blk, tp:tp + 1],
                                                                  in1=q_s[:, :], op0=ALU.mult, op1=ALU.add),
                          reads=["b_s", "q_s", "const"], writes=["q_s"])
                inproj(16 + blk, tilesm, 30, lambda bk, bank, c0, n: fw.op(
                    "dve", lambda e: e.tensor_tensor(out=act[:, 8 + blk, c0:c0 + n], in0=bank[:, :n], in1=q_s[:, c0:c0 + n], op=ALU.mult),
                    reads=[bk, "q_s"], writes=[("act", 8 + blk)]))

            for (c0, n) in tilesm:
                b1k, bank1 = fw.nb()
                b2k, bank2 = fw.nb()
                for blk in range(8):
                    sk, sq = sqs[blk % 3]
                    fw.op("act", lambda e: e.activation(out=sq[:, :n], in_=cT[:, blk, c0:c0 + n], func=AF.Square),
                          reads=[("cT", blk)], writes=[sk])
                    fw.op("pe", lambda e: e.matmul(bank2[:, :n], lhsT=ones[:], rhs=sq[:, :n], start=(blk == 0), stop=(blk == 7)),
                          reads=[sk, "const"], writes=[b2k])
                    fw.op("pe", lambda e: e.matmul(bank1[:, :n], lhsT=ones[:], rhs=cT[:, blk, c0:c0 + n], start=(blk == 0), stop=(blk == 7)),
                          reads=[("cT", blk), "const"], writes=[b1k])
                fw.op("act", lambda e: e.mul(out=mean[:, c0:c0 + n], in_=bank1[:, :n], mul=1.0 / 1024), reads=[b1k], writes=["q_s"])
                sk, sq = sqs[0]
                fw.op("dve", lambda e: e.tensor_tensor(out=sq[:, :n], in0=mean[:, c0:c0 + n], in1=mean[:, c0:c0 + n], op=ALU.mult),
                      reads=["q_s"], writes=[sk])
                fw.op("dve", lambda e: e.scalar_tensor_tensor(out=sq[:, :n], in0=bank2[:, :n], scalar=1.0 / 1024, in1=sq[:, :n],
                                                              op0=ALU.mult, op1=ALU.subtract),
                      reads=[b2k, sk], writes=[sk])
                fw.op("act", lambda e: e.activation(out=sq[:, :n], in_=sq[:, :n], func=AF.Sqrt, bias=LN_EPS), reads=[sk], writes=[sk])
                fw.op("dve", lambda e: e.reciprocal(out=rstd[:, c0:c0 + n], in_=sq[:, :n]), reads=[sk], writes=["rstd"])
            for blk in range(8):
                tk, tt = t_s[blk % 2]
                fw.op("dve", lambda e: e.tensor_tensor(out=tt[:, 0:WM], in0=cT[:, blk, :], in1=mean[:, :], op=ALU.subtract),
                      reads=[("cT", blk), "q_s"], writes=[tk])
                fw.op("dve", lambda e: e.tensor_tensor(out=tt[:, 0:WM], in0=tt[:, 0:WM], in1=rstd[:, :], op=ALU.mult),
                      reads=[tk, "rstd"], writes=[tk])
                fw.op("act", lambda e: e.activation(out=act[:, blk, :], in_=tt[:, 0:WM], func=AF.Silu,
                                                    scale=cvec[:, 1, blk:blk + 1], bias=cvec[:, 2, blk:blk + 1]),
                      reads=[tk, "const"], writes=[("act", blk)])

            emit_outproj(fw, wi, hT, 30, act, wout_r)
            fw.barrier()

        emit_rmsnorm(fw, hT, 30, hn, WM, C["g2"], ones, sqs, rss, EPS)
        emit_ffn(fw, nc, hT, 30, hn, wup_r, wdn_r, C["dwv"], C["dwg"], C["fmask"])

        for kc in range(KC):
            if standalone:
                fw.dma("sp", h1T[kc], hT[:, kc, HALO:HALO + HALF], reads=[("h", kc)])
            else:
                store(fw, kc, hT[:, kc, HALO:HALO + HALF], [("h", kc)])
        if standalone:
            fw.finish("sp")
        else:
            fw.barrier()
        print("K1: ins", fw.nins, "waits", fw.nwaits)
    return nc


def blk_layout(w, kc=KC):
    K, N = w.shape
    return np.ascontiguousarray(w.reshape(K // P, P, N // P, P).transpose(2, 1, 0, 3).reshape(N // P, P, (K // P) * P))


def vec_layout(v):
    return np.ascontiguousarray(v.reshape(-1, P).T)


def prep_common():
    return dict(ident=np.eye(P, dtype=np.float32), ones=np.ones((P, P), np.float32))


def prep_ffn(w_up, dw, w_down):
    up_v = blk_layout(w_up[:, :DFF])
    up_g = blk_layout(w_up[:, DFF:])
    wup_r = np.ascontiguousarray(np.concatenate([up_v, up_g], axis=2))
    dwv = np.ascontiguousarray(dw[:, :DFF].reshape(3, NJ, P).transpose(2, 1, 0))
    dwg = np.ascontiguousarray(dw[:, DFF:].reshape(3, NJ, P).transpose(2, 1, 0))
    wd = w_down.reshape(NHF, JH, P, KC, P).transpose(0, 3, 2, 1, 4).reshape(NHF, KC, P, JH * P)
    return dict(wup_r=wup_r, dwv=dwv, dwg=dwg, wdn_r=np.ascontiguousarray(wd))


def prep_k1_common(inp):
    common = prep_common()
    common.update(prep_ffn(inp["ffn_w_up"][0], inp["ffn_dw"][0], inp["ffn_w_down"][0]))
    common["g1"] = vec_layout(inp["norm_mix"][0])
    common["g2"] = vec_layout(inp["norm_ffn"][0])
    common["win_r"] = blk_layout(inp["ab_w_in"][0])
    common["cdw"] = np.ascontiguousarray(inp["ab_conf_dw"][0].reshape(31, 8, P).transpose(2, 1, 0))
    common["cvec"] = np.ascontiguousarray(np.stack([inp["ab_conf_dw_b"][0].reshape(8, P).T, inp["ab_conf_ln_g"][0].reshape(8, P).T,
                                                    inp["ab_conf_ln_b"][0].reshape(8, P).T], axis=1))
    common["sdw"] = np.ascontiguousarray(inp["ab_sc_dw"][0].reshape(3, 8, P).transpose(2, 1, 0))
    common["wout_r"] = blk_layout(inp["ab_w_out"][0])
    return common


def k1_windows(inp, b):
    f = np.float32
    full = np.concatenate([inp["meta"], inp["x"][b]], axis=0)
    full = np.concatenate([np.zeros((HALO, D), f), full], axis=0)
    return [np.ascontiguousarray(full[half * HALF:half * HALF + W0]) for half in range(2)]


def prep_k1(inp):
    f = np.float32
    common = prep_k1_common(inp)
    maps = []
    B = inp["x"].shape[0]
    for b in range(B):
        full = np.concatenate([inp["meta"], inp["x"][b]], axis=0)
        full = np.concatenate([np.zeros((HALO, D), f), full], axis=0)
        for half in range(2):
            s = half * HALF
            m = dict(common)
            m["xin"] = np.ascontiguousarray(full[s:s + W0])
            m["fmask"] = np.full((P, 2), 0.0 if half == 0 else 1.0, f)
            maps.append(m)
    return maps


def build_k3(ctx=None, pfx="", overrides=None, load_h=None, load_act=None):
    standalone = ctx is None
    nc = bass.Bass("TRN2", target_bir_lowering=False) if standalone else ctx[0]
    din = make_din(nc, pfx, overrides)
    if standalone:
        h1w = din("h1w", [KC, P, WM])
        oyw = din("oyw", [KC, P, WM])
    ident_d = din("ident", [P, P])
    ones_d = din("ones", [P, P])
    fmask_d = din("fmask", [P, 2])
    g2_d = din("g2", [P, KC])
    gf_d = din("gf", [P, KC])
    wout_r = din("wout_r", [KC, P, KC * P])
    wup_r = din("wup_r", [NJ, P, 2 * KC * P])
    dwv_d = din("dwv", [P, NJ, 3])
    dwg_d = din("dwg", [P, NJ, 3])
    wdn_r = din("wdn_r", [NHF, KC, P, JH * P])
    out = nc.dram_tensor("out", [HALF, D], F32, kind="ExternalOutput").ap()

    with ExitStack() as es:
        fw = FW(nc, es) if standalone else ctx[1]
        C = load_consts(fw, es, nc, [("ident", ident_d, [P, P]), ("ones", ones_d, [P, P]), ("fmask", fmask_d, [P, 2]),
                                     ("g2", g2_d, [P, KC]), ("gf", gf_d, [P, KC]),
                                     ("dwv", dwv_d, [P, NJ, 3]), ("dwg", dwg_d, [P, NJ, 3])])
        ident, ones = C["ident"], C["ones"]
        hT = fw.sb(es, "hT", [P, KC, WM])
        hn = fw.sb(es, "hn", [P, KC, WM], BF16)
        sqs = [(("sq", i), fw.sb(es, "sq%d" % i, [P, 356])) for i in range(3)]
        rss = [(("rs", i), fw.sb(es, "rs%d" % i, [P, 356])) for i in range(2)]
        if standalone:
            for kc in range(KC):
                fw.dma("sp", hT[:, kc, :], h1w[kc], writes=[("h", kc)])
        else:
            load_h(fw, hT)
        with ExitStack() as e2:
            act = fw.sb(e2, "act", [P, KC, WM], BF16)
            wo = WStream(fw, e2, "wo", [P, KC, P], 2)
            if standalone:
                for kb in range(KC):
                    fw.dma("pool", act[:, kb, :], oyw[kb], writes=[("act", kb)])
            else:
                load_act(fw, act)
            emit_outproj(fw, wo, hT, 0, act, wout_r)
            fw.barrier()
        emit_rmsnorm(fw, hT, 0, hn, WM, C["g2"], ones, sqs, rss, EPS)
        emit_ffn(fw, nc, hT, 0, hn, wup_r, wdn_r, C["dwv"], C["dwg"], C["fmask"])
        with ExitStack() as e5:
            hf = fw.sb(e5, "hf", [P, KC, 344])
            ots = [(("ot", i), fw.sb(e5, "ot%d" % i, [P, D])) for i in range(2)]
            gf = C["gf"]
            oi = 0
            for ti, (o0, n) in enumerate(split3(HALF)):
                rs_out = rss[ti % 2]
                emit_sumsq_rstd(fw, lambda b: hT[:, b, 2 + o0:2 + o0 + n], KC, n, ones, sqs, rs_out, 1.0 / D, EPS,
                                lambda b: ("h", b))
                rk, rs = rs_out
                for kc in range(KC):
                    fw.op("dve", lambda e: e.scalar_tensor_tensor(out=hf[:, kc, :n], in0=hT[:, kc, 2 + o0:2 + o0 + n], scalar=gf[:, kc:kc + 1],
                                                                  in1=rs[:, :n], op0=ALU.mult, op1=ALU.mult),
                          reads=[("h", kc), rk, "const"], writes=[("hf", kc)])
                t0 = 0
                while t0 < n:
                    nr = min(P, n - t0)
                    otk, ot = ots[oi % 2]
                    oi += 1
                    for gq in range(4):
                        bk, bank = fw.nb()
                        for q in range(4):
                            kc = gq * 4 + q
                            fw.op("pe", lambda e: e.transpose(bank[:nr, q * P:(q + 1) * P], hf[:, kc, t0:t0 + nr], ident[:, :]),
                                  reads=[("hf", kc), "const"], writes=[bk])
                        if gq % 2 == 0:
                            fw.op("dve", lambda e: e.tensor_copy(out=ot[:nr, gq * 512:(gq + 1) * 512], in_=bank[:nr, :]), reads=[bk], writes=[otk])
                        else:
                            fw.op("act", lambda e: e.copy(out=ot[:nr, gq * 512:(gq + 1) * 512], in_=bank[:nr, :]), reads=[bk], writes=[otk])
                    fw.dma("sp", out[o0 + t0:o0 + t0 + nr, :], ot[:nr, :], reads=[otk])
                    t0 += nr
        fw.finish("sp")
        print("K3: ins", fw.nins, "waits", fw.nwaits)
    return nc


def prep_k3_common(inp):
    common = prep_common()
    common.update(prep_ffn(inp["ffn_w_up"][1], inp["ffn_dw"][1], inp["ffn_w_down"][1]))
    common["g2"] = vec_layout(inp["norm_ffn"][1])
    common["gf"] = vec_layout(inp["norm_final"])
    common["wout_r"] = blk_layout(inp["cd_w_out"][0])
    return common


def prep_k3(inp, h1T_list, oyT_list):
    f = np.float32
    common = prep_k3_common(inp)
    maps = []
    for b in range(len(h1T_list)):
        hp = np.concatenate([np.zeros((D, 2), f), h1T_list[b]], axis=1)
        op = np.concatenate([np.zeros((D, 2), f), oyT_list[b]], axis=1)
        for half in range(2):
            s = half * HALF
            m = dict(common)
            m["h1w"] = np.ascontiguousarray(hp[:, s:s + WM]).reshape(KC, P, WM)
            m["oyw"] = np.ascontiguousarray(op[:, s:s + WM]).reshape(KC, P, WM)
            m["fmask"] = np.full((P, 2), 0.0 if half == 0 else 1.0, f)
            maps.append(m)
    return maps


CH = 64
TP = 2112
NSEG = 3
TS = TP // NSEG
NCH = TS // CH
C05 = float(np.exp(-0.5))
GN_EPS = 64e-5


class StopBuild(Exception):
    pass


def build_k2(stop_at=None, ctx=None, pfx="", overrides=None, load_h=None, store=None):
    standalone = ctx is None
    nc = bass.Bass("TRN2", target_bir_lowering=False) if standalone else ctx[0]

    pcur = {"seg": 0, "pr": 0, "fw": None}

    def chk(name):
        if pcur["fw"] is not None:
            pcur["fw"].mark("post%s_s%d_p%d" % (name, pcur["seg"], pcur["pr"]))
        if stop_at == name:
            raise StopBuild()

    din = make_din(nc, pfx, overrides)
    if standalone:
        h1p = din("h1p", [KC, P, TP])
    cs_specs = [("ident", [P, P]), ("ones", [P, P]), ("bones", [P, P]), ("M4", [CH, 4 * CH]), ("M5", [CH, 6 * CH]),
                ("mincl", [CH, CH]), ("rmask", [P, TS]), ("g1", [P, KC]), ("rwp", [P, 10, 4]), ("mul", [P, 2]),
                ("w2z", [P, 512]), ("a2z", [P, 512]), ("hm", [P, 4]), ("g2s", [P, 512]), ("gw2", [16, 256]), ("glab", [P, 2]), ("glan", [P, 4])]
    cd = {n: din(n, s) for n, s in cs_specs}
    wr_r = din("wr_r", [4, 3, P, KC * P])
    wl_r = din("wl_r", [2, P, KC * P])
    wg_r = din("wg_r", [2, 6, P, KC * P])
    wglr_r = din("wglr_r", [P, KC * 16])
    if standalone:
        oy = nc.dram_tensor("oy", [8, P, L], F32, kind="ExternalOutput").ap()
    tiles2 = [(0, TS // 2), (TS // 2, TS // 2)]

    es = ExitStack()
    stopped = [False]
    if True:
        fw = FW(nc, es) if standalone else ctx[1]
        fw.flush()
        fw.defer = True
        C = load_consts(fw, es, nc, [(n, cd[n], s) for n, s in cs_specs])
        ident, ones, bones, M4, M5, mincl, rmask = (C[k] for k in ("ident", "ones", "bones", "M4", "M5", "mincl", "rmask"))
        rwp, mul, w2z, a2z, hm, g2s, gw2, glab, glan = (C[k] for k in ("rwp", "mul", "w2z", "a2z", "hm", "g2s", "gw2", "glab", "glan"))
        MU_R, MU_K, MU_V, W0I, A0I, KKI, KAI, RKI, LGI, LBI = range(10)
        omr = fw.sb(es, "omr", [P, 10, 4])
        oml = fw.sb(es, "oml", [P, 2])
        fw.op("dve", lambda e: e.tensor_scalar(out=omr[:], in0=rwp[:], scalar1=-1.0, scalar2=1.0, op0=ALU.mult, op1=ALU.add),
              reads=["const"], writes=["const2"])
        fw.op("dve", lambda e: e.tensor_scalar(out=oml[:], in0=mul[:], scalar1=-1.0, scalar2=1.0, op0=ALU.mult, op1=ALU.add),
              reads=["const"], writes=["const2"])
        identb = fw.sb(es, "identb", [P, P], BF16)
        fw.op("dve", lambda e: e.tensor_copy(out=identb[:], in_=ident[:]), reads=["const"], writes=["const2"])
        CK = ["const", "const2"]
        hn = fw.sb(es, "hn", [P, KC, TS], BF16)
        wi = WStream(fw, es, "wi", [P, KC, P], 2)
        wglr = fw.sb(es, "wglr", [P, KC, 16], BF16)
        fw.dma("pool", wglr[:], wglr_r.rearrange("p (k c) -> p k c", k=KC), writes=["wglr"])
        sqs = [(("sq", i), fw.sb(es, "sq%d" % i, [P, 352])) for i in range(3)]
        rss = [(("rs", i), fw.sb(es, "rs%d" % i, [P, 352])) for i in range(2)]
        carry = fw.sb(es, "carry", [P, 14])
        Hs = fw.sb(es, "Hs", [P, 4, CH])
        Scar = fw.sb(es, "Scar", [P, 2, 256])
        tl0 = fw.sb(es, "tl0", [P, TS])
        sgx = fw.sb(es, "sgx", [P, TS])
        glr = fw.sb(es, "glr", [16, TS])
        fw.op("dve", lambda e: e.memset(carry[:], 0.0), writes=["carry"])
        fw.op("dve", lambda e: e.memset(Hs[:], 0.0), writes=["Hs"])
        fw.op("dve", lambda e: e.memset(Scar[:], 0.0), writes=["Scar"])

        def inproj2(src_ap, evac, M=P, wt_override=None):
            if wt_override is None:
                wk, wt = wi.load(src_ap.rearrange("p (k c) -> p k c", k=KC))
            else:
                wk, wt = wt_override
            for (c0, n) in tiles2:
                bk, bank = fw.nb()
                for kc in range(KC):
                    fw.op("pe", lambda e: e.matmul(bank[0:M, :n], lhsT=wt[:, kc, 0:M], rhs=hn[:, kc, c0:c0 + n],
                                                   start=(kc == 0), stop=(kc == KC - 1)),
                          reads=[wk] + [("hn", kc)], writes=[bk])
                evac(bk, bank, c0, n)

        evac_flip = [0]

        def evac_copy(dst, dkey, off=0, M=P):
            def f(bk, bank, c0, n):
                evac_flip[0] ^= 1
                if evac_flip[0]:
                    fw.op("act", lambda e: e.copy(out=dst[0:M, off + c0:off + c0 + n], in_=bank[0:M, :n]), reads=[bk], writes=[dkey])
                else:
                    fw.op("dve", lambda e: e.tensor_copy(out=dst[0:M, off + c0:off + c0 + n], in_=bank[0:M, :n]), reads=[bk], writes=[dkey])
            return f

        def lerp(raw, rkey, cidx, mu_ap, om_ap, tmp, tkey, dst, dkey):
            fw.op("dve", lambda e: e.tensor_copy(out=raw[:, 0:1], in_=carry[:, cidx:cidx + 1]), reads=["carry"], writes=[rkey])
            fw.op("dve", lambda e: e.tensor_scalar(out=tmp[:, :], in0=raw[:, 1:1 + TS], scalar1=om_ap, scalar2=None, op0=ALU.mult),
                  reads=[rkey] + CK, writes=[tkey])
            fw.op("dve", lambda e: e.scalar_tensor_tensor(out=dst[:, :], in0=raw[:, 0:TS], scalar=mu_ap, in1=tmp[:, :],
                                                          op0=ALU.mult, op1=ALU.add),
                  reads=[rkey, tkey] + CK, writes=[dkey])
            fw.op("dve", lambda e: e.tensor_copy(out=carry[:, cidx:cidx + 1], in_=raw[:, TS:TS + 1]), reads=[rkey], writes=["carry"])

        def c3(t):
            return t[:, :].rearrange("p (c k) -> p c k", k=CH)

        for seg in range(NSEG):
          try:
            s0 = seg * TS
            nvalid = min(TS, L - s0)
            pcur["seg"] = seg
            pcur["pr"] = 0
            pcur["fw"] = fw
            fw.mark("start_s%d_p0" % seg)
            with ExitStack() as ea:
                hT = fw.sb(ea, "hTs", [P, KC, TS])
                for kc in range(KC):
                    if standalone:
                        fw.dma("sp", hT[:, kc, :], h1p[kc][:, s0:s0 + TS], writes=[("h", kc)])
                    else:
                        load_h(fw, kc, hT[:, kc, :], s0, [("h", kc)])
                for ti, (c0, n) in enumerate(tiles2):
                    rs_out = rss[ti % 2]
                    emit_sumsq_rstd(fw, lambda b: hT[:, b, c0:c0 + n], KC, n, ones, sqs, rs_out, 1.0 / D, EPS, lambda b: ("h", b))
                    rk, rs = rs_out
                    for kc in range(KC):
                        fw.op("dve", lambda e: e.scalar_tensor_tensor(out=hn[:, kc, c0:c0 + n], in0=hT[:, kc, c0:c0 + n], scalar=C["g1"][:, kc:kc + 1],
                                                                      in1=rs[:, :n], op0=ALU.mult, op1=ALU.mult),
                              reads=[("h", kc), rk] + CK, writes=[("hn", kc)])
                fw.barrier()
            chk("A")

            with ExitStack() as eb:
                raw = fw.sb(eb, "rawl", [P, 1 + TS])
                tmp = fw.sb(eb, "tmpl", [P, TS])
                inproj2(wl_r[0], evac_copy(raw, "rawl", off=1))
                lerp(raw, "rawl", 0, mul[:, 0:1], oml[:, 0:1], tmp, "tmpl", tl0, "tl0")
                fw.op("act", lambda e: e.activation(out=tl0[0:64, :], in_=tl0[0:64, :], func=AF.Tanh), reads=["tl0"], writes=["tl0"])
                inproj2(wl_r[1], evac_copy(raw, "rawl", off=1))
                lerp(raw, "rawl", 1, mul[:, 1:2], oml[:, 1:2], tmp, "tmpl", sgx, "sgx")
                fw.op("act", lambda e: e.activation(out=sgx[:, :], in_=sgx[:, :], func=AF.Sigmoid), reads=["sgx"], writes=["sgx"])
                inproj2(None, evac_copy(glr, "glr", M=16), M=16, wt_override=("wglr", wglr))
                fw.barrier()

            chk("B")
            eR = ExitStack()
            RC = {}

            def salloc(name, shape, dtype=F32):
                if name not in RC:
                    RC[name] = fw.sb(eR, name, shape, dtype)
                return RC[name]

            for pr in range(4):
                pcur["pr"] = pr

                def pk(n, par=pr % 2):
                    return "%s_%d" % (n, par)
                with ExitStack() as ep:
                    names = ["v", "epos", "rt0", "rt1", "at0", "at1", "kt", "bt", "kg", "bg", "gg", "bonus", "ynT"]
                    BFN = ("v", "rt0", "rt1", "at0", "at1", "kt", "bt", "kg", "bg")
                    T_ = {n: salloc("p%d_%s" % (pr % 2, n), [P, TS], BF16 if n in BFN else F32) for n in names}
                    v, epos, rt0, rt1, at0, at1, kt, bt, kg, bg, gg, bonus, ynT = (T_[n] for n in names)
                    rtz = [rt0, rt1]
                    atz = [at0, at1]
                    with ExitStack() as eq:
                        tn = ["r", "k", "sgm", "a", "kk", "kmod", "b", "cs", "e1", "e2"]
                        Q_ = {n: salloc("q_" + n, [P, TS]) for n in tn}
                        r, k, sgm, a, kk, kmod, b, cs, e1, e2 = (Q_[n] for n in tn)
                        raws = [salloc("raw%d" % i, [P, 1 + TS]) for i in range(2)]
                        for i, (dst, dkey, mi) in enumerate([(r, "r", MU_R), (k, "k", MU_K), (v, pk("v"), MU_V)]):
                            raw = raws[i % 2]
                            rkey = "raw%d" % (i % 2)
                            inproj2(wr_r[pr, i], evac_copy(raw, rkey, off=1))
                            lerp(raw, rkey, 2 + pr * 3 + i, rwp[:, mi, pr:pr + 1], omr[:, mi, pr:pr + 1], e1, "e1", dst, dkey)
                        pc = slice(pr * P, (pr + 1) * P)
                        for (c0, n) in tiles2:
                            bk, bank = fw.nb()
                            fw.op("pe", lambda e: e.matmul(bank[:, :n], lhsT=w2z[:, pc], rhs=tl0[:, c0:c0 + n], start=True, stop=True),
                                  reads=["tl0"] + CK, writes=[bk])
                            fw.op("act", lambda e: e.activation(out=sgm[:, c0:c0 + n], in_=bank[:, :n], func=AF.Sigmoid, bias=rwp[:, W0I, pr:pr + 1]),
                                  reads=[bk] + CK, writes=["sgm"])
                            bk, bank = fw.nb()
                            fw.op("pe", lambda e: e.matmul(bank[:, :n], lhsT=a2z[:, pc], rhs=tl0[:, c0:c0 + n], start=True, stop=True),
                                  reads=["tl0"] + CK, writes=[bk])
                            fw.op("act", lambda e: e.activation(out=a[:, c0:c0 + n], in_=bank[:, :n], func=AF.Sigmoid, bias=rwp[:, A0I, pr:pr + 1]),
                                  reads=[bk] + CK, writes=["a"])
                            bk, bank = fw.nb()
                            fw.op("pe", lambda e: e.matmul(bank[:, :n], lhsT=g2s[:, pc], rhs=sgx[:, c0:c0 + n], start=True, stop=True),
                                  reads=["sgx"] + CK, writes=[bk])
                            fw.op("act", lambda e: e.copy(out=gg[:, c0:c0 + n], in_=bank[:, :n]), reads=[bk], writes=[pk("gg")])
                        fw.op("dve", lambda e: e.tensor_scalar(out=kk[:, :], in0=k[:, :], scalar1=rwp[:, KKI, pr:pr + 1], scalar2=None, op0=ALU.mult),
                              reads=["k"] + CK, writes=["kk"])
                        fw.op("act", lambda e: e.activation(out=e1[:, :], in_=kk[:, :], func=AF.Square), reads=["kk"], writes=["e1"])
                        for (c0, n) in tiles2:
                            bk, bank = fw.nb()
                            fw.op("pe", lambda e: e.matmul(bank[:, :n], lhsT=bones[:, :], rhs=e1[:, c0:c0 + n], start=True, stop=True),
                                  reads=["e1"] + CK, writes=[bk])
                            fw.op("act", lambda e: e.activation(out=e2[:, c0:c0 + n], in_=bank[:, :n], func=AF.Sqrt, bias=1e-30),
                                  reads=[bk], writes=["e2"])
                        fw.op("dve", lambda e: e.tensor_scalar(out=e2[:, :], in0=e2[:, :], scalar1=1e-12, scalar2=None, op0=ALU.max),
                              reads=["e2"], writes=["e2"])
                        fw.op("dve", lambda e: e.reciprocal(out=e2[:, :], in_=e2[:, :]), reads=["e2"], writes=["e2"])
                        fw.op("dve", lambda e: e.tensor_tensor(out=kk[:, :], in0=kk[:, :], in1=e2[:, :], op=ALU.mult),
                              reads=["kk", "e2"], writes=["kk"])
                        fw.op("dve", lambda e: e.tensor_scalar(out=e1[:, :], in0=a[:, :], scalar1=rwp[:, KAI, pr:pr + 1],
                                                               scalar2=omr[:, KAI, pr:pr + 1], op0=ALU.mult, op1=ALU.add),
                              reads=["a"] + CK, writes=["e1"])
                        fw.op("dve", lambda e: e.tensor_tensor(out=kmod[:, :], in0=k[:, :], in1=e1[:, :], op=ALU.mult),
                              reads=["k", "e1"], writes=["kmod"])
                        fw.op("dve", lambda e: e.tensor_tensor(out=b[:, :], in0=kk[:, :], in1=a[:, :], op=ALU.mult),
                              reads=["kk", "a"], writes=["b"])
                        fw.op("dve", lambda e: e.scalar_tensor_tensor(out=e1[:, :], in0=r[:, :], scalar=rwp[:, RKI, pr:pr + 1], in1=kmod[:, :],
                                                                      op0=ALU.mult, op1=ALU.mult),
                              reads=["r", "kmod"] + CK, writes=["e1"])
                        for (c0, n) in tiles2:
                            bk, bank = fw.nb()
                            fw.op("pe", lambda e: e.matmul(bank[:, :n], lhsT=bones[:, :], rhs=e1[:, c0:c0 + n], start=True, stop=True),
                                  reads=["e1"] + CK, writes=[bk])
                            fw.op("dve", lambda e: e.tensor_tensor(out=bonus[:, c0:c0 + n], in0=bank[:, :n], in1=v[:, c0:c0 + n], op=ALU.mult),
                                  reads=[bk, pk("v")], writes=[pk("bonus")])
                        fw.op("dve", lambda e: e.tensor_tensor_scan(out=cs[:, :], data0=rmask[:, :], data1=sgm[:, :], initial=0.0,
                                                                    op0=ALU.mult, op1=ALU.add),
                              reads=["sgm"] + CK, writes=["cs"])
                        fw.op("act", lambda e: e.activation(out=epos[:, :], in_=cs[:, :], func=AF.Exp, scale=-C05), reads=["cs"], writes=[pk("epos")])
                        for h in range(2):
                            fw.op("dve", lambda e: e.scalar_tensor_tensor(out=rtz[h][:, :], in0=r[:, :], scalar=hm[:, h:h + 1], in1=epos[:, :],
                                                                          op0=ALU.mult, op1=ALU.mult),
                                  reads=["r", pk("epos")] + CK, writes=[pk("rt")])
                        fw.op("act", lambda e: e.activation(out=e2[:, :], in_=cs[:, :], func=AF.Exp, scale=C05), reads=["cs"], writes=["e2"])
                        fw.op("dve", lambda e: e.tensor_tensor(out=kt[:, :], in0=kmod[:, :], in1=e2[:, :], op=ALU.mult),
                              reads=["kmod", "e2"], writes=[pk("kt")])
                        fw.op("dve", lambda e: e.tensor_tensor(out=bt[:, :], in0=b[:, :], in1=e2[:, :], op=ALU.mult),
                              reads=["b", "e2"], writes=[pk("bt")])
                        fw.op("dve", lambda e: e.tensor_tensor(out=e2[:, :], in0=cs[:, :], in1=sgm[:, :], op=ALU.subtract),
                              reads=["cs", "sgm"], writes=["e2"])
                        fw.op("act", lambda e: e.activation(out=e2[:, :], in_=e2[:, :], func=AF.Exp, scale=-C05), reads=["e2"], writes=["e2"])
                        for h in range(2):
                            fw.op("dve", lambda e: e.scalar_tensor_tensor(out=atz[h][:, :], in0=kk[:, :], scalar=hm[:, 2 + h:3 + h], in1=e2[:, :],
                                                                          op0=ALU.mult, op1=ALU.mult),
                                  reads=["kk", "e2"] + CK, writes=[pk("at")])
                        fw.op("dve", lambda e: e.tensor_tensor(out=c3(e2), in0=c3(cs)[:, :, CH - 1:CH].to_broadcast([P, NCH, CH]), in1=c3(cs),
                                                               op=ALU.subtract),
                              reads=["cs"], writes=["e2"])
                        fw.op("act", lambda e: e.activation(out=e2[:, :], in_=e2[:, :], func=AF.Exp, scale=-C05), reads=["e2"], writes=["e2"])
                        fw.op("dve", lambda e: e.tensor_tensor(out=kg[:, :], in0=kmod[:, :], in1=e2[:, :], op=ALU.mult),
                              reads=["kmod", "e2"], writes=[pk("kg")])
                        fw.op("dve", lambda e: e.tensor_tensor(out=bg[:, :], in0=b[:, :], in1=e2[:, :], op=ALU.mult),
                              reads=["b", "e2"], writes=[pk("bg")])
                    chk("Cpre")
                    with ExitStack() as ec:
                        tok = salloc("tok", [CH, NCH, 3, P], BF16)
                        LA = [salloc("LA%d" % i, [CH, NCH, 4, CH], BF16) for i in range(2)]
                        A3 = salloc("A3", [CH, NCH, 6, CH], BF16)
                        Tt = salloc("Tt", [CH, NCH, 2, CH], BF16)
                        Tt32 = salloc("Tt32", [CH, NCH, 2, CH])
                        Wsb = [salloc("Wsb%d" % i, [CH, P], BF16) for i in range(2)]
                        Usb = salloc("Usb", [CH, NCH, P], BF16)
                        Hs16 = salloc("Hs16", [P, NCH + 1, CH], BF16)
                        fw.op("act", lambda e: e.copy(out=Hs16[:, 0, :], in_=Hs[:, pr, :]), reads=[("Hs", pr)], writes=[("Hs16", 0)])
                        Ysb = salloc("Ysb", [CH, 2, P])
                        gsq = salloc("gsq", [P, TS])
                        gmean = salloc("gmean", [P, TS])
                        grs = salloc("grs", [P, TS])
                        hps = [slice(0, 64), slice(64, 128)]
                        for c in range(NCH):
                            cc = slice(c * CH, (c + 1) * CH)
                            bk, bank = fw.nb()
                            bankb = bank[0:CH, 0:3 * P // 2].bitcast(BF16)
                            for i, (src, skey) in enumerate([(v, pk("v")), (kg, pk("kg")), (bg, pk("bg"))]):
                                fw.op("pe", lambda e: e.transpose(bankb[:, i * P:(i + 1) * P], src[:, cc], identb[:, :]),
                                      reads=[skey] + CK, writes=[bk])
                            srcv = bankb[:, 0:3 * P].rearrange("p (a b) -> p a b", a=3)
                            if c % 2 == 0:
                                fw.op("dve", lambda e: e.tensor_copy(out=tok[:, c, :, :], in_=srcv), reads=[bk], writes=[("tok", c)])
                            else:
                                fw.op("act", lambda e: e.copy(out=tok[:, c, :, :], in_=srcv), reads=[bk], writes=[("tok", c)])
                        chk("C1")
                        for c in range(NCH):
                            cc = slice(c * CH, (c + 1) * CH)
                            bk, bank = fw.nb()
                            for h in range(2):
                                hp = hps[h]
                                fw.op("pe", lambda e: e.matmul(bank[0:CH, h * CH:(h + 1) * CH], lhsT=bt[:, cc], rhs=atz[h][:, cc], start=True, stop=True),
                                      reads=[pk("bt"), pk("at")], writes=[bk])
                                fw.op("pe", lambda e: e.matmul(bank[0:CH, 128 + h * CH:128 + (h + 1) * CH], lhsT=atz[h][:, cc], rhs=bt[:, cc], start=True, stop=True),
                                      reads=[pk("bt"), pk("at")], writes=[bk])
                            fw.op("dve", lambda e: e.tensor_tensor(out=LA[0][:, c, :, :], in0=bank[0:CH, 0:256].rearrange("p (a b) -> p a b", a=4),
                                                                   in1=M4[:, :].rearrange("p (a b) -> p a b", a=4), op=ALU.mult),
                                  reads=[bk] + CK, writes=[("LA0", c)])
                            bk, bank = fw.nb()
                            for h in range(2):
                                hp = hps[h]
                                fw.op("pe", lambda e: e.matmul(bank[0:CH, h * CH:(h + 1) * CH], lhsT=kt[:, cc], rhs=rtz[h][:, cc], start=True, stop=True),
                                      reads=[pk("kt"), pk("rt")], writes=[bk])
                                fw.op("pe", lambda e: e.matmul(bank[0:CH, 128 + h * CH:128 + (h + 1) * CH], lhsT=bt[:, cc], rhs=rtz[h][:, cc], start=True, stop=True),
                                      reads=[pk("bt"), pk("rt")], writes=[bk])
                                fw.op("pe", lambda e: e.matmul(bank[0:CH, 256 + h * CH:256 + (h + 1) * CH], lhsT=kt[:, cc], rhs=atz[h][:, cc], start=True, stop=True),
                                      reads=[pk("kt"), pk("at")], writes=[bk])
                            fw.op("dve", lambda e: e.tensor_tensor(out=A3[:, c, :, :], in0=bank[0:CH, 0:384].rearrange("p (a b) -> p a b", a=6),
                                                                   in1=M5[:, :].rearrange("p (a b) -> p a b", a=6), op=ALU.mult),
                                  reads=[bk] + CK, writes=[("A3", c)])
                            fw.op("dve", lambda e: e.tensor_tensor(out=Tt32[:, c, :, :], in0=LA[0][:, c, 0:2, :],
                                                                   in1=ident[0:CH, 0:CH].unsqueeze(1).to_broadcast([CH, 2, CH]), op=ALU.add),
                                  reads=[("LA0", c)] + CK, writes=[("Tt32", c)])
                            fw.op("act", lambda e: e.copy(out=Tt[:, c, :, :], in_=Tt32[:, c, :, :]), reads=[("Tt32", c)], writes=[("Tt", c)])
                        chk("C2")
                        cur = 0
                        for lvl in range(1, 6):
                            nxt = 1 - cur
                            last = (lvl == 5)
                            for c in range(NCH):
                                bk, bank = fw.nb()
                                for h in range(2):
                                    Lh = LA[cur][:, c, h, :]
                                    Ah = LA[cur][:, c, 2 + h, :]
                                    fw.op("pe", lambda e: e.matmul(bank[0:CH, 128 + h * CH:128 + (h + 1) * CH], lhsT=Lh, rhs=Ah, start=True, stop=True),
                                          reads=[("LA%d" % cur, c)], writes=[bk])
                                    if not last:
                                        fw.op("pe", lambda e: e.matmul(bank[0:CH, h * CH:(h + 1) * CH], lhsT=Ah, rhs=Lh, start=True, stop=True),
                                              reads=[("LA%d" % cur, c)], writes=[bk])
                                lo = 128 if last else 0
                                dstv = LA[nxt][:, c, lo // CH:4, :]
                                srcv = bank[0:CH, lo:256].rearrange("p (a b) -> p a b", b=CH)
                                if c % 2 == 0:
                                    fw.op("act", lambda e: e.copy(out=dstv, in_=srcv), reads=[bk], writes=[("LA%d" % nxt, c)])
                                else:
                                    fw.op("dve", lambda e: e.tensor_copy(out=dstv, in_=srcv), reads=[bk], writes=[("LA%d" % nxt, c)])
                            for c in range(NCH):
                                bk, bank = fw.nb()
                                for h in range(2):
                                    fw.op("pe", lambda e: e.matmul(bank[0:CH, h * CH:(h + 1) * CH], lhsT=LA[nxt][:, c, 2 + h, :], rhs=Tt[:, c, h, :],
                                                                   start=True, stop=True),
                                          reads=[("LA%d" % nxt, c), ("Tt", c)], writes=[bk])
                                fw.op("dve", lambda e: e.tensor_tensor(out=Tt32[:, c, :, :], in0=Tt32[:, c, :, :],
                                                                       in1=bank[0:CH, 0:128].rearrange("p (a b) -> p a b", a=2), op=ALU.add),
                                      reads=[bk, ("Tt32", c)], writes=[("Tt32", c)])
                                fw.op("act", lambda e: e.copy(out=Tt[:, c, :, :], in_=Tt32[:, c, :, :]), reads=[("Tt32", c)], writes=[("Tt", c)])
                            cur = nxt
                        chk("C3")
                        for c in range(NCH):
                            cc = slice(c * CH, (c + 1) * CH)
                            bk, bank = fw.nb()
                            for h in range(2):
                                hp = hps[h]
                                fw.op("pe", lambda e: e.matmul(bank[0:CH, h * CH:(h + 1) * CH], lhsT=atz[h][:, cc], rhs=Hs16[:, c, :], start=True, stop=False),
                                      reads=[pk("at"), ("Hs16", c)], writes=[bk])
                                fw.op("pe", lambda e: e.matmul(bank[0:CH, h * CH:(h + 1) * CH], lhsT=A3[:, c, 4 + h, :], rhs=tok[:, c, 0, h * CH:(h + 1) * CH],
                                                               start=False, stop=True),
                                      reads=[("A3", c), ("tok", c)], writes=[bk])
                            W = Wsb[c % 2]
                            wkey = ("Wsb", c % 2)
                            fw.op("act", lambda e: e.copy(out=W[:, :], in_=bank[0:CH, 0:P]), reads=[bk], writes=[wkey])
                            bk, bank = fw.nb()
                            for h in range(2):
                                fw.op("pe", lambda e: e.matmul(bank[0:CH, h * CH:(h + 1) * CH], lhsT=Tt[:, c, h, :], rhs=W[:, h * CH:(h + 1) * CH],
                                                               start=True, stop=True),
                                      reads=[("Tt", c), wkey], writes=[bk])
                            fw.op("dve", lambda e: e.tensor_copy(out=Usb[:, c, :], in_=bank[0:CH, 0:P]), reads=[bk], writes=[("Usb", c)])
                            bk, bank = fw.nb()
                            for h in range(2):
                                fw.op("pe", lambda e: e.matmul(bank[:, h * CH:(h + 1) * CH], lhsT=tok[:, c, 2, :], rhs=Usb[:, c, h * CH:(h + 1) * CH],
                                                               start=True, stop=False),
                                      reads=[("tok", c), ("Usb", c)], writes=[bk])
                                fw.op("pe", lambda e: e.matmul(bank[:, h * CH:(h + 1) * CH], lhsT=tok[:, c, 1, :], rhs=tok[:, c, 0, h * CH:(h + 1) * CH],
                                                               start=False, stop=True),
                                      reads=[("tok", c)], writes=[bk])
                            for h in range(2):
                                hp = hps[h]
                                fw.op("dve", lambda e: e.scalar_tensor_tensor(out=Hs16[hp, c + 1, :], in0=Hs[hp, pr, :],
                                                                              scalar=epos[hp, c * CH + CH - 1:c * CH + CH],
                                                                              in1=bank[hp, h * CH:(h + 1) * CH], op0=ALU.mult, op1=ALU.add),
                                      reads=[bk, ("Hs", pr), pk("epos")], writes=[("Hs16", c + 1)])
                            for h in range(2):
                                hp = hps[h]
                                fw.op("dve", lambda e: e.scalar_tensor_tensor(out=Hs[hp, pr, :], in0=Hs[hp, pr, :],
                                                                              scalar=epos[hp, c * CH + CH - 1:c * CH + CH],
                                                                              in1=bank[hp, h * CH:(h + 1) * CH], op0=ALU.mult, op1=ALU.add),
                                      reads=[bk, ("Hs", pr), pk("epos")], writes=[("Hs", pr)])
                        chk("C4")
                        for c in range(NCH):
                            cc = slice(c * CH, (c + 1) * CH)
                            bk, bank = fw.nb()
                            for h in range(2):
                                hp = hps[h]
                                o_ = bank[0:CH, h * CH:(h + 1) * CH]
                                fw.op("pe", lambda e: e.matmul(o_, lhsT=rtz[h][:, cc], rhs=Hs16[:, c, :], start=True, stop=False),
                                      reads=[pk("rt"), ("Hs16", c)], writes=[bk])
                                fw.op("pe", lambda e: e.matmul(o_, lhsT=A3[:, c, 2 + h, :], rhs=Usb[:, c, h * CH:(h + 1) * CH], start=False, stop=False),
                                      reads=[("A3", c), ("Usb", c)], writes=[bk])
                                fw.op("pe", lambda e: e.matmul(o_, lhsT=A3[:, c, h, :], rhs=tok[:, c, 0, h * CH:(h + 1) * CH], start=False, stop=True),
                                      reads=[("A3", c), ("tok", c)], writes=[bk])
                            fw.op("act", lambda e: e.copy(out=Ysb[:, c % 2, :], in_=bank[0:CH, 0:P]), reads=[bk], writes=[("Ysb", c % 2)])
                            bk2, bank2 = fw.nb()
                            fw.op("pe", lambda e: e.transpose(bank2[:, 0:CH], Ysb[:, c % 2, :], ident[0:CH, 0:CH]), reads=[("Ysb", c % 2)] + CK, writes=[bk2])
                            fw.op("dve", lambda e: e.tensor_copy(out=ynT[:, cc], in_=bank2[:, 0:CH]), reads=[bk2], writes=[pk("ynT")])
                        chk("C5a")
                        fw.op("act", lambda e: e.activation(out=gsq[:, :], in_=ynT[:, :], func=AF.Square), reads=[pk("ynT")], writes=["gsq"])
                        for (c0, n) in tiles2:
                            bk, bank = fw.nb()
                            fw.op("pe", lambda e: e.matmul(bank[:, :n], lhsT=bones[:, :], rhs=ynT[:, c0:c0 + n], start=True, stop=True),
                                  reads=[pk("ynT")] + CK, writes=[bk])
                            fw.op("act", lambda e: e.mul(out=gmean[:, c0:c0 + n], in_=bank[:, :n], mul=1.0 / CH), reads=[bk], writes=["gmean"])
                            bk, bank = fw.nb()
                            fw.op("pe", lambda e: e.matmul(bank[:, :n], lhsT=bones[:, :], rhs=gsq[:, c0:c0 + n], start=True, stop=True),
                                  reads=["gsq"] + CK, writes=[bk])
                            fw.op("act", lambda e: e.mul(out=grs[:, c0:c0 + n], in_=bank[:, :n], mul=1.0 / CH), reads=[bk], writes=["grs"])
                        fw.op("dve", lambda e: e.tensor_tensor(out=gsq[:, :], in0=gmean[:, :], in1=gmean[:, :], op=ALU.mult), reads=["gmean"], writes=["gsq"])
                        fw.op("dve", lambda e: e.tensor_tensor(out=grs[:, :], in0=grs[:, :], in1=gsq[:, :], op=ALU.subtract), reads=["grs", "gsq"], writes=["grs"])
                        fw.op("act", lambda e: e.activation(out=grs[:, :], in_=grs[:, :], func=AF.Sqrt, bias=GN_EPS), reads=["grs"], writes=["grs"])
                        fw.op("dve", lambda e: e.reciprocal(out=grs[:, :], in_=grs[:, :]), reads=["grs"], writes=["grs"])
                        fw.op("dve", lambda e: e.tensor_tensor(out=ynT[:, :], in0=ynT[:, :], in1=gmean[:, :], op=ALU.subtract), reads=[pk("ynT"), "gmean"], writes=[pk("ynT")])
                        fw.op("dve", lambda e: e.tensor_tensor(out=ynT[:, :], in0=ynT[:, :], in1=grs[:, :], op=ALU.mult), reads=[pk("ynT"), "grs"], writes=[pk("ynT")])
                        fw.op("act", lambda e: e.activation(out=ynT[:, :], in_=ynT[:, :], func=AF.Identity,
                                                            scale=rwp[:, LGI, pr:pr + 1], bias=rwp[:, LBI, pr:pr + 1]),
                              reads=[pk("ynT")] + CK, writes=[pk("ynT")])
                        chk("C5c")
                        fw.op("dve", lambda e: e.tensor_tensor(out=ynT[:, :], in0=ynT[:, :], in1=bonus[:, :], op=ALU.add),
                              reads=[pk("ynT"), pk("bonus")], writes=[pk("ynT")])
                        fw.op("dve", lambda e: e.tensor_tensor(out=ynT[:, :], in0=ynT[:, :], in1=gg[:, :], op=ALU.mult),
                              reads=[pk("ynT"), pk("gg")], writes=[pk("ynT")])
                        if standalone:
                            fw.dma("sp", oy[4 + pr][:, s0:s0 + nvalid], ynT[:, 0:nvalid], reads=[pk("ynT")])
                        else:
                            store(fw, 4 + pr, ynT[:, 0:nvalid], s0, nvalid, [pk("ynT")])
                        chk("C5d")

            fw.barrier()
            eR.close()
            chk("C5")
            eG = ExitStack()
            GC_ = {}

            def galloc(name, shape, dtype=F32):
                if name not in GC_:
                    GC_[name] = fw.sb(eG, name, shape, dtype)
                return GC_[name]

            for hh in range(2):
                pcur["pr"] = 4 + hh

                def gk_(n, par=hh % 2):
                    return "%s_g%d" % (n, par)
                fw.mark("gla_s%d_p%d" % (seg, 4 + hh))
                with ExitStack() as eg:
                    gn = ["q", "k", "v0", "v1", "go0", "go1", "ls", "cs", "e1", "e2", "qd", "kd", "kgm", "ob"]
                    GBF = ("v0", "v1", "qd", "kd", "kgm")
                    G_ = {n: galloc("g%d_%s" % (hh % 2, n), [P, TS], BF16 if n in GBF else F32) for n in gn}
                    q, k, v0, v1, go0, go1, ls, cs, e1, e2, qd, kd, kgm, ob = (G_[n] for n in gn)
                    oT = galloc("g%d_oT" % (hh % 2), [P, 2, TS])
                    vtok = galloc("vtok", [CH, NCH, 3, P], BF16)
                    sc = galloc("sc", [CH, NCH, CH], BF16)
                    Ss = galloc("Ss", [P, 256])
                    Ss16 = galloc("Ss16", [P, NCH + 1, 256], BF16)
                    rsg = galloc("g%d_rsg" % (hh % 2), [P, TS])
                    for i, (dst, dkey) in enumerate([(q, gk_("q")), (k, gk_("gk")), (v0, gk_("v0")), (v1, gk_("v1")), (go0, gk_("go0")), (go1, gk_("go1"))]):
                        inproj2(wg_r[hh, i], evac_copy(dst, dkey))
                    hc = slice(hh * P, (hh + 1) * P)
                    for (c0, n) in tiles2:
                        bk, bank = fw.nb()
                        fw.op("pe", lambda e: e.matmul(bank[:, :n], lhsT=gw2[0:16, hc], rhs=glr[0:16, c0:c0 + n], start=True, stop=True),
                              reads=["glr"] + CK, writes=[bk])
                        fw.op("act", lambda e: e.activation(out=e1[:, c0:c0 + n], in_=bank[:, :n], func=AF.Sigmoid, bias=glab[:, hh:hh + 1]),
                              reads=[bk] + CK, writes=[gk_("ge1")])
                    fw.op("act", lambda e: e.activation(out=ls[:, :], in_=e1[:, :], func=AF.Ln), reads=[gk_("ge1")], writes=[gk_("ls")])
                    fw.op("dve", lambda e: e.tensor_tensor_scan(out=cs[:, :], data0=rmask[:, :], data1=ls[:, :], initial=0.0,
                                                                op0=ALU.mult, op1=ALU.add),
                          reads=[gk_("ls")] + CK, writes=[gk_("gcs")])
                    fw.op("act", lambda e: e.activation(out=e1[:, :], in_=cs[:, :], func=AF.Exp, scale=1.0 / 16), reads=[gk_("gcs")], writes=[gk_("ge1")])
                    fw.op("dve", lambda e: e.scalar_tensor_tensor(out=qd[:, :], in0=q[:, :], scalar=float(128 ** -0.5), in1=e1[:, :],
                                                                  op0=ALU.mult, op1=ALU.mult),
                          reads=[gk_("q"), gk_("ge1")], writes=[gk_("qd")])
                    fw.op("act", lambda e: e.activation(out=e2[:, :], in_=cs[:, :], func=AF.Exp, scale=-1.0 / 16), reads=[gk_("gcs")], writes=[gk_("ge2")])
                    fw.op("dve", lambda e: e.tensor_tensor(out=kd[:, :], in0=k[:, :], in1=e2[:, :], op=ALU.mult), reads=[gk_("gk"), gk_("ge2")], writes=[gk_("kd")])
                    fw.op("dve", lambda e: e.tensor_tensor(out=c3(e2), in0=c3(cs)[:, :, CH - 1:CH].to_broadcast([P, NCH, CH]), in1=c3(cs),
                                                           op=ALU.subtract),
                          reads=[gk_("gcs")], writes=[gk_("ge2")])
                    fw.op("act", lambda e: e.activation(out=e2[:, :], in_=e2[:, :], func=AF.Exp, scale=1.0 / 16), reads=[gk_("ge2")], writes=[gk_("ge2")])
                    fw.op("dve", lambda e: e.tensor_tensor(out=kgm[:, :], in0=k[:, :], in1=e2[:, :], op=ALU.mult), reads=[gk_("gk"), gk_("ge2")], writes=[gk_("kgm")])
                    fw.op("dve", lambda e: e.tensor_copy(out=Ss[:, :], in_=Scar[:, hh, :]), reads=["Scar"], writes=["Ss"])
                    fw.op("act", lambda e: e.copy(out=Ss16[:, 0, :], in_=Scar[:, hh, :]), reads=["Scar"], writes=[("Ss16", 0)])
                    for c in range(NCH):
                        cc = slice(c * CH, (c + 1) * CH)
                        bk, bank = fw.nb()
                        bankb = bank[0:CH, 0:3 * P // 2].bitcast(BF16)
                        for i, (src, skey) in enumerate([(v0, gk_("v0")), (v1, gk_("v1")), (kgm, gk_("kgm"))]):
                            fw.op("pe", lambda e: e.transpose(bankb[:, i * P:(i + 1) * P], src[:, cc], identb[:, :]),
                                  reads=[skey] + CK, writes=[bk])
                        srcv = bankb[:, 0:3 * P].rearrange("p (a b) -> p a b", a=3)
                        if c % 2 == 0:
                            fw.op("dve", lambda e: e.tensor_copy(out=vtok[:, c, :, :], in_=srcv), reads=[bk], writes=[("vtok", c)])
                        else:
                            fw.op("act", lambda e: e.copy(out=vtok[:, c, :, :], in_=srcv), reads=[bk], writes=[("vtok", c)])
                        bk, bank = fw.nb()
                        fw.op("pe", lambda e: e.matmul(bank[0:CH, 0:CH], lhsT=kd[:, cc], rhs=qd[:, cc], start=True, stop=True),
                              reads=[gk_("kd"), gk_("qd")], writes=[bk])
                        fw.op("dve", lambda e: e.tensor_tensor(out=sc[:, c, :], in0=bank[0:CH, 0:CH], in1=mincl[:, :], op=ALU.mult),
                              reads=[bk] + CK, writes=[("sc", c)])
                    for c in range(NCH):
                        bk, bank = fw.nb()
                        fw.op("pe", lambda e: e.matmul(bank[:, 0:256], lhsT=vtok[:, c, 2, :], rhs=vtok[:, c, 0:2, :], start=True, stop=True),
                              reads=[("vtok", c)], writes=[bk])
                        fw.op("dve", lambda e: e.scalar_tensor_tensor(out=Ss16[:, c + 1, :], in0=Ss[:, :], scalar=e1[:, c * CH + CH - 1:c * CH + CH],
                                                                      in1=bank[:, 0:256], op0=ALU.mult, op1=ALU.add),
                              reads=[bk, "Ss", gk_("ge1")], writes=[("Ss16", c + 1)])
                        fw.op("dve", lambda e: e.scalar_tensor_tensor(out=Ss[:, :], in0=Ss[:, :], scalar=e1[:, c * CH + CH - 1:c * CH + CH],
                                                                      in1=bank[:, 0:256], op0=ALU.mult, op1=ALU.add),
                              reads=[bk, "Ss", gk_("ge1")], writes=["Ss"])
                    fw.op("dve", lambda e: e.tensor_copy(out=Scar[:, hh, :], in_=Ss[:, :]), reads=["Ss"], writes=["Scar"])
                    for c in range(NCH):
                        cc = slice(c * CH, (c + 1) * CH)
                        bk, bank = fw.nb()
                        for vb in range(2):
                            o_ = bank[:, vb * CH:(vb + 1) * CH]
                            fw.op("pe", lambda e: e.matmul(o_, lhsT=vtok[:, c, vb, :], rhs=sc[:, c, :], start=True, stop=False),
                                  reads=[("vtok", c), ("sc", c)], writes=[bk])
                            fw.op("pe", lambda e: e.matmul(o_, lhsT=Ss16[:, c, vb * P:(vb + 1) * P], rhs=qd[:, cc], start=False, stop=True),
                                  reads=[("Ss16", c), gk_("qd")], writes=[bk])
                        srcv = bank[:, 0:2 * CH].rearrange("p (a b) -> p a b", a=2)
                        if c % 2 == 0:
                            fw.op("act", lambda e: e.copy(out=oT[:, :, cc], in_=srcv), reads=[bk], writes=[gk_("oT")])
                        else:
                            fw.op("dve", lambda e: e.tensor_copy(out=oT[:, :, cc], in_=srcv), reads=[bk], writes=[gk_("oT")])
                    for (c0, n) in tiles2:
                        bk, bank = fw.nb()
                        for vb in range(2):
                            sk, sq = sqs[vb]
                            fw.op("act", lambda e: e.activation(out=sq[:, :n], in_=oT[:, vb, c0:c0 + n], func=AF.Square), reads=[gk_("oT")], writes=[sk])
                            fw.op("pe", lambda e: e.matmul(bank[:, :n], lhsT=ones[:, :], rhs=sq[:, :n], start=(vb == 0), stop=(vb == 1)),
                                  reads=[sk] + CK, writes=[bk])
                        fw.op("act", lambda e: e.activation(out=rsg[:, c0:c0 + n], in_=bank[:, :n], func=AF.Sqrt, scale=1.0 / 256, bias=EPS),
                              reads=[bk], writes=[gk_("rsg")])
                    fw.op("dve", lambda e: e.reciprocal(out=rsg[:, :], in_=rsg[:, :]), reads=[gk_("rsg")], writes=[gk_("rsg")])
                    for vb, (go, gokey) in enumerate([(go0, gk_("go0")), (go1, gk_("go1"))]):
                        fw.op("dve", lambda e: e.scalar_tensor_tensor(out=ob[:, :], in0=oT[:, vb, :], scalar=glan[:, hh * 2 + vb:hh * 2 + vb + 1],
                                                                      in1=rsg[:, :], op0=ALU.mult, op1=ALU.mult),
                              reads=[gk_("oT"), gk_("rsg")] + CK, writes=[gk_("ob")])
                        fw.op("act", lambda e: e.activation(out=e2[:, :], in_=go[:, :], func=AF.Silu), reads=[gokey], writes=[gk_("ge2")])
                        fw.op("dve", lambda e: e.tensor_tensor(out=ob[:, :], in0=ob[:, :], in1=e2[:, :], op=ALU.mult), reads=[gk_("ob"), gk_("ge2")], writes=[gk_("ob")])
                        if standalone:
                            fw.dma("sp", oy[hh * 2 + vb][:, s0:s0 + nvalid], ob[:, 0:nvalid], reads=[gk_("ob")])
                        else:
                            store(fw, hh * 2 + vb, ob[:, 0:nvalid], s0, nvalid, [gk_("ob")])
            fw.barrier()
            eG.close()
          except StopBuild:
            stopped[0] = True
            break
          except AssertionError:
            if stopped[0]:
                break
            raise
        fw.mark(None)
        if standalone:
            fw.finish("sp")
        else:
            fw.barrier()
        fw.defer = False
        print("K2: ins", fw.nins, "waits", fw.nwaits)
        if not stopped[0]:
            es.close()
    return nc


def prep_k2_common(inp):
    f = np.float32
    GC = 3088
    w_in = inp["cd_w_in"][0]
    mu = inp["cd_rw_mu"][0]
    s_le = np.triu(np.ones((CH, CH), f))
    s_lt = np.triu(np.ones((CH, CH), f), 1)
    t_gt = np.tril(np.ones((CH, CH), f), -1)
    rmask = np.ones((P, TS), f)
    rmask[:, ::CH] = 0.0
    bones = np.zeros((P, P), f)
    bones[:64, :64] = 1
    bones[64:, 64:] = 1
    common = prep_common()
    hm = np.zeros((P, 4), f)
    hm[:64, 0] = 1
    hm[64:, 1] = 1
    hm[:64, 2] = -1
    hm[64:, 3] = -1
    common.update(hm=hm, bones=bones, M4=np.concatenate([s_lt, s_lt, t_gt, t_gt], 1), M5=np.concatenate([s_le] * 4 + [s_lt] * 2, 1),
                  mincl=s_le, rmask=rmask, g1=vec_layout(inp["norm_mix"][1]))

    def blk(c0):
        return blk_layout(w_in[:, c0:c0 + P])[0]

    per_hs = []
    for hs in range(2):
        m = {}
        m["wr_r"] = np.stack([np.stack([blk(GC + i * 1024 + (hs * 4 + pr) * P) for i in range(3)]) for pr in range(4)])
        m["wl_r"] = np.stack([blk(GC + 3072), blk(GC + 3200)])
        gl = []
        for hh in range(2):
            gh = hs * 2 + hh
            gl.append(np.stack([blk(gh * P), blk(512 + gh * P), blk(1024 + gh * 256), blk(1024 + gh * 256 + P),
                                blk(2048 + gh * 256), blk(2048 + gh * 256 + P)]))
        m["wg_r"] = np.stack(gl)
        wg = w_in[:, 3072:3088]
        m["wglr_r"] = np.ascontiguousarray(wg.reshape(KC, P, 16).transpose(1, 0, 2).reshape(P, KC * 16))
        ch = slice(hs * 512, (hs + 1) * 512)

        def pv(vec):
            return vec.reshape(4, P).T
        rwp = np.stack([pv(mu[0:1024][ch]), pv(mu[1024:2048][ch]), pv(mu[2048:3072][ch]), pv(inp["cd_rw_w0"][0][ch]),
                        pv(inp["cd_rw_a0"][0][ch]), pv(inp["cd_rw_kk"][0][ch]), pv(inp["cd_rw_ka"][0][ch]),
                        pv(inp["cd_rw_rk"][0][ch]), pv(inp["cd_rw_ln_g"][0][ch]), pv(inp["cd_rw_ln_b"][0][ch])], axis=1)
        m["rwp"] = np.ascontiguousarray(rwp.astype(f))
        m["mul"] = np.ascontiguousarray(np.stack([mu[3072:3200], mu[3200:3328]], 1))
        z64 = np.zeros((64, 512), f)
        m["w2z"] = np.ascontiguousarray(np.concatenate([inp["cd_rw_w2"][0][:, ch], z64], 0))
        m["a2z"] = np.ascontiguousarray(np.concatenate([z64, inp["cd_rw_a2"][0][:, ch]], 0))
        m["g2s"] = np.ascontiguousarray(inp["cd_rw_g2"][0][:, ch])
        gch = slice(hs * 256, (hs + 1) * 256)
        m["gw2"] = np.ascontiguousarray(inp["cd_gla_w2"][0][:, gch])
        m["glab"] = np.ascontiguousarray(inp["cd_gla_b"][0][gch].reshape(2, P).T)
        m["glan"] = np.ascontiguousarray(inp["cd_gla_norm_g"][0][ch].reshape(4, P).T)
        per_hs.append(m)
    return common, per_hs


def prep_k2(inp, h1T_list):
    f = np.float32
    common, per_hs = prep_k2_common(inp)
    maps = []
    for b in range(len(h1T_list)):
        hp = np.concatenate([h1T_list[b], np.zeros((D, TP - L), f)], axis=1).reshape(KC, P, TP)
        hp = np.ascontiguousarray(hp)
        for hs in range(2):
            m = dict(common)
            m.update(per_hs[hs])
            m["h1p"] = hp
            maps.append(m)
    return maps


def assemble_oy(res_pair):
    out = np.empty((KC, P, L), np.float32)
    for hs in range(2):
        out[hs * 4:hs * 4 + 4] = res_pair[hs][0:4]
        out[8 + hs * 4:8 + hs * 4 + 4] = res_pair[hs][4:8]
    return out.reshape(D, L)


def kernel_unfused(**inputs):
    inp = {k: np.ascontiguousarray(np.asarray(v, dtype=np.float32)) for k, v in inputs.items()}
    B = inp["x"].shape[0]
    cores = list(range(2 * B))
    r1 = run_bass_kernel_spmd(build_k1(), prep_k1(inp), core_ids=cores)
    h1T = [np.concatenate([np.asarray(r1.results[2 * b + h]["h1T"]).reshape(D, HALF) for h in range(2)], axis=1) for b in range(B)]
    r2 = run_bass_kernel_spmd(build_k2(), prep_k2(inp, h1T), core_ids=cores)
    oyT = [assemble_oy([np.asarray(r2.results[2 * b + h]["oy"]) for h in range(2)]) for b in range(B)]
    r3 = run_bass_kernel_spmd(build_k3(), prep_k3(inp, h1T, oyT), core_ids=cores)
    out = np.stack([np.concatenate([np.asarray(r3.results[2 * b + h]["out"]) for h in range(2)], axis=0)[N_META:] for b in range(B)])
    return np.ascontiguousarray(out.astype(np.float32))


def build_fused():
    nc = bass.Bass("TRN2", target_bir_lowering=False)
    h1s = nc.dram_tensor("h1s", [KC, P, 2 + TP], F32).ap()
    oys = nc.dram_tensor("oys", [KC, P, 2 + L], F32).ap()
    sel_d = nc.dram_tensor("sel", [P, 2], F32, kind="ExternalInput").ap()
    with ExitStack() as es:
        fw = FW(nc, es)
        ctx = (nc, fw)
        with ExitStack() as ez:
            z = fw.sb(ez, "zeros", [P, 64])
            fw.op("dve", lambda e: e.memset(z[:], 0.0), writes=["z"])
            for kc in range(KC):
                fw.dma("sp", h1s[kc][:, 0:2], z[:, 0:2], reads=["z"])
                fw.dma("sp", h1s[kc][:, 2 + L:2 + TP], z[:, 0:TP - L], reads=["z"])
                fw.dma("sp", oys[kc][:, 0:2], z[:, 0:2], reads=["z"])
            fw.barrier()
        for half, nm in enumerate(["A", "B"]):
            build_k1(ctx=ctx, pfx="k1_", overrides={"xin": "k1_xin" + nm, "fmask": "k1_fmask" + nm},
                     store=lambda fw_, kc, src, reads, half=half: fw_.dma(
                         "sp", h1s[kc][:, 2 + half * HALF:2 + (half + 1) * HALF], src, reads=reads))
        for hs, nm in enumerate(["a", "b"]):
            build_k2(ctx=ctx, pfx="k2%s_" % nm,
                     load_h=lambda fw_, kc, dst, s0, writes: fw_.dma("sp", dst, h1s[kc][:, 2 + s0:2 + s0 + TS], writes=writes),
                     store=lambda fw_, blk, src, s0, nvalid, reads, hs=hs: fw_.dma(
                         "sp", oys[(hs * 4 + blk) if blk < 4 else (4 + hs * 4 + blk)][:, 2 + s0:2 + s0 + nvalid], src, reads=reads))

        def blend_load(dst, src_dram, dkey):
            with ExitStack() as eb:
                sel = fw.sb(eb, "sel", [P, 2])
                fw.dma("sp", sel[:], sel_d, writes=["sel"])
                tA = [fw.sb(eb, "tA%d" % i, [P, WM]) for i in range(2)]
                tB = [fw.sb(eb, "tB%d" % i, [P, WM]) for i in range(2)]
                for kc in range(KC):
                    a_, b_ = tA[kc % 2], tB[kc % 2]
                    ak, bk = ("tA", kc % 2), ("tB", kc % 2)
                    fw.dma("sp", a_[:, :], src_dram[kc][:, 0:WM], writes=[ak])
                    fw.dma("sp", b_[:, :], src_dram[kc][:, HALF:HALF + WM], writes=[bk])
                    fw.op("dve", lambda e: e.tensor_scalar(out=a_[:, :], in0=a_[:, :], scalar1=sel[:, 0:1], scalar2=None, op0=ALU.mult),
                          reads=[ak, "sel"], writes=[ak])
                    fw.op("dve", lambda e: e.scalar_tensor_tensor(out=dst[:, kc, :], in0=b_[:, :], scalar=sel[:, 1:2], in1=a_[:, :],
                                                                  op0=ALU.mult, op1=ALU.add),
                          reads=[ak, bk, "sel"], writes=[(dkey, kc)])
                fw.barrier()

        build_k3(ctx=ctx, pfx="k3_", load_h=lambda fw_, hT: blend_load(hT, h1s, "h"),
                 load_act=lambda fw_, act: blend_load(act, oys, "act"))
        print("FUSED: ins", fw.nins, "waits", fw.nwaits)
    return nc


def prep_fused(inp):
    f = np.float32
    B = inp["x"].shape[0]
    base = {}
    for k, v in prep_k1_common(inp).items():
        base["k1_" + k] = v
    base["k1_fmaskA"] = np.zeros((P, 2), f)
    base["k1_fmaskB"] = np.ones((P, 2), f)
    common2, per_hs = prep_k2_common(inp)
    for hs, nm in enumerate(["a", "b"]):
        for k, v in list(common2.items()) + list(per_hs[hs].items()):
            base["k2%s_%s" % (nm, k)] = v
    for k, v in prep_k3_common(inp).items():
        base["k3_" + k] = v
    maps = []
    for b in range(B):
        wa, wb = k1_windows(inp, b)
        for half in range(2):
            m = dict(base)
            m["k1_xinA"] = wa
            m["k1_xinB"] = wb
            m["k3_fmask"] = np.full((P, 2), 0.0 if half == 0 else 1.0, f)
            sel = np.zeros((P, 2), f)
            sel[:, half] = 1.0
            m["sel"] = sel
            maps.append(m)
    return maps


def kernel_fused(**inputs):
    inp = {k: np.ascontiguousarray(np.asarray(v, dtype=np.float32)) for k, v in inputs.items()}
    B = inp["x"].shape[0]
    res = run_bass_kernel_spmd(build_fused(), prep_fused(inp), core_ids=list(range(2 * B)))
    out = np.stack([np.concatenate([np.asarray(res.results[2 * b + h]["out"]) for h in range(2)], axis=0)[N_META:] for b in range(B)])
    return np.ascontiguousarray(out.astype(np.float32))


kernel = kernel_unfused
```
